# Optimizing a Trainium2 kernel written in Bass

```python
import jax, jax.numpy as jnp
from jax import lax
import numpy as np

D_MODEL = 1024
BATCH = 16
SEQ = 2048
DEPTH = 2

CHUNK = 64
D_MIX = D_MODEL
HG_WIDTH = D_MIX // 2
HG_EXPAND = 128
HG_HEADS = HG_WIDTH // HG_EXPAND
HG_DK = HG_EXPAND
HG_DV = HG_WIDTH // HG_HEADS
GM_WIDTH = D_MIX - HG_WIDTH
GM_HEADS = 4
GM_DH = GM_WIDTH // GM_HEADS
GM_BLOCK = 128
D_FF = 2816
N_MOD = 9
RMS_EPS = 1e-6
LN_EPS = 1e-5
IN_COLS = 4 * HG_WIDTH + 2 * GM_WIDTH

kernel_name = "hybrid_hgrn2_gmlp_macaron_adaln"


def rms_norm(x, gain, eps=RMS_EPS):
    xf = x.astype(jnp.float32)
    y = xf * lax.rsqrt(jnp.mean(xf * xf, axis=-1, keepdims=True) + eps)
    return (y * gain.astype(jnp.float32)).astype(x.dtype)


def modulate(x, shift, scale):
    return x * (1 + scale[:, None, :]) + shift[:, None, :]


def swiglu_ffn(x, w13, w2):
    a, b = jnp.split(x @ w13, 2, axis=-1)
    return (jax.nn.silu(a) * b) @ w2


def hgrn2_mixer(q, f_raw, i, g, lb, gnorm_gain):
    B, L, _ = q.shape
    nC = L // CHUNK
    dt = q.dtype
    lbf = lb.astype(jnp.float32)
    f = lbf + (1 - lbf) * jax.nn.sigmoid(f_raw.astype(jnp.float32))
    log_f = jnp.log(f)
    k = 1.0 - f

    def to_chunks(t, d):
        return t.astype(jnp.float32).reshape(B, nC, CHUNK, HG_HEADS, d).transpose(1, 0, 3, 2, 4)

    qc = to_chunks(q, HG_DK)
    kc = to_chunks(k, HG_DK)
    vc = to_chunks(i, HG_DV)
    bc = jnp.cumsum(to_chunks(log_f, HG_DK), axis=3)
    tri = jnp.tril(jnp.ones((CHUNK, CHUNK), dtype=bool))[:, :, None]

    def step(S, inp):
        qt, kt, vt, bt = inp
        diff = bt[:, :, :, None, :] - bt[:, :, None, :, :]
        decay = jnp.where(tri, jnp.exp(jnp.where(tri, diff, 0.0)), 0.0)
        A = jnp.einsum('bhtk,bhtsk,bhsk->bhts', qt, decay, kt)
        o = (jnp.einsum('bhts,bhsv->bhtv', A, vt)
             + jnp.einsum('bhtk,bhkv->bhtv', qt * jnp.exp(bt), S))
        b_last = bt[:, :, -1:, :]
        S = (jnp.exp(b_last[:, :, 0, :])[..., None] * S
             + jnp.einsum('bhsk,bhsv->bhkv', kt * jnp.exp(b_last - bt), vt))
        return S, o

    S0 = jnp.zeros((B, HG_HEADS, HG_DK, HG_DV), jnp.float32)
    _, o = lax.scan(step, S0, (qc, kc, vc, bc))
    o = o.transpose(1, 0, 3, 2, 4).reshape(B, L, HG_HEADS, HG_DV)
    o = rms_norm(o, gnorm_gain.reshape(HG_HEADS, HG_DV))
    o = o.reshape(B, L, HG_WIDTH) * jax.nn.silu(g.astype(jnp.float32))
    return o.astype(dt)


def gmlp_mixer(u, v, ln_gain, w_sp, b_sp, out_gain):
    B, L, _ = u.shape
    nB = L // GM_BLOCK
    dt = u.dtype
    u = jax.nn.gelu(u).reshape(B, nB, GM_BLOCK, GM_HEADS, GM_DH)
    v = jax.nn.gelu(v).reshape(B, nB, GM_BLOCK, GM_HEADS, GM_DH)
    vf = v.astype(jnp.float32)
    mu = jnp.mean(vf, axis=-1, keepdims=True)
    var = jnp.mean(jnp.square(vf - mu), axis=-1, keepdims=True)
    vn = ((vf - mu) * lax.rsqrt(var + LN_EPS) * ln_gain.reshape(GM_HEADS, GM_DH).astype(jnp.float32)).astype(dt)
    cpos = jnp.arange(GM_BLOCK) // CHUNK
    mask = cpos[:, None] >= cpos[None, :]
    w = jnp.where(mask[None], w_sp, 0.0)
    mixed = (jnp.einsum('hts,bnshd->bnthd', w, vn)
             + b_sp.T[None, None, :, :, None])
    y = rms_norm(u * mixed, out_gain.reshape(GM_HEADS, GM_DH))
    return y.reshape(B, L, GM_WIDTH)


def setup_inputs(seed: int = 0) -> dict:
    key = jax.random.key(seed)
    ks = jax.random.split(key, 18)

    def nrm(k, shape, scale):
        return scale * jax.random.normal(k, shape, jnp.float32)

    return {
        "x": nrm(ks[0], (BATCH, SEQ, D_MODEL), 1.0),
        "c": nrm(ks[1], (BATCH, D_MODEL), 1.0),
        "w_ada": nrm(ks[2], (DEPTH, D_MODEL, N_MOD * D_MODEL), 0.5 * D_MODEL ** -0.5),
        "b_ada": nrm(ks[3], (DEPTH, N_MOD * D_MODEL), 0.02),
        "norm_gain": 1.0 + nrm(ks[4], (DEPTH, 3, D_MODEL), 0.05),
        "ffn1_w13": nrm(ks[5], (DEPTH, D_MODEL, 2 * D_FF), D_MODEL ** -0.5),
        "ffn1_w2": nrm(ks[6], (DEPTH, D_FF, D_MODEL), D_FF ** -0.5),
        "w_in": nrm(ks[7], (DEPTH, D_MODEL, IN_COLS), D_MODEL ** -0.5),
        "hg_lb_logits": nrm(ks[8], (DEPTH, HG_WIDTH), 0.5),
        "hg_gnorm": 1.0 + nrm(ks[9], (DEPTH, HG_WIDTH), 0.05),
        "gm_ln_gain": 1.0 + nrm(ks[10], (DEPTH, GM_WIDTH), 0.05),
        "gm_w_spatial": nrm(ks[11], (DEPTH, GM_HEADS, GM_BLOCK, GM_BLOCK), 0.5 * GM_BLOCK ** -0.5),
        "gm_b_spatial": 1.0 + nrm(ks[12], (DEPTH, GM_HEADS, GM_BLOCK), 0.1),
        "gm_out_gain": 1.0 + nrm(ks[13], (DEPTH, GM_WIDTH), 0.05),
        "w_out": nrm(ks[14], (DEPTH, D_MIX, D_MODEL), D_MIX ** -0.5),
        "ffn2_w13": nrm(ks[15], (DEPTH, D_MODEL, 2 * D_FF), D_MODEL ** -0.5),
        "ffn2_w2": nrm(ks[16], (DEPTH, D_FF, D_MODEL), D_FF ** -0.5),
        "final_gain": 1.0 + nrm(ks[17], (D_MODEL,), 0.05),
    }


def reference(x, c, w_ada, b_ada, norm_gain, ffn1_w13, ffn1_w2, w_in, hg_lb_logits, hg_gnorm,
              gm_ln_gain, gm_w_spatial, gm_b_spatial, gm_out_gain, w_out, ffn2_w13, ffn2_w2,
              final_gain):
    B = x.shape[0]
    lb_all = jnp.cumsum(jax.nn.softmax(hg_lb_logits.astype(jnp.float32), axis=0), axis=0)
    lb_all = lb_all - lb_all[0]
    splits = [HG_WIDTH, 2 * HG_WIDTH, 3 * HG_WIDTH, 4 * HG_WIDTH, 4 * HG_WIDTH + GM_WIDTH]
    c_act = jax.nn.silu(c)
    h = x
    for l in range(DEPTH):
        mod = (c_act @ w_ada[l] + b_ada[l]).reshape(B, N_MOD, D_MODEL)
        sh1, sc1, g1 = mod[:, 0], mod[:, 1], mod[:, 2]
        sh2, sc2, g2 = mod[:, 3], mod[:, 4], mod[:, 5]
        sh3, sc3, g3 = mod[:, 6], mod[:, 7], mod[:, 8]

        y = modulate(rms_norm(h, norm_gain[l, 0]), sh1, sc1)
        h = h + 0.5 * g1[:, None, :] * swiglu_ffn(y, ffn1_w13[l], ffn1_w2[l])

        y = modulate(rms_norm(h, norm_gain[l, 1]), sh2, sc2)
        proj = y @ w_in[l]
        q, f_raw, i_in, g_out, u, v = jnp.split(proj, splits, axis=-1)
        o_hg = hgrn2_mixer(q, f_raw, i_in, g_out, lb_all[l], hg_gnorm[l])
        o_gm = gmlp_mixer(u, v, gm_ln_gain[l], gm_w_spatial[l], gm_b_spatial[l], gm_out_gain[l])
        mix = jnp.concatenate([o_hg, o_gm], axis=-1) @ w_out[l]
        h = h + g2[:, None, :] * mix

        y = modulate(rms_norm(h, norm_gain[l, 2]), sh3, sc3)
        h = h + 0.5 * g3[:, None, :] * swiglu_ffn(y, ffn2_w13[l], ffn2_w2[l])

    return rms_norm(h, final_gain)
```

```python
from contextlib import ExitStack
import numpy as np
import concourse.bass as bass
import concourse.mybir as mybir
from concourse.bass_utils import run_bass_kernel_spmd

F32 = mybir.dt.float32
BF16 = mybir.dt.bfloat16
AF = mybir.ActivationFunctionType
ALU = mybir.AluOpType
AX = mybir.AxisListType

D = 1024
SEQ = 2048
DEPTH = 2
DFF = 2816
T = 512
NCORES = 8
BPC = 2
RMS_EPS = 1e-6
LN_EPS = 1e-5
NSLOT = 4

C_BADA = 0
C_NG = 144
C_LB = 192
C_GN = 200
C_LNG = 208
C_OG = 216
C_FG = 224
C_C = 232
NV = 256


class Sched:
    ENGS = ("pe", "act", "dve", "pool", "sp")

    def __init__(self, nc, es, dry=False):
        self.nc = nc
        self.es = es
        self.dry = dry
        self.ops = {e: [] for e in self.ENGS}
        self.sem = {}
        self.cnt = {}
        self.seen = {e: {} for e in self.ENGS}
        self.bufs = {}
        for e in self.ENGS:
            self._mk("E_" + e)

    def _mk(self, name):
        if name not in self.cnt:
            if not self.dry:
                self.sem[name] = self.es.enter_context(self.nc.semaphore(name))
            self.cnt[name] = 0
        return name

    def op(self, eng, fn, reads=(), writes=(), dma=None):
        if self.dry:
            return None
        deps = {}
        for k in reads:
            b = self.bufs.get(k)
            if b and b["w"]:
                s, v = b["w"]
                deps[s] = max(deps.get(s, 0), v)
        for k in writes:
            b = self.bufs.get(k)
            if b:
                if b["w"]:
                    s, v = b["w"]
                    deps[s] = max(deps.get(s, 0), v)
                for s, v in b["r"]:
                    deps[s] = max(deps.get(s, 0), v)
        own = "E_" + eng
        waits = []
        for s, v in deps.items():
            if s == own and eng == "pe":
                continue
            if self.seen[eng].get(s, 0) >= v:
                continue
            self.seen[eng][s] = v
            waits.append((s, v))
        if dma is None:
            sname, inc = own, 1
        else:
            sname, inc = self._mk("D_" + dma), 16
        self.cnt[sname] += inc
        tok = (sname, self.cnt[sname])
        self.ops[eng].append((waits, fn, sname, inc))
        for k in reads:
            b = self.bufs.setdefault(k, {"w": None, "r": []})
            b["r"].append(tok)
        for k in writes:
            b = self.bufs.setdefault(k, {"w": None, "r": []})
            b["w"] = tok
            b["r"] = []
        return tok

    def emit(self):
        S = self
        with self.nc.Block() as block:
            def run(name):
                def body(e):
                    for waits, fn, sname, inc in S.ops[name]:
                        for s, v in waits:
                            e.wait_ge(S.sem[s], v)
                        if fn is None:
                            e.nop().then_inc(S.sem[sname], inc)
                            continue
                        fn(e).then_inc(S.sem[sname], inc)
                return body
            block.tensor(run("pe"))
            block.scalar(run("act"))
            block.vector(run("dve"))
            block.gpsimd(run("pool"))
            block.sync(run("sp"))


def _program(nc, es, S, nt, wseq_known, dbg, D_, counts=None):
    xT, wA, wB, wada, vecsT, wspT, bsp, outT, wsA, wsB = D_
    op = S.op
    dry = S.dry

    def sb(name, shape, dt=F32):
        return es.enter_context(nc.sbuf_tensor(name, shape, dt))

    if not dry:
        hT = [sb(f"hT{i}", [128, 8, T]) for i in range(2)]
        yTf = sb("yTf", [128, 8, T], BF16)
        yTm = sb("yTm", [128, 8, T], BF16)
        gT = sb("gT", [128, 22, T], BF16)
        ssq = [sb(f"ssq{i}", [128, T]) for i in range(2)]
        ssqb = [sb(f"ssqb{i}", [128, T], BF16) for i in range(2)]
        tmpsq = sb("tmpsq", [128, T])
        slots = [sb(f"slot{i}", [128, 4096], BF16) for i in range(NSLOT)]
        tmpf = [sb(f"tmpf{i}", [128, T]) for i in range(2)]
        rstd = sb("rstd", [128, T])
        sact = [sb(f"sact{i}", [128, T]) for i in range(2)]
        vec = sb("vec", [128, NV])
        cact = sb("cact", [128, 16])
        cactb = sb("cactb", [128, 16], BF16)
        modT = sb("modT", [128, DEPTH, 72, 2])
        modA = sb("modA", [128, DEPTH, 3, 2, 8])
        modG = sb("modG", [128, DEPTH, 3, 2, 8])
        lbt = sb("lbt", [128, DEPTH, 4])
        omlt = sb("omlt", [128, DEPTH, 4])
        nomlt = sb("nomlt", [128, DEPTH, 4])
        lbd = sb("lbd", [128, 4])
        identb = sb("identb", [128, 128], BF16)
        identf = sb("identf", [128, 128])
        onesb = sb("onesb", [128, 128], BF16)
        cmask = sb("cmask", [128, 128])
        gmask = sb("gmask", [128, 128])
        rmask = sb("rmask", [128, T])
        WmT = sb("WmT", [128, DEPTH, 4, 128], BF16)
        bspb = sb("bspb", [128, DEPTH, 512])
        epsr = sb("epsr", [128, 1])
        epsl = sb("epsl", [128, 1])
        dummy = sb("dummyt", [128, 1])
        epraw = sb("epraw", [128, 4096], BF16)
        Ep = epraw[:, 0:2048].rearrange("p (h t) -> p h t", h=4)
        Ep2 = epraw[:, 2048:4096].rearrange("p (h t) -> p h t", h=4)
        epf = epraw[:].bitcast(F32)
        osb, o1, zz, z1 = epf[:, 0:512], epf[:, 512:1024], epf[:, 1024:1536], epf[:, 1536:2048]
        z2 = sb("z2", [128, 512])
        wspf = z2
        vg4 = epf[:, 0:2048].rearrange("p (a t) -> p a t", a=4)
        ktT = sb("ktT", [128, 4, T], BF16)
        kdT = sb("kdT", [128, 4, T], BF16)
        qtT = sb("qtT", [128, 4, T], BF16)
        qdT = sb("qdT", [128, 4, T], BF16)
        sgT = sb("sgT", [128, 4, T], BF16)
        ugT = sb("ugT", [128, 4, T], BF16)
        vi = sb("vi", [128, 4, 512], BF16)
        vhat = sb("vhat", [128, 4, 512], BF16)
        mixraw = sb("mixraw", [128, 8 * T], BF16)
        mixT = mixraw[:].rearrange("p (k t) -> p k t", k=8)
        mixf = mixraw[:].bitcast(F32)
        sig, logf, kk, bcs = mixf[:, 0:512], mixf[:, 512:1024], mixf[:, 1024:1536], mixf[:, 1536:2048]
        br = sb("br", [128, T])
        sigx = sb("sigx", [128, 3, T])
        Em = sb("Em", [128, T])
        er = sb("er", [128, 4, 8])
        wc = sb("wc", [128, 4, 8])
        dl = sb("dl", [128, 4, 8])
        Sst = sb("Sst", [128, DEPTH, 4, 128])
        Shat = sb("Shat", [128, 2, 4, 128], BF16)
        dst = sb("dst", [128, 4, 128])
        ktok = sb("ktok", [128, 4, 128], BF16)
        ATm = sb("ATm", [128, 4, 128], BF16)
        osq = sb("osq", [128, 512], BF16)
        zsq = sb("zsq", [128, 512], BF16)
        st1 = sb("st1", [128, 4, 4])
        st2 = sb("st2", [128, 4, 4])
        mean = sb("mean", [128, 4, 4])
        msq = sb("msq", [128, 4, 4])
        var = sb("var", [128, 4, 4])
        lrs = sb("lrs", [128, 4, 4])
        ps = [es.enter_context(nc.psum_tensor(f"ps{i}", [128, 512], F32)) for i in range(8)]

    def P(i):
        return ("ps", i)

    dbg_names = []

    wseq = [] if wseq_known is None else wseq_known
    wstate = {"issued": 0, "cur": 0, "seen": set()}

    def slot_key(i):
        return ("slot", i % NSLOT)

    def prefetch(upto):
        while wstate["issued"] <= min(upto, len(wseq) - 1):
            i = wstate["issued"]
            kind, l, blk = wseq[i]
            sl = slots[i % NSLOT]
            n = 2816 if kind == "B" else 4096
            src = {"A": wA, "B": wB, "ada": wada}[kind]
            scr = {"A": wsA, "B": wsB}.get(kind)
            bkey = (kind, l, blk)
            if scr is None or bkey not in wstate["seen"]:
                op("pool", lambda e, sl=sl, l=l, blk=blk, n=n, src=src: e.dma_start(out=sl[:, 0:n], in_=src[l, blk]),
                   writes=[slot_key(i)], dma=f"w{i % NSLOT}")
                if scr is not None:
                    wstate["seen"].add(bkey)
                    op("sp", lambda e, sl=sl, l=l, blk=blk, n=n, scr=scr: e.dma_start(out=scr[l, blk], in_=sl[:, 0:n]),
                       reads=[slot_key(i)], writes=[("scr",) + bkey], dma=f"sw{i % NSLOT}")
            else:
                op("sp", lambda e, sl=sl, l=l, blk=blk, n=n, scr=scr: e.dma_start(out=sl[:, 0:n], in_=scr[l, blk]),
                   reads=[("scr",) + bkey], writes=[slot_key(i)], dma=f"w{i % NSLOT}")
            wstate["issued"] += 1

    def next_w(spec):
        i = wstate["cur"]
        wstate["cur"] += 1
        if dry:
            wseq.append(spec)
            return None, None
        assert wseq[i] == spec, (wseq[i], spec)
        prefetch(i + NSLOT - 1)
        return slots[i % NSLOT], slot_key(i)

    if not dry:
        op("sp", lambda e: e.dma_start(out=vec[:], in_=vecsT), writes=["vec"], dma="vec")
        for l in range(DEPTH):
            op("sp", lambda e, l=l: e.dma_start(out=bspb[:, l, :], in_=bsp[l].partition_broadcast(128)),
               writes=[("bspb", l)], dma=f"bspb{l}")
        op("dve", lambda e: e.memset(epsr[:], RMS_EPS), writes=["epsr"])
        op("dve", lambda e: e.memset(epsl[:], LN_EPS), writes=["epsl"])
        op("pool", lambda e: e.memset(identf[:], 0.0), writes=["identf"])
        op("pool", lambda e: e.affine_select(out=identf[:], in_=identf[:], pattern=[[-1, 128]],
                                             compare_op=ALU.not_equal, fill=1.0, base=0, channel_multiplier=1),
           reads=["identf"], writes=["identf"])
        op("dve", lambda e: e.tensor_copy(out=identb[:], in_=identf[:]), reads=["identf"], writes=["identb"])
        op("dve", lambda e: e.memset(onesb[:], 1.0), writes=["onesb"])
        op("pool", lambda e: e.memset(cmask[:], 1.0), writes=["cmask"])
        op("pool", lambda e: e.affine_select(out=cmask[:], in_=cmask[:], pattern=[[1, 128]], compare_op=ALU.is_ge,
                                             fill=0.0, base=0, channel_multiplier=-1),
           reads=["cmask"], writes=["cmask"])
        for i_ in range(3):
            op("pool", lambda e, i_=i_: e.memset(cmask[i_ * 32:(i_ + 1) * 32, (i_ + 1) * 32:128], 0.0),
               reads=["cmask"], writes=["cmask"])
        op("pool", lambda e: e.memset(gmask[:], 1.0), writes=["gmask"])
        op("pool", lambda e: e.memset(gmask[64:128, 0:64], 0.0), reads=["gmask"], writes=["gmask"])
        op("dve", lambda e: e.memset(rmask[:], 1.0), writes=["rmask"])
        op("dve", lambda e: e.memset(rmask[:].rearrange("p (c t) -> p c t", t=64)[:, :, 0:1], 0.0),
           reads=["rmask"], writes=["rmask"])
        for l in range(DEPTH):
            op("sp", lambda e, l=l: e.dma_start(out=wspf[:], in_=wspT[l]), writes=["z2"], dma="wsp")
            op("dve", lambda e, l=l: e.tensor_tensor(
                out=WmT[:, l, :, :], in0=wspf[:].rearrange("p (h t) -> p h t", h=4),
                in1=gmask[:].unsqueeze(1).to_broadcast([128, 4, 128]), op=ALU.mult),
               reads=["z2", "gmask"], writes=[("WmT", l)])
        op("dve", lambda e: e.memset(lbt[:, 0, :], 0.0), writes=["lb0"])
        op("dve", lambda e: e.memset(omlt[:, 0, :], 1.0), writes=["oml0"])
        op("dve", lambda e: e.memset(nomlt[:, 0, :], -1.0), writes=["noml0"])
        op("dve", lambda e: e.tensor_tensor(out=lbd[:], in0=vec[:, C_LB + 4:C_LB + 8], in1=vec[:, C_LB:C_LB + 4],
                                            op=ALU.subtract), reads=["vec"], writes=["lbd"])
        op("act", lambda e: e.activation(out=lbt[:, 1, :], in_=lbd[:], func=AF.Sigmoid), reads=["lbd"], writes=["lb1"])
        op("act", lambda e: e.activation(out=omlt[:, 1, :], in_=lbd[:], func=AF.Sigmoid, scale=-1.0),
           reads=["lbd"], writes=["oml1"])
        op("dve", lambda e: e.tensor_scalar(out=nomlt[:, 1, :], in0=omlt[:, 1, :], scalar1=-1.0, scalar2=None,
                                            op0=ALU.mult), reads=["oml1"], writes=["noml1"])
        op("act", lambda e: e.activation(out=cact[:], in_=vec[:, C_C:C_C + 16], func=AF.Silu),
           reads=["vec"], writes=["cact"])
        op("dve", lambda e: e.tensor_copy(out=cactb[:], in_=cact[:]), reads=["cact"], writes=["cactb"])

    def dump(name, ap, keys):
        if not dbg or dry:
            return
        shp = [int(x) for x in ap.shape]
        dt_ = nc.dram_tensor("dbg_" + name, shp, ap.dtype, kind="ExternalOutput").ap()
        dbg_names.append("dbg_" + name)
        op("sp", lambda e: e.dma_start(out=dt_, in_=ap), reads=keys, writes=[("dbg", name)], dma="dbg_" + name)
        op("sp", None, reads=[("dbg", name)])

    def adaln(l):
        def mm_blk(cb, sl, sk):
            slv = sl[:, 0:4096].rearrange("p (k n) -> p k n", k=8)
            r = cb % 2

            def mm(e):
                for k in range(8):
                    ins = e.matmul(ps[r][0:2, :], lhsT=cactb[:, 2 * k:2 * k + 2], rhs=slv[:, k, :],
                                   start=(k == 0), stop=(k == 7))
                return ins
            op("pe", mm, reads=[sk, "cactb"], writes=[P(r)])
            op("act", lambda e: e.copy(out=sact[r][0:2, :], in_=ps[r][0:2, :]), reads=[P(r)], writes=[("sact", r)])

        def tr_blk(cb):
            r = cb % 2

            def tr(e):
                for q in range(4):
                    col = (cb * 4 + q) * 2
                    ins = e.transpose(out=ps[2][:, col:col + 2], in_=sact[r][0:2, q * 128:(q + 1) * 128],
                                      identity=identf[0:2, 0:2])
                return ins
            op("pe", tr, reads=[("sact", r), "identf"], writes=[P(2)])

        for cb in range(18):
            sl, sk = next_w(("ada", l, cb))
            if not dry:
                mm_blk(cb, sl, sk)
                if cb > 0:
                    tr_blk(cb - 1)
            if cb % 2 == 1:
                yield None
        if not dry:
            tr_blk(17)
        if dry:
            return
        pm = ps[2]
        op("dve", lambda e: e.tensor_tensor(
            out=modT[:, l, :, :], in0=pm[:, 0:144].rearrange("p (c b) -> p c b", b=2),
            in1=vec[:, C_BADA + l * 72:C_BADA + (l + 1) * 72].unsqueeze(2).to_broadcast([128, 72, 2]),
            op=ALU.add), reads=[P(2), "vec"], writes=[("modT", l)])
        for sub in range(3):
            for b in range(BPC):
                ng = vec[:, C_NG + (l * 3 + sub) * 8:C_NG + (l * 3 + sub) * 8 + 8]
                op("dve", lambda e, sub=sub, b=b, ng=ng: e.scalar_tensor_tensor(
                    out=modA[:, l, sub, b, :], in0=modT[:, l, (3 * sub + 1) * 8:(3 * sub + 2) * 8, b], scalar=1.0,
                    in1=ng, op0=ALU.add, op1=ALU.mult), reads=[("modT", l), "vec"], writes=[("modA", l, sub, b)])
                gs = 1.0 if sub == 1 else 0.5
                op("dve", lambda e, sub=sub, b=b, gs=gs: e.tensor_scalar(
                    out=modG[:, l, sub, b, :], in0=modT[:, l, (3 * sub + 2) * 8:(3 * sub + 3) * 8, b], scalar1=gs,
                    scalar2=None, op0=ALU.mult), reads=[("modT", l)], writes=[("modG", l, sub, b)])

    def rsqrt_psum(dst_t, dkey, pb, scale):
        op("act", lambda e: e.activation(out=dst_t[:], in_=ps[pb][:], func=AF.Ln, scale=scale, bias=epsr[:]),
           reads=[P(pb), "epsr"], writes=[dkey])
        op("act", lambda e: e.activation(out=dst_t[:], in_=dst_t[:], func=AF.Exp, scale=-0.5),
           reads=[dkey], writes=[dkey])

    def sq_accum(h, hkeys, si, dc):
        if dc == 0:
            op("pool", lambda e: e.tensor_tensor(out=ssq[si][:], in0=h[:, 0, :], in1=h[:, 0, :], op=ALU.mult),
               reads=[hkeys[0]], writes=[("ssq", si)])
            return
        op("pool", lambda e: e.tensor_tensor(out=tmpsq[:], in0=h[:, dc, :], in1=h[:, dc, :], op=ALU.mult),
           reads=[hkeys[dc]], writes=["tmpsq"])
        if dc < 7:
            op("pool", lambda e: e.tensor_tensor(out=ssq[si][:], in0=ssq[si][:], in1=tmpsq[:], op=ALU.add),
               reads=[("ssq", si), "tmpsq"], writes=[("ssq", si)])
        else:
            op("pool", lambda e: e.tensor_tensor(out=ssqb[si][:], in0=ssq[si][:], in1=tmpsq[:], op=ALU.add),
               reads=[("ssq", si), "tmpsq"], writes=[("ssqb", si)])

    def rms_squares(h, hkeys, si):
        for dc in range(8):
            sq_accum(h, hkeys, si, dc)

    def rms_rstd(si, pb):
        op("pe", lambda e: e.matmul(ps[pb][:], lhsT=onesb[:], rhs=ssqb[si][:], start=True, stop=True),
           reads=[("ssqb", si), "onesb"], writes=[P(pb)])
        rsqrt_psum(rstd, "rstd", pb, 1.0 / D)

    def norm_mod(h, hkeys, y, ykeys, l, sub, b, pb):
        for dc in range(8):
            tf = tmpf[dc % 2]
            op("dve", lambda e, dc=dc, tf=tf: e.scalar_tensor_tensor(
                out=tf[:], in0=h[:, dc, :], scalar=modA[:, l, sub, b, dc:dc + 1], in1=rstd[:],
                op0=ALU.mult, op1=ALU.mult),
               reads=[hkeys[dc], "rstd", ("modA", l, sub, b)], writes=[("tmpf", dc % 2)])
            op("pool", lambda e, dc=dc, tf=tf: e.tensor_scalar(
                out=y[:, dc, :], in0=tf[:], scalar1=modT[:, l, 3 * sub * 8 + dc, b:b + 1], scalar2=None, op0=ALU.add),
               reads=[("tmpf", dc % 2), ("modT", l)], writes=[ykeys[dc]])

    ykf = [("yf", dc) for dc in range(8)]
    ykm = [("ym", dc) for dc in range(8)]
    pctr = {"ab": 0, "o": 0, "mo": 0}

    def ffn(h, hkeys, l, f, b, si, need_sq=False):
        sub = 0 if f == 0 else 2
        yield ("acq", ("N", "Fy"), None)
        if not dry and need_sq:
            rms_squares(h, hkeys, si)
        yield None
        if not dry:
            rms_rstd(si, 0)
        yield None
        if not dry:
            norm_mod(h, hkeys, yTf, ykf, l, sub, b, 0)
        yield ("rel", ("N",))
        yield ("acq", ("Fg",), None)
        for blk in range(11):
            sl, sk = next_w(("A", l, f * 11 + blk))
            if not dry:
                slv = sl[:, 0:4096].rearrange("p (k n) -> p k n", k=8)
                for ch in range(2):
                    r = pctr["ab"] % 2
                    pctr["ab"] += 1
                    pa, pb = r, 2 + r
                    hc = blk * 2 + ch

                    def mm(e, slv=slv, ch=ch, pa=pa, pb=pb):
                        for k in range(8):
                            e.matmul(ps[pa][:], lhsT=slv[:, k, ch * 128:(ch + 1) * 128], rhs=yTf[:, k, :],
                                     start=(k == 0), stop=(k == 7))
                        for k in range(8):
                            ins = e.matmul(ps[pb][:], lhsT=slv[:, k, 256 + ch * 128:256 + (ch + 1) * 128],
                                           rhs=yTf[:, k, :], start=(k == 0), stop=(k == 7))
                        return ins
                    op("pe", mm, reads=[sk] + ykf, writes=[P(pa), P(pb)])
                    op("act", lambda e, pa=pa, r=r: e.activation(out=sact[r][:], in_=ps[pa][:], func=AF.Silu),
                       reads=[P(pa)], writes=[("sact", r)])
                    op("dve", lambda e, pb=pb, r=r, hc=hc: e.tensor_tensor(out=gT[:, hc, :], in0=sact[r][:],
                                                                           in1=ps[pb][:], op=ALU.mult),
                       reads=[("sact", r), P(pb)], writes=[("g", hc)])
            yield None
        yield ("rel", ("Fy",))
        gkeys = [("g", i) for i in range(22)]
        for dc in range(8):
            sl, sk = next_w(("B", l, f * 8 + dc))
            if not dry:
                slv = sl[:, 0:2816].rearrange("p (k n) -> p k n", k=22)
                po = 2 + pctr["o"] % 2
                pctr["o"] += 1

                def mm(e, slv=slv, po=po):
                    for k in range(22):
                        ins = e.matmul(ps[po][:], lhsT=slv[:, k, :], rhs=gT[:, k, :], start=(k == 0), stop=(k == 21))
                    return ins
                op("pe", mm, reads=[sk] + gkeys, writes=[P(po)])
                op("dve", lambda e, dc=dc, po=po: e.scalar_tensor_tensor(
                    out=h[:, dc, :], in0=ps[po][:], scalar=modG[:, l, sub, b, dc:dc + 1], in1=h[:, dc, :],
                    op0=ALU.mult, op1=ALU.add),
                   reads=[P(po), ("modG", l, sub, b), hkeys[dc]], writes=[hkeys[dc]])
                sq_accum(h, hkeys, si, dc)
            if dc % 2 == 1:
                yield None
        yield ("rel", ("Fg",))

    def c3(ap):
        return ap.rearrange("p (c t) -> p c t", t=64)

    def c32(ap):
        return ap.rearrange("p (c t) -> p c t", t=32)

    def h4(ap):
        return ap.rearrange("p (h t) -> p h t", h=4)

    mdone = [0] * DEPTH

    def mixer(h, hkeys, l, b, j, ti, si):
        yield ("acq", ("N", "M"), (lambda: mdone[l] == ti))
        yield None
        if not dry:
            rms_rstd(si, 4)
        yield None
        if not dry:
            norm_mod(h, hkeys, yTm, ykm, l, 1, b, 4)
        yield ("rel", ("N",))
        if j == 0 and not dry:
            op("dve", lambda e: e.memset(Sst[:, l, :, :], 0.0), writes=[("S", l)])

        def proj_fm(slv, hd, pb):
            def mm(e):
                for k in range(8):
                    ins = e.matmul(ps[pb][:], lhsT=slv[:, k, hd * 128:(hd + 1) * 128], rhs=yTm[:, k, :],
                                   start=(k == 0), stop=(k == 7))
                return ins
            return mm

        def proj_tm(slv, s, pb):
            def mm(e):
                for k in range(8):
                    ins = e.matmul(ps[pb][:], lhsT=yTm[:, k, s * 128:(s + 1) * 128], rhs=slv[:, k, :],
                                   start=(k == 0), stop=(k == 7))
                return ins
            return mm

        A_ = lambda sl: sl[:, 0:4096].rearrange("p (k n) -> p k n", k=8)
        sigs = None if dry else [(sig, "sig0"), (sigx[:, 0, :], "sig1"), (sigx[:, 1, :], "sig2"), (sigx[:, 2, :], "sig3")]
        sl, sk = next_w(("A", l, 23))
        if not dry:
            slv = A_(sl)
            for hd in range(4):
                op("pe", proj_fm(slv, hd, 4 + hd), reads=[sk] + ykm, writes=[P(4 + hd)])
            for hd in range(4):
                sb_, skey = sigs[hd]
                op("act", lambda e, hd=hd, sb_=sb_: e.activation(out=sb_, in_=ps[4 + hd][:], func=AF.Sigmoid),
                   reads=[P(4 + hd)], writes=[skey])
        yield None
        sl, sk = next_w(("A", l, 25))
        if not dry:
            slv = A_(sl)
            for hd in range(4):
                op("pe", proj_fm(slv, hd, 4 + hd), reads=[sk] + ykm, writes=[P(4 + hd)])
            for hd in range(4):
                tb, tk = (br, "br") if hd % 2 == 0 else (Em, "Em")
                op("act", lambda e, hd=hd, tb=tb: e.activation(out=tb[:], in_=ps[4 + hd][:], func=AF.Silu),
                   reads=[P(4 + hd)], writes=[tk])
                op("dve", lambda e, hd=hd, tb=tb: e.tensor_scalar(
                    out=sgT[:, hd, :], in0=tb[:], scalar1=vec[:, C_GN + l * 4 + hd:C_GN + l * 4 + hd + 1],
                    scalar2=None, op0=ALU.mult), reads=[tk, "vec"], writes=[("sg", hd)])
        yield None
        sl, sk = next_w(("A", l, 26))
        if not dry:
            slv = A_(sl)
            for hd in range(4):
                op("pe", proj_fm(slv, hd, 4 + hd), reads=[sk] + ykm, writes=[P(4 + hd)])
            for hd in range(4):
                op("act", lambda e, hd=hd: e.activation(out=ugT[:, hd, :], in_=ps[4 + hd][:],
                                                        func=AF.Gelu_apprx_tanh),
                   reads=[P(4 + hd)], writes=[("ug", hd)])
        yield None
        sl, sk = next_w(("A", l, 24))
        if not dry:
            slv = A_(sl)
            for s in range(4):
                op("pe", proj_tm(slv, s, 4 + s), reads=[sk] + ykm, writes=[P(4 + s)])
            for s in range(4):
                op("act", lambda e, s=s: e.copy(out=vi[:, s, :], in_=ps[4 + s][:]), reads=[P(4 + s)],
                   writes=[("vi", s)])
        yield None
        sl, sk = next_w(("A", l, 27))
        vgk4 = ["vg0", "vg1", "vg2", "vg3"]
        if not dry:
            slv = A_(sl)
            for s in range(4):
                op("pe", proj_tm(slv, s, 4 + s), reads=[sk] + ykm, writes=[P(4 + s)])
            for s in range(4):
                op("act", lambda e, s=s: e.activation(out=vg4[:, s, :], in_=ps[4 + s][:], func=AF.Gelu_apprx_tanh),
                   reads=[P(4 + s)], writes=[vgk4[s]])
            for s in range(4):
                op("dve", lambda e, s=s: e.tensor_reduce(out=st1[:, s, :], in_=h4(vg4[:, s, :]), axis=AX.X, op=ALU.add),
                   reads=[vgk4[s]], writes=["st1"])
                op("dve", lambda e, s=s: e.tensor_tensor(out=z2[:], in0=vg4[:, s, :], in1=vg4[:, s, :], op=ALU.mult),
                   reads=[vgk4[s]], writes=["z2"])
                op("dve", lambda e, s=s: e.tensor_reduce(out=st2[:, s, :], in_=h4(z2[:]), axis=AX.X, op=ALU.add),
                   reads=["z2"], writes=["st2"])
            op("pool", lambda e: e.tensor_scalar(out=mean[:], in0=st1[:], scalar1=1.0 / 128, scalar2=None,
                                                 op0=ALU.mult), reads=["st1"], writes=["mean"])
            op("pool", lambda e: e.tensor_tensor(out=msq[:], in0=mean[:], in1=mean[:], op=ALU.mult),
               reads=["mean"], writes=["msq"])
            op("pool", lambda e: e.tensor_scalar(out=var[:], in0=st2[:], scalar1=1.0 / 128, scalar2=None,
                                                 op0=ALU.mult), reads=["st2"], writes=["var"])
            op("pool", lambda e: e.tensor_tensor(out=var[:], in0=var[:], in1=msq[:], op=ALU.subtract),
               reads=["var", "msq"], writes=["var"])
        yield None
        if not dry:
            op("act", lambda e: e.activation(out=lrs[:], in_=var[:], func=AF.Ln, scale=1.0, bias=epsl[:]),
               reads=["var", "epsl"], writes=["lrs"])
            op("act", lambda e: e.activation(out=lrs[:], in_=lrs[:], func=AF.Exp, scale=-0.5),
               reads=["lrs"], writes=["lrs"])
            for s in range(4):
                for hd in range(4):
                    op("pool", lambda e, s=s, hd=hd: e.tensor_scalar(
                        out=vhat[:, s, hd * 128:(hd + 1) * 128], in0=vg4[:, s, hd * 128:(hd + 1) * 128],
                        scalar1=mean[:, s, hd:hd + 1], scalar2=lrs[:, s, hd:hd + 1], op0=ALU.subtract, op1=ALU.mult),
                       reads=[vgk4[s], "mean", "lrs"], writes=[("vhat", s, hd)])
        yield None
        for hd in range(4):
            if not dry:
                sb_, skey = sigs[hd]
                op("act", lambda e, hd=hd, sb_=sb_: e.activation(out=logf[:], in_=sb_, func=AF.Ln,
                                                                  scale=omlt[:, l, hd:hd + 1], bias=lbt[:, l, hd:hd + 1]),
                   reads=[skey, f"oml{l}", f"lb{l}"], writes=["logf"])
                op("dve", lambda e, hd=hd, sb_=sb_: e.tensor_scalar(out=kk[:], in0=sb_, scalar1=nomlt[:, l, hd:hd + 1],
                                                                    scalar2=omlt[:, l, hd:hd + 1], op0=ALU.mult, op1=ALU.add),
                   reads=[skey, f"oml{l}", f"noml{l}"], writes=["kk"])
                op("dve", lambda e: e.tensor_tensor_scan(out=bcs[:], data0=rmask[:], data1=logf[:], initial=0.0,
                                                         op0=ALU.mult, op1=ALU.add),
                   reads=["logf", "rmask"], writes=["bcs"])
                op("dve", lambda e: e.tensor_tensor(out=c3(br[:]), in0=c3(bcs[:]),
                                                    in1=c3(bcs[:])[:, :, 31:32].to_broadcast([128, 8, 64]),
                                                    op=ALU.subtract), reads=["bcs"], writes=["br"])
                op("act", lambda e, hd=hd: e.activation(out=Ep[:, hd, :], in_=br[:], func=AF.Exp),
                   reads=["br"], writes=[("Ep", hd), "vg0", "vg1", "vg2", "vg3"])
                op("act", lambda e: e.activation(out=Em[:], in_=br[:], func=AF.Exp, scale=-1.0),
                   reads=["br"], writes=["Em"])
                op("act", lambda e, hd=hd: e.activation(out=dl[:, hd, :], in_=c3(br[:])[:, :, 63], func=AF.Exp),
                   reads=["br"], writes=[("dl", hd)])
                op("act", lambda e, hd=hd: e.activation(out=er[:, hd, :], in_=c3(bcs[:])[:, :, 31], func=AF.Exp),
                   reads=["bcs"], writes=[("er", hd)])
                op("act", lambda e, hd=hd: e.activation(out=wc[:, hd, :], in_=c3(bcs[:])[:, :, 63], func=AF.Exp),
                   reads=["bcs"], writes=[("wc", hd)])
                op("dve", lambda e, hd=hd: e.tensor_tensor(out=ktT[:, hd, :], in0=kk[:], in1=Em[:], op=ALU.mult),
                   reads=["kk", "Em"], writes=[("kt", hd)])
                op("dve", lambda e: e.tensor_tensor(out=c32(br[:]), in0=c32(bcs[:]),
                                                    in1=c32(bcs[:])[:, :, 15:16].to_broadcast([128, 16, 32]),
                                                    op=ALU.subtract), reads=["bcs"], writes=["br"])
                op("act", lambda e, hd=hd: e.activation(out=Ep2[:, hd, :], in_=br[:], func=AF.Exp),
                   reads=["br"], writes=[("Ep2", hd), "vg0", "vg1", "vg2", "vg3"])
                op("act", lambda e: e.activation(out=Em[:], in_=br[:], func=AF.Exp, scale=-1.0),
                   reads=["br"], writes=["Em"])
                op("dve", lambda e, hd=hd: e.tensor_tensor(out=kdT[:, hd, :], in0=kk[:], in1=Em[:], op=ALU.mult),
                   reads=["kk", "Em"], writes=[("kd", hd)])
            yield None
        sl, sk = next_w(("A", l, 22))
        if not dry:
            slv = A_(sl)
            for hd in range(4):
                op("pe", proj_fm(slv, hd, 4 + hd), reads=[sk] + ykm, writes=[P(4 + hd)])
            for hd in range(4):
                pb = 4 + hd
                op("dve", lambda e, hd=hd, pb=pb: e.tensor_tensor(out=qtT[:, hd, :], in0=ps[pb][:], in1=Ep[:, hd, :],
                                                                  op=ALU.mult),
                   reads=[P(pb), ("Ep", hd)], writes=[("qt", hd)])
                op("dve", lambda e, hd=hd, pb=pb: e.tensor_tensor(out=qdT[:, hd, :], in0=ps[pb][:], in1=Ep2[:, hd, :],
                                                                  op=ALU.mult),
                   reads=[P(pb), ("Ep2", hd)], writes=[("qd", hd)])
        yield None
        if not dry:
            op("dve", lambda e: e.memset(dummy[:], 0.0),
               reads=[("Ep", hd) for hd in range(4)] + [("Ep2", hd) for hd in range(4)],
               writes=["osb", "o1", "zz", "z1", "dummy"])

        def subtile_steps(s):
            ts = slice(s * 128, (s + 1) * 128)
            kts = [("kt", hd) for hd in range(4)]
            qts = [("qt", hd) for hd in range(4)]
            ptr = ps[4][:].bitcast(BF16)[:, 0:512].rearrange("p (h k) -> p h k", h=4)
            pat = h4(ps[5][:])
            pat2 = h4(ps[6][:])
            pdb = (4, 7)
            po = h4(ps[6][:])
            pmm = h4(ps[7][:])

            def st_a():
                def tr(e):
                    for hd in range(4):
                        ins = e.transpose(out=ptr[:, hd, :], in_=ktT[:, hd, ts], identity=identb[:])
                    return ins
                op("pe", tr, reads=kts + ["identb"], writes=[P(4)])

                def at(e):
                    for hd in range(4):
                        ins = e.matmul(pat[:, hd, :], lhsT=kdT[:, hd, ts], rhs=qdT[:, hd, ts], start=True, stop=True)
                    return ins
                op("pe", at, reads=[("kd", hd) for hd in range(4)] + [("qd", hd) for hd in range(4)], writes=[P(5)])

                def at2(e):
                    for hd in range(4):
                        for cc in range(2):
                            r0 = cc * 64
                            t0 = s * 128 + cc * 64
                            ins = e.matmul(pat2[r0:r0 + 32, hd, r0 + 32:r0 + 64], lhsT=ktT[:, hd, t0:t0 + 32],
                                           rhs=qtT[:, hd, t0 + 32:t0 + 64], start=True, stop=True)
                    return ins
                op("pe", at2, reads=kts + qts, writes=[P(6)])

                def gm(e):
                    for hd in range(4):
                        ins = e.matmul(pmm[:, hd, :], lhsT=vhat[:, s, hd * 128:(hd + 1) * 128], rhs=WmT[:, l, hd, :],
                                       start=True, stop=True)
                    return ins
                op("pe", gm, reads=[("vhat", s, hd) for hd in range(4)] + [("WmT", l)], writes=[P(7)])
                op("act", lambda e: e.copy(out=ktok[:], in_=ptr), reads=[P(4)], writes=["ktok"])
                op("dve", lambda e: e.tensor_tensor(out=ATm[:], in0=pat,
                                                    in1=cmask[:].unsqueeze(1).to_broadcast([128, 4, 128]),
                                                    op=ALU.mult), reads=[P(5), "cmask"], writes=["ATm"])
                for cc in range(2):
                    r0 = cc * 64
                    op("act", lambda e, r0=r0: e.copy(out=ATm[r0:r0 + 32, :, r0 + 32:r0 + 64],
                                                      in_=pat2[r0:r0 + 32, :, r0 + 32:r0 + 64]),
                       reads=[P(6)], writes=["ATm"])
                for hd in range(4):
                    op("dve", lambda e, hd=hd: e.scalar_tensor_tensor(
                        out=z1[:, hd * 128:(hd + 1) * 128], in0=pmm[:, hd, :],
                        scalar=vec[:, C_LNG + l * 4 + hd:C_LNG + l * 4 + hd + 1],
                        in1=bspb[:, l, hd * 128:(hd + 1) * 128], op0=ALU.mult, op1=ALU.add),
                       reads=[P(7), "vec", ("bspb", l)], writes=["z1"])
                op("dve", lambda e: e.tensor_tensor(out=h4(zz[:]), in0=h4(z1[:]), in1=ugT[:, :, ts], op=ALU.mult),
                   reads=["z1"] + [("ug", hd) for hd in range(4)], writes=["zz"])
                op("act", lambda e: e.activation(out=zsq[:], in_=zz[:], func=AF.Square), reads=["zz"], writes=["zsq"])

            def st_b():
                for cc in range(2):
                    pds = h4(ps[pdb[cc]][:])
                    rows = slice(cc * 64, (cc + 1) * 64)

                    def dsm(e, pds=pds, rows=rows):
                        for hd in range(4):
                            ins = e.matmul(pds[:, hd, :], lhsT=ktok[rows, hd, :], rhs=vi[rows, s, hd * 128:(hd + 1) * 128],
                                           start=True, stop=True)
                        return ins
                    op("pe", dsm, reads=["ktok", ("vi", s)], writes=[P(pdb[cc])])
                op("pe", lambda e: e.matmul(ps[5][:], lhsT=onesb[:], rhs=zsq[:], start=True, stop=True),
                   reads=["zsq", "onesb"], writes=[P(5)])
                for cc in range(2):
                    c = 2 * s + cc
                    pds = h4(ps[pdb[cc]][:])

                    def bc(t, c=c):
                        return t[:, :, c:c + 1].to_broadcast([128, 4, 128])
                    op("dve", lambda e, cc=cc, bc=bc: e.tensor_tensor(out=Shat[:, cc, :, :], in0=Sst[:, l, :, :],
                                                                      in1=bc(er), op=ALU.mult),
                       reads=[("S", l)] + [("er", hd) for hd in range(4)], writes=[("Shat", cc)])
                    op("dve", lambda e, pds=pds, bc=bc: e.tensor_tensor(out=dst[:], in0=pds, in1=bc(dl), op=ALU.mult),
                       reads=[P(pdb[cc])] + [("dl", hd) for hd in range(4)], writes=["dst"])
                    op("dve", lambda e, bc=bc: e.tensor_tensor(out=Sst[:, l, :, :], in0=Sst[:, l, :, :], in1=bc(wc),
                                                               op=ALU.mult),
                       reads=[("S", l)] + [("wc", hd) for hd in range(4)], writes=[("S", l)])
                    op("dve", lambda e: e.tensor_tensor(out=Sst[:, l, :, :], in0=Sst[:, l, :, :], in1=dst[:],
                                                        op=ALU.add), reads=[("S", l), "dst"], writes=[("S", l)])
                rsqrt_psum(z2, "z2", 5, 1.0 / 128)
                op("dve", lambda e: e.tensor_tensor(out=z2[:], in0=z2[:], in1=zz[:], op=ALU.mult),
                   reads=["z2", "zz"], writes=["z2"])
                for hd in range(4):
                    op("dve", lambda e, hd=hd: e.tensor_scalar(
                        out=mixT[:, 4 + hd, ts], in0=z2[:, hd * 128:(hd + 1) * 128],
                        scalar1=vec[:, C_OG + l * 4 + hd:C_OG + l * 4 + hd + 1], scalar2=None, op0=ALU.mult),
                       reads=["z2", "vec"], writes=[("mixg", s, hd)])

            def st_c():
                def om(e):
                    for hd in range(4):
                        e.matmul(po[:, hd, :], lhsT=vi[:, s, hd * 128:(hd + 1) * 128], rhs=ATm[:, hd, :],
                                 start=True, stop=False)
                        for cc in range(2):
                            tc = slice(s * 128 + cc * 64, s * 128 + (cc + 1) * 64)
                            ins = e.matmul(po[:, hd, cc * 64:(cc + 1) * 64], lhsT=Shat[:, cc, hd, :],
                                           rhs=qtT[:, hd, tc], start=False, stop=(cc == 1))
                    return ins
                op("pe", om, reads=[("vi", s), "ATm", ("Shat", 0), ("Shat", 1)] + qts, writes=[P(6)])
                op("act", lambda e: e.copy(out=osb[:], in_=ps[6][:]), reads=[P(6)], writes=["osb"])
                op("act", lambda e: e.activation(out=osq[:], in_=ps[6][:], func=AF.Square), reads=[P(6)],
                   writes=["osq"])

            def st_d():
                op("pe", lambda e: e.matmul(ps[5][:], lhsT=onesb[:], rhs=osq[:], start=True, stop=True),
                   reads=["osq", "onesb"], writes=[P(5)])
                rsqrt_psum(o1, "o1", 5, 1.0 / 128)
                op("dve", lambda e: e.tensor_tensor(out=o1[:], in0=o1[:], in1=osb[:], op=ALU.mult),
                   reads=["o1", "osb"], writes=["o1"])
                op("dve", lambda e: e.tensor_tensor(out=mixT[:, 0:4, ts], in0=h4(o1[:]), in1=sgT[:, :, ts],
                                                    op=ALU.mult),
                   reads=["o1"] + [("sg", hd) for hd in range(4)], writes=[("mix", s)])
            return [st_a, st_b, st_c, st_d]

        for s in range(4):
            steps = [None] * 4 if dry else subtile_steps(s)
            for st in steps:
                if st is not None:
                    st()
                yield None
        mkeys = [("mix", s) for s in range(4)] + [("mixg", s, hd) for s in range(4) for hd in range(4)]
        for wb in range(2):
            sl, sk = next_w(("A", l, 28 + wb))
            if not dry:
                slv = sl[:, 0:4096].rearrange("p (k n) -> p k n", k=8)
                for dcl in range(4):
                    dc = wb * 4 + dcl
                    po = 4 + pctr["mo"] % 4
                    pctr["mo"] += 1

                    def mm(e, slv=slv, dcl=dcl, po=po):
                        for k in range(8):
                            ins = e.matmul(ps[po][:], lhsT=slv[:, k, dcl * 128:(dcl + 1) * 128], rhs=mixT[:, k, :],
                                           start=(k == 0), stop=(k == 7))
                        return ins
                    op("pe", mm, reads=[sk] + mkeys, writes=[P(po)])
                    op("dve", lambda e, dc=dc, po=po: e.scalar_tensor_tensor(
                        out=h[:, dc, :], in0=ps[po][:], scalar=modG[:, l, 1, b, dc:dc + 1], in1=h[:, dc, :],
                        op0=ALU.mult, op1=ALU.add),
                       reads=[P(po), ("modG", l, 1, b), hkeys[dc]], writes=[hkeys[dc]])
                    sq_accum(h, hkeys, si, dc)
            yield None
        mdone[l] = ti + 1
        yield ("rel", ("M",))

    tile_list = [(b, j) for b in range(BPC) for j in range(nt)]
    out_keys = []

    def load_tile(ti, hb):
        if dry:
            return
        b, j = tile_list[ti]
        src = xT[b].rearrange("(dc p) t -> p dc t", p=128)[:, :, j * T:(j + 1) * T]
        op("pool", lambda e, hb=hb, src=src: e.dma_start(out=hT[hb][:], in_=src),
           writes=[("h", hb, dc) for dc in range(8)], dma=f"x{hb}")

    def stream(si):
        hb = si
        h = None if dry else hT[hb]
        hkeys = [("h", hb, dc) for dc in range(8)]
        if si == 1:
            yield ("sync",)
        for ti in range(si, len(tile_list), 2):
            b, j = tile_list[ti]
            load_tile(ti, hb)
            for l in range(DEPTH):
                dd = (ti == 0 and l == 0)
                yield from ffn(h, hkeys, l, 0, b, si, need_sq=(l == 0))
                if dd:
                    dump("h_ffn1", None if dry else h[:], hkeys)
                yield ("sync",)
                yield from mixer(h, hkeys, l, b, j, ti, si)
                yield ("sync",)
                if dd:
                    dump("h_mix", None if dry else h[:], hkeys)
                if ti == 0 and l == 0:
                    for r in ffn(h, hkeys, l, 1, b, si):
                        if isinstance(r, tuple) and r[0] == "rel" and r[1] == ("Fg",):
                            yield from adaln(1)
                        yield r
                else:
                    yield from ffn(h, hkeys, l, 1, b, si)
                if dd:
                    dump("h_ffn2", None if dry else h[:], hkeys)
            yield ("acq", ("N",), None)
            yield None
            if not dry:
                rms_rstd(si, 0)
                for dc in range(8):
                    og = tmpf[dc % 2]
                    op("dve", lambda e, dc=dc, og=og: e.scalar_tensor_tensor(
                        out=og[:], in0=h[:, dc, :], scalar=vec[:, C_FG + dc:C_FG + dc + 1], in1=rstd[:],
                        op0=ALU.mult, op1=ALU.mult), reads=[hkeys[dc], "vec", "rstd"], writes=[("tmpf", dc % 2)])
                    ok = ("out", ti, dc)
                    op("pool", lambda e, dc=dc, og=og, b=b, j=j: e.dma_start(
                        out=outT[b, dc * 128:(dc + 1) * 128, j * T:(j + 1) * T], in_=og[:]),
                       reads=[("tmpf", dc % 2)], writes=[ok], dma=f"o{dc % 2}")
                    out_keys.append(ok)
            yield ("rel", ("N",))

    for _ in adaln(0):
        pass

    gens = [stream(0), stream(1)]
    active = [True, len(tile_list) > 1]
    pending = [None, None]
    at_sync = [False, False]
    held = {}
    seg_idx = [0, 0]
    done = [0, 0]
    seg_counts = [[0], [0]]

    def frac(si):
        if counts is None or seg_idx[si] >= len(counts[si]):
            return done[si]
        return done[si] / max(counts[si][seg_idx[si]], 1)

    while any(active):
        cands = [si for si in (0, 1) if active[si] and not at_sync[si]]
        if not cands:
            for si in (0, 1):
                if at_sync[si]:
                    at_sync[si] = False
                    seg_idx[si] += 1
                    done[si] = 0
                    seg_counts[si].append(0)
            continue
        cands.sort(key=lambda si: (frac(si), si))
        advanced = False
        for si in cands:
            if pending[si] is not None:
                locks, cond = pending[si]
                if all(k not in held for k in locks) and (cond is None or cond()):
                    for k in locks:
                        held[k] = si
                    pending[si] = None
                else:
                    continue
            advanced = True
            try:
                r = next(gens[si])
            except StopIteration:
                active[si] = False
                break
            done[si] += 1
            seg_counts[si][-1] += 1
            if isinstance(r, tuple):
                if r[0] == "acq":
                    pending[si] = (r[1], r[2])
                elif r[0] == "rel":
                    for k in r[1]:
                        assert held.get(k) == si, (k, held, si)
                        del held[k]
                elif r[0] == "sync":
                    at_sync[si] = True
            break
        assert advanced, ("scheduler deadlock", pending, held, at_sync)
    if dry:
        return wseq, seg_counts
    assert wstate["cur"] == len(wseq), (wstate, len(wseq))
    op("sp", None, reads=out_keys[-16:])
    return wseq, dbg_names


def build_nc(nt=SEQ // T, dbg=False):
    nc = bass.Bass("TRN2", target_bir_lowering=False)
    xT = nc.dram_tensor("xT", [BPC, D, SEQ], F32, kind="ExternalInput").ap()
    wA = nc.dram_tensor("wA", [DEPTH, 30, 128, 4096], F32, kind="ExternalInput").ap()
    wB = nc.dram_tensor("wB", [DEPTH, 16, 128, 2816], F32, kind="ExternalInput").ap()
    wada = nc.dram_tensor("wada", [DEPTH, 18, 128, 4096], F32, kind="ExternalInput").ap()
    vecsT = nc.dram_tensor("vecsT", [128, NV], F32, kind="ExternalInput").ap()
    wspT = nc.dram_tensor("wspT", [DEPTH, 128, 512], F32, kind="ExternalInput").ap()
    bsp = nc.dram_tensor("bsp", [DEPTH, 512], F32, kind="ExternalInput").ap()
    outT = nc.dram_tensor("outT", [BPC, D, SEQ], F32, kind="ExternalOutput").ap()
    wsA = nc.dram_tensor("wsA", [DEPTH, 30, 128, 4096], BF16, kind="Internal").ap()
    wsB = nc.dram_tensor("wsB", [DEPTH, 16, 128, 2816], BF16, kind="Internal").ap()
    D_ = (xT, wA, wB, wada, vecsT, wspT, bsp, outT, wsA, wsB)
    _, counts = _program(nc, None, Sched(nc, None, dry=True), nt, None, dbg, D_)
    wseq, _ = _program(nc, None, Sched(nc, None, dry=True), nt, None, dbg, D_, counts=counts)
    print("segment step counts", [c[:6] for c in counts])
    with ExitStack() as es:
        S = Sched(nc, es)
        _, dbg_names = _program(nc, es, S, nt, wseq, dbg, D_, counts=counts)
        print("sbuf bytes remaining", nc.sbuf_bytes_remaining, "ops", {k: len(v) for k, v in S.ops.items()})
        S.emit()
    nc._dbg_names = dbg_names
    return nc


def _blkA(W):
    return W.reshape(8, 128, 512).transpose(1, 0, 2).reshape(128, 4096)


def _blkB(W):
    return W.reshape(22, 128, 128).transpose(1, 0, 2).reshape(128, 2816)


def _prep_shared(w_ada, b_ada, norm_gain, ffn1_w13, ffn1_w2, w_in, hg_lb_logits, hg_gnorm, gm_ln_gain,
                 gm_w_spatial, gm_b_spatial, gm_out_gain, w_out, ffn2_w13, ffn2_w2, final_gain):
    f = lambda a: np.asarray(a, dtype=np.float32)
    wA = np.empty((DEPTH, 30, 128, 4096), np.float32)
    wB = np.empty((DEPTH, 16, 128, 2816), np.float32)
    wada = np.empty((DEPTH, 18, 128, 4096), np.float32)
    for l in range(DEPTH):
        for fi, (w13, w2) in enumerate(((f(ffn1_w13)[l], f(ffn1_w2)[l]), (f(ffn2_w13)[l], f(ffn2_w2)[l]))):
            for i in range(11):
                blk = np.concatenate([w13[:, 256 * i:256 * (i + 1)], w13[:, DFF + 256 * i:DFF + 256 * (i + 1)]], axis=1)
                wA[l, fi * 11 + i] = _blkA(blk)
            for i in range(8):
                wB[l, fi * 8 + i] = _blkB(w2[:, 128 * i:128 * (i + 1)])
        for i in range(6):
            wA[l, 22 + i] = _blkA(f(w_in)[l][:, 512 * i:512 * (i + 1)])
        for i in range(2):
            wA[l, 28 + i] = _blkA(f(w_out)[l][:, 512 * i:512 * (i + 1)])
        for i in range(18):
            wada[l, i] = _blkA(f(w_ada)[l][:, 512 * i:512 * (i + 1)])
    vecs = np.zeros((128, NV), np.float32)
    for l in range(DEPTH):
        vecs[:, C_BADA + l * 72:C_BADA + (l + 1) * 72] = f(b_ada)[l].reshape(72, 128).T
        for sub in range(3):
            c0 = C_NG + (l * 3 + sub) * 8
            vecs[:, c0:c0 + 8] = f(norm_gain)[l, sub].reshape(8, 128).T
        vecs[:, C_LB + l * 4:C_LB + l * 4 + 4] = f(hg_lb_logits)[l].reshape(4, 128).T
        vecs[:, C_GN + l * 4:C_GN + l * 4 + 4] = f(hg_gnorm)[l].reshape(4, 128).T
        vecs[:, C_LNG + l * 4:C_LNG + l * 4 + 4] = f(gm_ln_gain)[l].reshape(4, 128).T
        vecs[:, C_OG + l * 4:C_OG + l * 4 + 4] = f(gm_out_gain)[l].reshape(4, 128).T
    vecs[:, C_FG:C_FG + 8] = f(final_gain).reshape(8, 128).T
    wspT = np.ascontiguousarray(f(gm_w_spatial).transpose(0, 3, 1, 2).reshape(DEPTH, 128, 512))
    bspv = np.ascontiguousarray(f(gm_b_spatial).reshape(DEPTH, 512))
    return wA, wB, wada, vecs, wspT, bspv


def kernel(x, c, w_ada, b_ada, norm_gain, ffn1_w13, ffn1_w2, w_in, hg_lb_logits, hg_gnorm, gm_ln_gain,
           gm_w_spatial, gm_b_spatial, gm_out_gain, w_out, ffn2_w13, ffn2_w2, final_gain, _nt=SEQ // T,
           _cores=NCORES, _dbg=None):
    x = np.asarray(x, dtype=np.float32)
    c = np.asarray(c, dtype=np.float32)
    wA, wB, wada, vecs, wspT, bspv = _prep_shared(
        w_ada, b_ada, norm_gain, ffn1_w13, ffn1_w2, w_in, hg_lb_logits, hg_gnorm, gm_ln_gain, gm_w_spatial,
        gm_b_spatial, gm_out_gain, w_out, ffn2_w13, ffn2_w2, final_gain)
    in_maps = []
    for core in range(_cores):
        xs = x[core * BPC:(core + 1) * BPC]
        xTc = np.ascontiguousarray(xs.transpose(0, 2, 1))
        v = vecs.copy()
        cs = c[core * BPC:(core + 1) * BPC]
        v[:, C_C:C_C + 16] = cs.reshape(BPC, 8, 128).transpose(2, 1, 0).reshape(128, 16)
        in_maps.append({"xT": xTc, "wA": wA, "wB": wB, "wada": wada, "vecsT": v, "wspT": wspT, "bsp": bspv})
    nc = build_nc(_nt, dbg=_dbg is not None)
    res = run_bass_kernel_spmd(nc, in_maps, core_ids=list(range(_cores)))
    out = np.empty((NCORES * BPC, SEQ, D), np.float32)
    if _dbg is not None:
        for k in nc._dbg_names:
            _dbg[k] = np.asarray(res.results[0][k])
    for core in range(_cores):
        out[core * BPC:(core + 1) * BPC] = res.results[core]["outT"].transpose(0, 2, 1)
    return out
```

```python
from contextlib import ExitStack
import numpy as np
import concourse.bass as bass
import concourse.mybir as mybir
from concourse.bass_utils import run_bass_kernel_spmd

F32 = mybir.dt.float32
BF16 = mybir.dt.bfloat16
AF = mybir.ActivationFunctionType
ALU = mybir.AluOpType
AX = mybir.AxisListType

D = 1024
SEQ = 2048
DEPTH = 2
DFF = 2816
T = 512
NCORES = 8
BPC = 2
RMS_EPS = 1e-6
LN_EPS = 1e-5
NSLOT = 4

C_BADA = 0
C_NG = 144
C_LB = 192
C_GN = 200
C_LNG = 208
C_OG = 216
C_FG = 224
C_C = 232
NV = 256


class Sched:
    ENGS = ("pe", "act", "dve", "pool", "sp")

    def __init__(self, nc, es, dry=False):
        self.nc = nc
        self.es = es
        self.dry = dry
        self.ops = {e: [] for e in self.ENGS}
        self.sem = {}
        self.cnt = {}
        self.seen = {e: {} for e in self.ENGS}
        self.bufs = {}
        for e in self.ENGS:
            self._mk("E_" + e)

    def _mk(self, name):
        if name not in self.cnt:
            if not self.dry:
                self.sem[name] = self.es.enter_context(self.nc.semaphore(name))
            self.cnt[name] = 0
        return name

    def op(self, eng, fn, reads=(), writes=(), dma=None):
        if self.dry:
            return None
        deps = {}
        for k in reads:
            b = self.bufs.get(k)
            if b and b["w"]:
                s, v = b["w"]
                deps[s] = max(deps.get(s, 0), v)
        for k in writes:
            b = self.bufs.get(k)
            if b:
                if b["w"]:
                    s, v = b["w"]
                    deps[s] = max(deps.get(s, 0), v)
                for s, v in b["r"]:
                    deps[s] = max(deps.get(s, 0), v)
        own = "E_" + eng
        waits = []
        for s, v in deps.items():
            if s == own and eng == "pe":
                continue
            if self.seen[eng].get(s, 0) >= v:
                continue
            self.seen[eng][s] = v
            waits.append((s, v))
        if dma is None:
            sname, inc = own, 1
        else:
            sname, inc = self._mk("D_" + dma), 16
        self.cnt[sname] += inc
        tok = (sname, self.cnt[sname])
        self.ops[eng].append((waits, fn, sname, inc))
        for k in reads:
            b = self.bufs.setdefault(k, {"w": None, "r": []})
            b["r"].append(tok)
        for k in writes:
            b = self.bufs.setdefault(k, {"w": None, "r": []})
            b["w"] = tok
            b["r"] = []
        return tok

    def emit(self):
        S = self
        with self.nc.Block() as block:
            def run(name):
                def body(e):
                    for waits, fn, sname, inc in S.ops[name]:
                        for s, v in waits:
                            e.wait_ge(S.sem[s], v)
                        if fn is None:
                            e.nop().then_inc(S.sem[sname], inc)
                            continue
                        fn(e).then_inc(S.sem[sname], inc)
                return body
            block.tensor(run("pe"))
            block.scalar(run("act"))
            block.vector(run("dve"))
            block.gpsimd(run("pool"))
            block.sync(run("sp"))


def _program(nc, es, S, nt, wseq_known, dbg, D_, counts=None):
    xT, wA, wB, wada, vecsT, wspT, bsp, outT, wsA, wsB = D_
    op = S.op
    dry = S.dry

    def sb(name, shape, dt=F32):
        return es.enter_context(nc.sbuf_tensor(name, shape, dt))

    if not dry:
        hT = [sb(f"hT{i}", [128, 8, T]) for i in range(2)]
        yTf = sb("yTf", [128, 8, T], BF16)
        yTm = sb("yTm", [128, 8, T], BF16)
        gT = sb("gT", [128, 22, T], BF16)
        sq = sb("sq", [128, 8, T], BF16)
        slots = [sb(f"slot{i}", [128, 4096], BF16) for i in range(NSLOT)]
        tmpf = [sb(f"tmpf{i}", [128, T]) for i in range(2)]
        rstd = sb("rstd", [128, T])
        sact = [sb(f"sact{i}", [128, T]) for i in range(2)]
        vec = sb("vec", [128, NV])
        cact = sb("cact", [128, 16])
        cactb = sb("cactb", [128, 16], BF16)
        modT = sb("modT", [128, DEPTH, 72, 2])
        modA = sb("modA", [128, DEPTH, 3, 2, 8])
        modG = sb("modG", [128, DEPTH, 3, 2, 8])
        lbt = sb("lbt", [128, DEPTH, 4])
        omlt = sb("omlt", [128, DEPTH, 4])
        nomlt = sb("nomlt", [128, DEPTH, 4])
        lbd = sb("lbd", [128, 4])
        identb = sb("identb", [128, 128], BF16)
        identf = sb("identf", [128, 128])
        onesb = sb("onesb", [128, 128], BF16)
        cmask = sb("cmask", [128, 128])
        gmask = sb("gmask", [128, 128])
        rmask = sb("rmask", [128, T])
        WmT = sb("WmT", [128, DEPTH, 4, 128], BF16)
        bspb = sb("bspb", [128, DEPTH, 512])
        epsr = sb("epsr", [128, 1])
        epsl = sb("epsl", [128, 1])
        dummy = sb("dummyt", [128, 1])
        epraw = sb("epraw", [128, 4096], BF16)
        Ep = epraw[:, 0:2048].rearrange("p (h t) -> p h t", h=4)
        Ep2 = epraw[:, 2048:4096].rearrange("p (h t) -> p h t", h=4)
        epf = epraw[:].bitcast(F32)
        osb, o1, zz, z1 = epf[:, 0:512], epf[:, 512:1024], epf[:, 1024:1536], epf[:, 1536:2048]
        z2 = sb("z2", [128, 512])
        wspf = z2
        vg4 = epf[:, 0:2048].rearrange("p (a t) -> p a t", a=4)
        ktT = sb("ktT", [128, 4, T], BF16)
        kdT = sb("kdT", [128, 4, T], BF16)
        qtT = sb("qtT", [128, 4, T], BF16)
        qdT = sb("qdT", [128, 4, T], BF16)
        sgT = sb("sgT", [128, 4, T], BF16)
        ugT = sb("ugT", [128, 4, T], BF16)
        vi = sb("vi", [128, 4, 512], BF16)
        vhat = sb("vhat", [128, 4, 512], BF16)
        mixraw = sb("mixraw", [128, 8 * T], BF16)
        mixT = mixraw[:].rearrange("p (k t) -> p k t", k=8)
        mixf = mixraw[:].bitcast(F32)
        sig, logf, kk, bcs = mixf[:, 0:512], mixf[:, 512:1024], mixf[:, 1024:1536], mixf[:, 1536:2048]
        br = sb("br", [128, T])
        sigx = sb("sigx", [128, 3, T])
        Em = sb("Em", [128, T])
        er = sb("er", [128, 4, 8])
        wc = sb("wc", [128, 4, 8])
        dl = sb("dl", [128, 4, 8])
        Sst = sb("Sst", [128, DEPTH, 4, 128])
        Shat = sb("Shat", [128, 2, 4, 128], BF16)
        dst = sb("dst", [128, 4, 128])
        ktok = sb("ktok", [128, 4, 128], BF16)
        ATm = sb("ATm", [128, 4, 128], BF16)
        osq = sb("osq", [128, 512], BF16)
        zsq = sb("zsq", [128, 512], BF16)
        st1 = sb("st1", [128, 4, 4])
        st2 = sb("st2", [128, 4, 4])
        mean = sb("mean", [128, 4, 4])
        msq = sb("msq", [128, 4, 4])
        var = sb("var", [128, 4, 4])
        lrs = sb("lrs", [128, 4, 4])
        ps = [es.enter_context(nc.psum_tensor(f"ps{i}", [128, 512], F32)) for i in range(8)]

    def P(i):
        return ("ps", i)

    dbg_names = []

    wseq = [] if wseq_known is None else wseq_known
    wstate = {"issued": 0, "cur": 0, "seen": set()}

    def slot_key(i):
        return ("slot", i % NSLOT)

    def prefetch(upto):
        while wstate["issued"] <= min(upto, len(wseq) - 1):
            i = wstate["issued"]
            kind, l, blk = wseq[i]
            sl = slots[i % NSLOT]
            n = 2816 if kind == "B" else 4096
            src = {"A": wA, "B": wB, "ada": wada}[kind]
            scr = {"A": wsA, "B": wsB}.get(kind)
            bkey = (kind, l, blk)
            if scr is None or bkey not in wstate["seen"]:
                op("pool", lambda e, sl=sl, l=l, blk=blk, n=n, src=src: e.dma_start(out=sl[:, 0:n], in_=src[l, blk]),
                   writes=[slot_key(i)], dma=f"w{i % NSLOT}")
                if scr is not None:
                    wstate["seen"].add(bkey)
                    op("sp", lambda e, sl=sl, l=l, blk=blk, n=n, scr=scr: e.dma_start(out=scr[l, blk], in_=sl[:, 0:n]),
                       reads=[slot_key(i)], writes=[("scr",) + bkey], dma=f"sw{i % NSLOT}")
            else:
                op("sp", lambda e, sl=sl, l=l, blk=blk, n=n, scr=scr: e.dma_start(out=sl[:, 0:n], in_=scr[l, blk]),
                   reads=[("scr",) + bkey], writes=[slot_key(i)], dma=f"w{i % NSLOT}")
            wstate["issued"] += 1

    def next_w(spec):
        i = wstate["cur"]
        wstate["cur"] += 1
        if dry:
            wseq.append(spec)
            return None, None
        assert wseq[i] == spec, (wseq[i], spec)
        prefetch(i + NSLOT - 1)
        return slots[i % NSLOT], slot_key(i)

    if not dry:
        op("sp", lambda e: e.dma_start(out=vec[:], in_=vecsT), writes=["vec"], dma="vec")
        for l in range(DEPTH):
            op("sp", lambda e, l=l: e.dma_start(out=bspb[:, l, :], in_=bsp[l].partition_broadcast(128)),
               writes=[("bspb", l)], dma=f"bspb{l}")
        op("dve", lambda e: e.memset(epsr[:], RMS_EPS), writes=["epsr"])
        op("dve", lambda e: e.memset(epsl[:], LN_EPS), writes=["epsl"])
        op("pool", lambda e: e.memset(identf[:], 0.0), writes=["identf"])
        op("pool", lambda e: e.affine_select(out=identf[:], in_=identf[:], pattern=[[-1, 128]],
                                             compare_op=ALU.not_equal, fill=1.0, base=0, channel_multiplier=1),
           reads=["identf"], writes=["identf"])
        op("dve", lambda e: e.tensor_copy(out=identb[:], in_=identf[:]), reads=["identf"], writes=["identb"])
        op("dve", lambda e: e.memset(onesb[:], 1.0), writes=["onesb"])
        op("pool", lambda e: e.memset(cmask[:], 1.0), writes=["cmask"])
        op("pool", lambda e: e.affine_select(out=cmask[:], in_=cmask[:], pattern=[[1, 128]], compare_op=ALU.is_ge,
                                             fill=0.0, base=0, channel_multiplier=-1),
           reads=["cmask"], writes=["cmask"])
        for i_ in range(3):
            op("pool", lambda e, i_=i_: e.memset(cmask[i_ * 32:(i_ + 1) * 32, (i_ + 1) * 32:128], 0.0),
               reads=["cmask"], writes=["cmask"])
        op("pool", lambda e: e.memset(gmask[:], 1.0), writes=["gmask"])
        op("pool", lambda e: e.memset(gmask[64:128, 0:64], 0.0), reads=["gmask"], writes=["gmask"])
        op("dve", lambda e: e.memset(rmask[:], 1.0), writes=["rmask"])
        op("dve", lambda e: e.memset(rmask[:].rearrange("p (c t) -> p c t", t=64)[:, :, 0:1], 0.0),
           reads=["rmask"], writes=["rmask"])
        for l in range(DEPTH):
            op("sp", lambda e, l=l: e.dma_start(out=wspf[:], in_=wspT[l]), writes=["z2"], dma="wsp")
            op("dve", lambda e, l=l: e.tensor_tensor(
                out=WmT[:, l, :, :], in0=wspf[:].rearrange("p (h t) -> p h t", h=4),
                in1=gmask[:].unsqueeze(1).to_broadcast([128, 4, 128]), op=ALU.mult),
               reads=["z2", "gmask"], writes=[("WmT", l)])
        op("dve", lambda e: e.memset(lbt[:, 0, :], 0.0), writes=["lb0"])
        op("dve", lambda e: e.memset(omlt[:, 0, :], 1.0), writes=["oml0"])
        op("dve", lambda e: e.memset(nomlt[:, 0, :], -1.0), writes=["noml0"])
        op("dve", lambda e: e.tensor_tensor(out=lbd[:], in0=vec[:, C_LB + 4:C_LB + 8], in1=vec[:, C_LB:C_LB + 4],
                                            op=ALU.subtract), reads=["vec"], writes=["lbd"])
        op("act", lambda e: e.activation(out=lbt[:, 1, :], in_=lbd[:], func=AF.Sigmoid), reads=["lbd"], writes=["lb1"])
        op("act", lambda e: e.activation(out=omlt[:, 1, :], in_=lbd[:], func=AF.Sigmoid, scale=-1.0),
           reads=["lbd"], writes=["oml1"])
        op("dve", lambda e: e.tensor_scalar(out=nomlt[:, 1, :], in0=omlt[:, 1, :], scalar1=-1.0, scalar2=None,
                                            op0=ALU.mult), reads=["oml1"], writes=["noml1"])
        op("act", lambda e: e.activation(out=cact[:], in_=vec[:, C_C:C_C + 16], func=AF.Silu),
           reads=["vec"], writes=["cact"])
        op("dve", lambda e: e.tensor_copy(out=cactb[:], in_=cact[:]), reads=["cact"], writes=["cactb"])

    def dump(name, ap, keys):
        if not dbg or dry:
            return
        shp = [int(x) for x in ap.shape]
        dt_ = nc.dram_tensor("dbg_" + name, shp, ap.dtype, kind="ExternalOutput").ap()
        dbg_names.append("dbg_" + name)
        op("sp", lambda e: e.dma_start(out=dt_, in_=ap), reads=keys, writes=[("dbg", name)], dma="dbg_" + name)
        op("sp", None, reads=[("dbg", name)])

    def adaln(l):
        def mm_blk(cb, sl, sk):
            slv = sl[:, 0:4096].rearrange("p (k n) -> p k n", k=8)
            r = cb % 2

            def mm(e):
                for k in range(8):
                    ins = e.matmul(ps[r][0:2, :], lhsT=cactb[:, 2 * k:2 * k + 2], rhs=slv[:, k, :],
                                   start=(k == 0), stop=(k == 7))
                return ins
            op("pe", mm, reads=[sk, "cactb"], writes=[P(r)])
            op("act", lambda e: e.copy(out=sact[r][0:2, :], in_=ps[r][0:2, :]), reads=[P(r)], writes=[("sact", r)])

        def tr_blk(cb):
            r = cb % 2

            def tr(e):
                for q in range(4):
                    col = (cb * 4 + q) * 2
                    ins = e.transpose(out=ps[2][:, col:col + 2], in_=sact[r][0:2, q * 128:(q + 1) * 128],
                                      identity=identf[0:2, 0:2])
                return ins
            op("pe", tr, reads=[("sact", r), "identf"], writes=[P(2)])

        for cb in range(18):
            sl, sk = next_w(("ada", l, cb))
            if not dry:
                mm_blk(cb, sl, sk)
                if cb > 0:
                    tr_blk(cb - 1)
            if cb % 2 == 1:
                yield None
        if not dry:
            tr_blk(17)
        if dry:
            return
        pm = ps[2]
        op("dve", lambda e: e.tensor_tensor(
            out=modT[:, l, :, :], in0=pm[:, 0:144].rearrange("p (c b) -> p c b", b=2),
            in1=vec[:, C_BADA + l * 72:C_BADA + (l + 1) * 72].unsqueeze(2).to_broadcast([128, 72, 2]),
            op=ALU.add), reads=[P(2), "vec"], writes=[("modT", l)])
        for sub in range(3):
            for b in range(BPC):
                ng = vec[:, C_NG + (l * 3 + sub) * 8:C_NG + (l * 3 + sub) * 8 + 8]
                op("dve", lambda e, sub=sub, b=b, ng=ng: e.scalar_tensor_tensor(
                    out=modA[:, l, sub, b, :], in0=modT[:, l, (3 * sub + 1) * 8:(3 * sub + 2) * 8, b], scalar=1.0,
                    in1=ng, op0=ALU.add, op1=ALU.mult), reads=[("modT", l), "vec"], writes=[("modA", l, sub, b)])
                gs = 1.0 if sub == 1 else 0.5
                op("dve", lambda e, sub=sub, b=b, gs=gs: e.tensor_scalar(
                    out=modG[:, l, sub, b, :], in0=modT[:, l, (3 * sub + 2) * 8:(3 * sub + 3) * 8, b], scalar1=gs,
                    scalar2=None, op0=ALU.mult), reads=[("modT", l)], writes=[("modG", l, sub, b)])

    def rsqrt_psum(dst_t, dkey, pb, scale):
        op("act", lambda e: e.activation(out=dst_t[:], in_=ps[pb][:], func=AF.Ln, scale=scale, bias=epsr[:]),
           reads=[P(pb), "epsr"], writes=[dkey])
        op("act", lambda e: e.activation(out=dst_t[:], in_=dst_t[:], func=AF.Exp, scale=-0.5),
           reads=[dkey], writes=[dkey])

    def rms_squares(h, hkeys):
        for dc in range(8):
            op("act", lambda e, dc=dc: e.activation(out=sq[:, dc, :], in_=h[:, dc, :], func=AF.Square),
               reads=[hkeys[dc]], writes=[("sq", dc)])

    def rms_rstd(h, hkeys, pb):
        def mm(e):
            for dc in range(8):
                ins = e.matmul(ps[pb][:], lhsT=onesb[:], rhs=sq[:, dc, :], start=(dc == 0), stop=(dc == 7))
            return ins
        op("pe", mm, reads=[("sq", dc) for dc in range(8)] + ["onesb"], writes=[P(pb)])
        rsqrt_psum(rstd, "rstd", pb, 1.0 / D)

    def norm_mod(h, hkeys, y, ykeys, l, sub, b, pb):
        for dc in range(8):
            tf = tmpf[dc % 2]
            op("dve", lambda e, dc=dc, tf=tf: e.scalar_tensor_tensor(
                out=tf[:], in0=h[:, dc, :], scalar=modA[:, l, sub, b, dc:dc + 1], in1=rstd[:],
                op0=ALU.mult, op1=ALU.mult),
               reads=[hkeys[dc], "rstd", ("modA", l, sub, b)], writes=[("tmpf", dc % 2)])
            op("act", lambda e, dc=dc, tf=tf: e.activation(
                out=y[:, dc, :], in_=tf[:], func=AF.Identity,
                bias=modT[:, l, 3 * sub * 8 + dc, b:b + 1], scale=1.0),
               reads=[("tmpf", dc % 2), ("modT", l)], writes=[ykeys[dc]])

    ykf = [("yf", dc) for dc in range(8)]
    ykm = [("ym", dc) for dc in range(8)]
    pctr = {"ab": 0, "o": 0, "mo": 0}

    def ffn(h, hkeys, l, f, b):
        sub = 0 if f == 0 else 2
        yield ("acq", ("N", "Fy"), None)
        if not dry:
            rms_squares(h, hkeys)
        yield None
        if not dry:
            rms_rstd(h, hkeys, 0)
        yield None
        if not dry:
            norm_mod(h, hkeys, yTf, ykf, l, sub, b, 0)
        yield ("rel", ("N",))
        yield ("acq", ("Fg",), None)
        for blk in range(11):
            sl, sk = next_w(("A", l, f * 11 + blk))
            if not dry:
                slv = sl[:, 0:4096].rearrange("p (k n) -> p k n", k=8)
                for ch in range(2):
                    r = pctr["ab"] % 2
                    pctr["ab"] += 1
                    pa, pb = r, 2 + r
                    hc = blk * 2 + ch

                    def mm(e, slv=slv, ch=ch, pa=pa, pb=pb):
                        for k in range(8):
                            e.matmul(ps[pa][:], lhsT=slv[:, k, ch * 128:(ch + 1) * 128], rhs=yTf[:, k, :],
                                     start=(k == 0), stop=(k == 7))
                        for k in range(8):
                            ins = e.matmul(ps[pb][:], lhsT=slv[:, k, 256 + ch * 128:256 + (ch + 1) * 128],
                                           rhs=yTf[:, k, :], start=(k == 0), stop=(k == 7))
                        return ins
                    op("pe", mm, reads=[sk] + ykf, writes=[P(pa), P(pb)])
                    op("act", lambda e, pa=pa, r=r: e.activation(out=sact[r][:], in_=ps[pa][:], func=AF.Silu),
                       reads=[P(pa)], writes=[("sact", r)])
                    op("dve", lambda e, pb=pb, r=r, hc=hc: e.tensor_tensor(out=gT[:, hc, :], in0=sact[r][:],
                                                                           in1=ps[pb][:], op=ALU.mult),
                       reads=[("sact", r), P(pb)], writes=[("g", hc)])
            yield None
        yield ("rel", ("Fy",))
        gkeys = [("g", i) for i in range(22)]
        for dc in range(8):
            sl, sk = next_w(("B", l, f * 8 + dc))
            if not dry:
                slv = sl[:, 0:2816].rearrange("p (k n) -> p k n", k=22)
                po = 2 + pctr["o"] % 2
                pctr["o"] += 1

                def mm(e, slv=slv, po=po):
                    for k in range(22):
                        ins = e.matmul(ps[po][:], lhsT=slv[:, k, :], rhs=gT[:, k, :], start=(k == 0), stop=(k == 21))
                    return ins
                op("pe", mm, reads=[sk] + gkeys, writes=[P(po)])
                op("dve", lambda e, dc=dc, po=po: e.scalar_tensor_tensor(
                    out=h[:, dc, :], in0=ps[po][:], scalar=modG[:, l, sub, b, dc:dc + 1], in1=h[:, dc, :],
                    op0=ALU.mult, op1=ALU.add),
                   reads=[P(po), ("modG", l, sub, b), hkeys[dc]], writes=[hkeys[dc]])
            if dc % 2 == 1:
                yield None
        yield ("rel", ("Fg",))

    def c3(ap):
        return ap.rearrange("p (c t) -> p c t", t=64)

    def c32(ap):
        return ap.rearrange("p (c t) -> p c t", t=32)

    def h4(ap):
        return ap.rearrange("p (h t) -> p h t", h=4)

    mdone = [0] * DEPTH

    def mixer(h, hkeys, l, b, j, ti):
        yield ("acq", ("N", "M"), (lambda: mdone[l] == ti))
        if not dry:
            rms_squares(h, hkeys)
        yield None
        if not dry:
            rms_rstd(h, hkeys, 4)
        yield None
        if not dry:
            norm_mod(h, hkeys, yTm, ykm, l, 1, b, 4)
        yield ("rel", ("N",))
        if j == 0 and not dry:
            op("dve", lambda e: e.memset(Sst[:, l, :, :], 0.0), writes=[("S", l)])

        def proj_fm(slv, hd, pb):
            def mm(e):
                for k in range(8):
                    ins = e.matmul(ps[pb][:], lhsT=slv[:, k, hd * 128:(hd + 1) * 128], rhs=yTm[:, k, :],
                                   start=(k == 0), stop=(k == 7))
                return ins
            return mm

        def proj_tm(slv, s, pb):
            def mm(e):
                for k in range(8):
                    ins = e.matmul(ps[pb][:], lhsT=yTm[:, k, s * 128:(s + 1) * 128], rhs=slv[:, k, :],
                                   start=(k == 0), stop=(k == 7))
                return ins
            return mm

        A_ = lambda sl: sl[:, 0:4096].rearrange("p (k n) -> p k n", k=8)
        sigs = None if dry else [(sig, "sig0"), (sigx[:, 0, :], "sig1"), (sigx[:, 1, :], "sig2"), (sigx[:, 2, :], "sig3")]
        sl, sk = next_w(("A", l, 23))
        if not dry:
            slv = A_(sl)
            for hd in range(4):
                op("pe", proj_fm(slv, hd, 4 + hd), reads=[sk] + ykm, writes=[P(4 + hd)])
            for hd in range(4):
                sb_, skey = sigs[hd]
                op("act", lambda e, hd=hd, sb_=sb_: e.activation(out=sb_, in_=ps[4 + hd][:], func=AF.Sigmoid),
                   reads=[P(4 + hd)], writes=[skey])
        yield None
        sl, sk = next_w(("A", l, 25))
        if not dry:
            slv = A_(sl)
            for hd in range(4):
                op("pe", proj_fm(slv, hd, 4 + hd), reads=[sk] + ykm, writes=[P(4 + hd)])
            for hd in range(4):
                tb, tk = (br, "br") if hd % 2 == 0 else (Em, "Em")
                op("act", lambda e, hd=hd, tb=tb: e.activation(out=tb[:], in_=ps[4 + hd][:], func=AF.Silu),
                   reads=[P(4 + hd)], writes=[tk])
                op("dve", lambda e, hd=hd, tb=tb: e.tensor_scalar(
                    out=sgT[:, hd, :], in0=tb[:], scalar1=vec[:, C_GN + l * 4 + hd:C_GN + l * 4 + hd + 1],
                    scalar2=None, op0=ALU.mult), reads=[tk, "vec"], writes=[("sg", hd)])
        yield None
        sl, sk = next_w(("A", l, 26))
        if not dry:
            slv = A_(sl)
            for hd in range(4):
                op("pe", proj_fm(slv, hd, 4 + hd), reads=[sk] + ykm, writes=[P(4 + hd)])
            for hd in range(4):
                op("act", lambda e, hd=hd: e.activation(out=ugT[:, hd, :], in_=ps[4 + hd][:],
                                                        func=AF.Gelu_apprx_tanh),
                   reads=[P(4 + hd)], writes=[("ug", hd)])
        yield None
        sl, sk = next_w(("A", l, 24))
        if not dry:
            slv = A_(sl)
            for s in range(4):
                op("pe", proj_tm(slv, s, 4 + s), reads=[sk] + ykm, writes=[P(4 + s)])
            for s in range(4):
                op("act", lambda e, s=s: e.copy(out=vi[:, s, :], in_=ps[4 + s][:]), reads=[P(4 + s)],
                   writes=[("vi", s)])
        yield None
        sl, sk = next_w(("A", l, 27))
        vgk4 = ["vg0", "vg1", "vg2", "vg3"]
        if not dry:
            slv = A_(sl)
            for s in range(4):
                op("pe", proj_tm(slv, s, 4 + s), reads=[sk] + ykm, writes=[P(4 + s)])
            for s in range(4):
                op("act", lambda e, s=s: e.activation(out=vg4[:, s, :], in_=ps[4 + s][:], func=AF.Gelu_apprx_tanh),
                   reads=[P(4 + s)], writes=[vgk4[s]])
            for s in range(4):
                op("dve", lambda e, s=s: e.tensor_reduce(out=st1[:, s, :], in_=h4(vg4[:, s, :]), axis=AX.X, op=ALU.add),
                   reads=[vgk4[s]], writes=["st1"])
                op("dve", lambda e, s=s: e.tensor_tensor(out=z2[:], in0=vg4[:, s, :], in1=vg4[:, s, :], op=ALU.mult),
                   reads=[vgk4[s]], writes=["z2"])
                op("dve", lambda e, s=s: e.tensor_reduce(out=st2[:, s, :], in_=h4(z2[:]), axis=AX.X, op=ALU.add),
                   reads=["z2"], writes=["st2"])
            op("dve", lambda e: e.tensor_scalar(out=mean[:], in0=st1[:], scalar1=1.0 / 128, scalar2=None,
                                                op0=ALU.mult), reads=["st1"], writes=["mean"])
            op("dve", lambda e: e.tensor_tensor(out=msq[:], in0=mean[:], in1=mean[:], op=ALU.mult),
               reads=["mean"], writes=["msq"])
            op("dve", lambda e: e.scalar_tensor_tensor(out=var[:], in0=st2[:], scalar=1.0 / 128, in1=msq[:],
                                                       op0=ALU.mult, op1=ALU.subtract),
               reads=["st2", "msq"], writes=["var"])
            op("act", lambda e: e.activation(out=lrs[:], in_=var[:], func=AF.Ln, scale=1.0, bias=epsl[:]),
               reads=["var", "epsl"], writes=["lrs"])
            op("act", lambda e: e.activation(out=lrs[:], in_=lrs[:], func=AF.Exp, scale=-0.5),
               reads=["lrs"], writes=["lrs"])
        yield None
        if not dry:
            for s in range(4):
                for hd in range(4):
                    op("dve", lambda e, s=s, hd=hd: e.tensor_scalar(
                        out=vhat[:, s, hd * 128:(hd + 1) * 128], in0=vg4[:, s, hd * 128:(hd + 1) * 128],
                        scalar1=mean[:, s, hd:hd + 1], scalar2=lrs[:, s, hd:hd + 1], op0=ALU.subtract, op1=ALU.mult),
                       reads=[vgk4[s], "mean", "lrs"], writes=[("vhat", s, hd)])
        yield None
        for hd in range(4):
            if not dry:
                sb_, skey = sigs[hd]
                op("act", lambda e, hd=hd, sb_=sb_: e.activation(out=logf[:], in_=sb_, func=AF.Ln,
                                                                  scale=omlt[:, l, hd:hd + 1], bias=lbt[:, l, hd:hd + 1]),
                   reads=[skey, f"oml{l}", f"lb{l}"], writes=["logf"])
                op("dve", lambda e, hd=hd, sb_=sb_: e.tensor_scalar(out=kk[:], in0=sb_, scalar1=nomlt[:, l, hd:hd + 1],
                                                                    scalar2=omlt[:, l, hd:hd + 1], op0=ALU.mult, op1=ALU.add),
                   reads=[skey, f"oml{l}", f"noml{l}"], writes=["kk"])
                op("dve", lambda e: e.tensor_tensor_scan(out=bcs[:], data0=rmask[:], data1=logf[:], initial=0.0,
                                                         op0=ALU.mult, op1=ALU.add),
                   reads=["logf", "rmask"], writes=["bcs"])
                op("dve", lambda e: e.tensor_tensor(out=c3(br[:]), in0=c3(bcs[:]),
                                                    in1=c3(bcs[:])[:, :, 31:32].to_broadcast([128, 8, 64]),
                                                    op=ALU.subtract), reads=["bcs"], writes=["br"])
                op("act", lambda e, hd=hd: e.activation(out=Ep[:, hd, :], in_=br[:], func=AF.Exp),
                   reads=["br"], writes=[("Ep", hd), "vg0", "vg1", "vg2", "vg3"])
                op("act", lambda e: e.activation(out=Em[:], in_=br[:], func=AF.Exp, scale=-1.0),
                   reads=["br"], writes=["Em"])
                op("act", lambda e, hd=hd: e.activation(out=dl[:, hd, :], in_=c3(br[:])[:, :, 63], func=AF.Exp),
                   reads=["br"], writes=[("dl", hd)])
                op("act", lambda e, hd=hd: e.activation(out=er[:, hd, :], in_=c3(bcs[:])[:, :, 31], func=AF.Exp),
                   reads=["bcs"], writes=[("er", hd)])
                op("act", lambda e, hd=hd: e.activation(out=wc[:, hd, :], in_=c3(bcs[:])[:, :, 63], func=AF.Exp),
                   reads=["bcs"], writes=[("wc", hd)])
                op("dve", lambda e, hd=hd: e.tensor_tensor(out=ktT[:, hd, :], in0=kk[:], in1=Em[:], op=ALU.mult),
                   reads=["kk", "Em"], writes=[("kt", hd)])
                op("dve", lambda e: e.tensor_tensor(out=c32(br[:]), in0=c32(bcs[:]),
                                                    in1=c32(bcs[:])[:, :, 15:16].to_broadcast([128, 16, 32]),
                                                    op=ALU.subtract), reads=["bcs"], writes=["br"])
                op("act", lambda e, hd=hd: e.activation(out=Ep2[:, hd, :], in_=br[:], func=AF.Exp),
                   reads=["br"], writes=[("Ep2", hd), "vg0", "vg1", "vg2", "vg3"])
                op("act", lambda e: e.activation(out=Em[:], in_=br[:], func=AF.Exp, scale=-1.0),
                   reads=["br"], writes=["Em"])
                op("dve", lambda e, hd=hd: e.tensor_tensor(out=kdT[:, hd, :], in0=kk[:], in1=Em[:], op=ALU.mult),
                   reads=["kk", "Em"], writes=[("kd", hd)])
            yield None
        sl, sk = next_w(("A", l, 22))
        if not dry:
            slv = A_(sl)
            for hd in range(4):
                op("pe", proj_fm(slv, hd, 4 + hd), reads=[sk] + ykm, writes=[P(4 + hd)])
            for hd in range(4):
                pb = 4 + hd
                op("dve", lambda e, hd=hd, pb=pb: e.tensor_tensor(out=qtT[:, hd, :], in0=ps[pb][:], in1=Ep[:, hd, :],
                                                                  op=ALU.mult),
                   reads=[P(pb), ("Ep", hd)], writes=[("qt", hd)])
                op("dve", lambda e, hd=hd, pb=pb: e.tensor_tensor(out=qdT[:, hd, :], in0=ps[pb][:], in1=Ep2[:, hd, :],
                                                                  op=ALU.mult),
                   reads=[P(pb), ("Ep2", hd)], writes=[("qd", hd)])
        yield None
        if not dry:
            op("dve", lambda e: e.memset(dummy[:], 0.0),
               reads=[("Ep", hd) for hd in range(4)] + [("Ep2", hd) for hd in range(4)],
               writes=["osb", "o1", "zz", "z1", "dummy"])

        def subtile_steps(s):
            ts = slice(s * 128, (s + 1) * 128)
            kts = [("kt", hd) for hd in range(4)]
            qts = [("qt", hd) for hd in range(4)]
            ptr = ps[4][:].bitcast(BF16)[:, 0:512].rearrange("p (h k) -> p h k", h=4)
            pat = h4(ps[5][:])
            pat2 = h4(ps[6][:])
            pdb = (4, 7)
            po = h4(ps[6][:])
            pmm = h4(ps[7][:])

            def st_a():
                def tr(e):
                    for hd in range(4):
                        ins = e.transpose(out=ptr[:, hd, :], in_=ktT[:, hd, ts], identity=identb[:])
                    return ins
                op("pe", tr, reads=kts + ["identb"], writes=[P(4)])

                def at(e):
                    for hd in range(4):
                        ins = e.matmul(pat[:, hd, :], lhsT=kdT[:, hd, ts], rhs=qdT[:, hd, ts], start=True, stop=True)
                    return ins
                op("pe", at, reads=[("kd", hd) for hd in range(4)] + [("qd", hd) for hd in range(4)], writes=[P(5)])

                def at2(e):
                    for hd in range(4):
                        for cc in range(2):
                            r0 = cc * 64
                            t0 = s * 128 + cc * 64
                            ins = e.matmul(pat2[r0:r0 + 32, hd, r0 + 32:r0 + 64], lhsT=ktT[:, hd, t0:t0 + 32],
                                           rhs=qtT[:, hd, t0 + 32:t0 + 64], start=True, stop=True)
                    return ins
                op("pe", at2, reads=kts + qts, writes=[P(6)])

                def gm(e):
                    for hd in range(4):
                        ins = e.matmul(pmm[:, hd, :], lhsT=vhat[:, s, hd * 128:(hd + 1) * 128], rhs=WmT[:, l, hd, :],
                                       start=True, stop=True)
                    return ins
                op("pe", gm, reads=[("vhat", s, hd) for hd in range(4)] + [("WmT", l)], writes=[P(7)])
                op("act", lambda e: e.copy(out=ktok[:], in_=ptr), reads=[P(4)], writes=["ktok"])
                op("dve", lambda e: e.tensor_tensor(out=ATm[:], in0=pat,
                                                    in1=cmask[:].unsqueeze(1).to_broadcast([128, 4, 128]),
                                                    op=ALU.mult), reads=[P(5), "cmask"], writes=["ATm"])
                for cc in range(2):
                    r0 = cc * 64
                    op("act", lambda e, r0=r0: e.copy(out=ATm[r0:r0 + 32, :, r0 + 32:r0 + 64],
                                                      in_=pat2[r0:r0 + 32, :, r0 + 32:r0 + 64]),
                       reads=[P(6)], writes=["ATm"])
                for hd in range(4):
                    op("dve", lambda e, hd=hd: e.scalar_tensor_tensor(
                        out=z1[:, hd * 128:(hd + 1) * 128], in0=pmm[:, hd, :],
                        scalar=vec[:, C_LNG + l * 4 + hd:C_LNG + l * 4 + hd + 1],
                        in1=bspb[:, l, hd * 128:(hd + 1) * 128], op0=ALU.mult, op1=ALU.add),
                       reads=[P(7), "vec", ("bspb", l)], writes=["z1"])
                op("dve", lambda e: e.tensor_tensor(out=h4(zz[:]), in0=h4(z1[:]), in1=ugT[:, :, ts], op=ALU.mult),
                   reads=["z1"] + [("ug", hd) for hd in range(4)], writes=["zz"])
                op("act", lambda e: e.activation(out=zsq[:], in_=zz[:], func=AF.Square), reads=["zz"], writes=["zsq"])

            def st_b():
                for cc in range(2):
                    pds = h4(ps[pdb[cc]][:])
                    rows = slice(cc * 64, (cc + 1) * 64)

                    def dsm(e, pds=pds, rows=rows):
                        for hd in range(4):
                            ins = e.matmul(pds[:, hd, :], lhsT=ktok[rows, hd, :], rhs=vi[rows, s, hd * 128:(hd + 1) * 128],
                                           start=True, stop=True)
                        return ins
                    op("pe", dsm, reads=["ktok", ("vi", s)], writes=[P(pdb[cc])])
                op("pe", lambda e: e.matmul(ps[5][:], lhsT=onesb[:], rhs=zsq[:], start=True, stop=True),
                   reads=["zsq", "onesb"], writes=[P(5)])
                for cc in range(2):
                    c = 2 * s + cc
                    pds = h4(ps[pdb[cc]][:])

                    def bc(t, c=c):
                        return t[:, :, c:c + 1].to_broadcast([128, 4, 128])
                    op("dve", lambda e, cc=cc, bc=bc: e.tensor_tensor(out=Shat[:, cc, :, :], in0=Sst[:, l, :, :],
                                                                      in1=bc(er), op=ALU.mult),
                       reads=[("S", l)] + [("er", hd) for hd in range(4)], writes=[("Shat", cc)])
                    op("dve", lambda e, pds=pds, bc=bc: e.tensor_tensor(out=dst[:], in0=pds, in1=bc(dl), op=ALU.mult),
                       reads=[P(pdb[cc])] + [("dl", hd) for hd in range(4)], writes=["dst"])
                    op("dve", lambda e, bc=bc: e.tensor_tensor(out=Sst[:, l, :, :], in0=Sst[:, l, :, :], in1=bc(wc),
                                                               op=ALU.mult),
                       reads=[("S", l)] + [("wc", hd) for hd in range(4)], writes=[("S", l)])
                    op("dve", lambda e: e.tensor_tensor(out=Sst[:, l, :, :], in0=Sst[:, l, :, :], in1=dst[:],
                                                        op=ALU.add), reads=[("S", l), "dst"], writes=[("S", l)])
                rsqrt_psum(z2, "z2", 5, 1.0 / 128)
                op("dve", lambda e: e.tensor_tensor(out=z2[:], in0=z2[:], in1=zz[:], op=ALU.mult),
                   reads=["z2", "zz"], writes=["z2"])
                for hd in range(4):
                    op("dve", lambda e, hd=hd: e.tensor_scalar(
                        out=mixT[:, 4 + hd, ts], in0=z2[:, hd * 128:(hd + 1) * 128],
                        scalar1=vec[:, C_OG + l * 4 + hd:C_OG + l * 4 + hd + 1], scalar2=None, op0=ALU.mult),
                       reads=["z2", "vec"], writes=[("mixg", s, hd)])

            def st_c():
                def om(e):
                    for hd in range(4):
                        e.matmul(po[:, hd, :], lhsT=vi[:, s, hd * 128:(hd + 1) * 128], rhs=ATm[:, hd, :],
                                 start=True, stop=False)
                        for cc in range(2):
                            tc = slice(s * 128 + cc * 64, s * 128 + (cc + 1) * 64)
                            ins = e.matmul(po[:, hd, cc * 64:(cc + 1) * 64], lhsT=Shat[:, cc, hd, :],
                                           rhs=qtT[:, hd, tc], start=False, stop=(cc == 1))
                    return ins
                op("pe", om, reads=[("vi", s), "ATm", ("Shat", 0), ("Shat", 1)] + qts, writes=[P(6)])
                op("act", lambda e: e.copy(out=osb[:], in_=ps[6][:]), reads=[P(6)], writes=["osb"])
                op("act", lambda e: e.activation(out=osq[:], in_=ps[6][:], func=AF.Square), reads=[P(6)],
                   writes=["osq"])

            def st_d():
                op("pe", lambda e: e.matmul(ps[5][:], lhsT=onesb[:], rhs=osq[:], start=True, stop=True),
                   reads=["osq", "onesb"], writes=[P(5)])
                rsqrt_psum(o1, "o1", 5, 1.0 / 128)
                op("dve", lambda e: e.tensor_tensor(out=o1[:], in0=o1[:], in1=osb[:], op=ALU.mult),
                   reads=["o1", "osb"], writes=["o1"])
                op("dve", lambda e: e.tensor_tensor(out=mixT[:, 0:4, ts], in0=h4(o1[:]), in1=sgT[:, :, ts],
                                                    op=ALU.mult),
                   reads=["o1"] + [("sg", hd) for hd in range(4)], writes=[("mix", s)])
            return [st_a, st_b, st_c, st_d]

        for s in range(4):
            steps = [None] * 4 if dry else subtile_steps(s)
            for st in steps:
                if st is not None:
                    st()
                yield None
        mkeys = [("mix", s) for s in range(4)] + [("mixg", s, hd) for s in range(4) for hd in range(4)]
        for wb in range(2):
            sl, sk = next_w(("A", l, 28 + wb))
            if not dry:
                slv = sl[:, 0:4096].rearrange("p (k n) -> p k n", k=8)
                for dcl in range(4):
                    dc = wb * 4 + dcl
                    po = 4 + pctr["mo"] % 4
                    pctr["mo"] += 1

                    def mm(e, slv=slv, dcl=dcl, po=po):
                        for k in range(8):
                            ins = e.matmul(ps[po][:], lhsT=slv[:, k, dcl * 128:(dcl + 1) * 128], rhs=mixT[:, k, :],
                                           start=(k == 0), stop=(k == 7))
                        return ins
                    op("pe", mm, reads=[sk] + mkeys, writes=[P(po)])
                    op("dve", lambda e, dc=dc, po=po: e.scalar_tensor_tensor(
                        out=h[:, dc, :], in0=ps[po][:], scalar=modG[:, l, 1, b, dc:dc + 1], in1=h[:, dc, :],
                        op0=ALU.mult, op1=ALU.add),
                       reads=[P(po), ("modG", l, 1, b), hkeys[dc]], writes=[hkeys[dc]])
            yield None
        mdone[l] = ti + 1
        yield ("rel", ("M",))

    tile_list = [(b, j) for b in range(BPC) for j in range(nt)]
    out_keys = []

    def load_tile(ti, hb):
        if dry:
            return
        b, j = tile_list[ti]
        src = xT[b].rearrange("(dc p) t -> p dc t", p=128)[:, :, j * T:(j + 1) * T]
        op("pool", lambda e, hb=hb, src=src: e.dma_start(out=hT[hb][:], in_=src),
           writes=[("h", hb, dc) for dc in range(8)], dma=f"x{hb}")

    def stream(si):
        hb = si
        h = None if dry else hT[hb]
        hkeys = [("h", hb, dc) for dc in range(8)]
        if si == 1:
            yield ("sync",)
        for ti in range(si, len(tile_list), 2):
            b, j = tile_list[ti]
            load_tile(ti, hb)
            for l in range(DEPTH):
                dd = (ti == 0 and l == 0)
                yield from ffn(h, hkeys, l, 0, b)
                if dd:
                    dump("h_ffn1", None if dry else h[:], hkeys)
                yield ("sync",)
                yield from mixer(h, hkeys, l, b, j, ti)
                yield ("sync",)
                if dd:
                    dump("h_mix", None if dry else h[:], hkeys)
                if ti == 0 and l == 0:
                    for r in ffn(h, hkeys, l, 1, b):
                        if isinstance(r, tuple) and r[0] == "rel" and r[1] == ("Fg",):
                            yield from adaln(1)
                        yield r
                else:
                    yield from ffn(h, hkeys, l, 1, b)
                if dd:
                    dump("h_ffn2", None if dry else h[:], hkeys)
            yield ("acq", ("N",), None)
            if not dry:
                rms_squares(h, hkeys)
            yield None
            if not dry:
                rms_rstd(h, hkeys, 0)
                for dc in range(8):
                    og = tmpf[dc % 2]
                    op("dve", lambda e, dc=dc, og=og: e.scalar_tensor_tensor(
                        out=og[:], in0=h[:, dc, :], scalar=vec[:, C_FG + dc:C_FG + dc + 1], in1=rstd[:],
                        op0=ALU.mult, op1=ALU.mult), reads=[hkeys[dc], "vec", "rstd"], writes=[("tmpf", dc % 2)])
                    ok = ("out", ti, dc)
                    op("pool", lambda e, dc=dc, og=og, b=b, j=j: e.dma_start(
                        out=outT[b, dc * 128:(dc + 1) * 128, j * T:(j + 1) * T], in_=og[:]),
                       reads=[("tmpf", dc % 2)], writes=[ok], dma=f"o{dc % 2}")
                    out_keys.append(ok)
            yield ("rel", ("N",))

    for _ in adaln(0):
        pass

    gens = [stream(0), stream(1)]
    active = [True, len(tile_list) > 1]
    pending = [None, None]
    at_sync = [False, False]
    held = {}
    seg_idx = [0, 0]
    done = [0, 0]
    seg_counts = [[0], [0]]

    def frac(si):
        if counts is None or seg_idx[si] >= len(counts[si]):
            return done[si]
        return done[si] / max(counts[si][seg_idx[si]], 1)

    while any(active):
        cands = [si for si in (0, 1) if active[si] and not at_sync[si]]
        if not cands:
            for si in (0, 1):
                if at_sync[si]:
                    at_sync[si] = False
                    seg_idx[si] += 1
                    done[si] = 0
                    seg_counts[si].append(0)
            continue
        cands.sort(key=lambda si: (frac(si), si))
        advanced = False
        for si in cands:
            if pending[si] is not None:
                locks, cond = pending[si]
                if all(k not in held for k in locks) and (cond is None or cond()):
                    for k in locks:
                        held[k] = si
                    pending[si] = None
                else:
                    continue
            advanced = True
            try:
                r = next(gens[si])
            except StopIteration:
                active[si] = False
                break
            done[si] += 1
            seg_counts[si][-1] += 1
            if isinstance(r, tuple):
                if r[0] == "acq":
                    pending[si] = (r[1], r[2])
                elif r[0] == "rel":
                    for k in r[1]:
                        assert held.get(k) == si, (k, held, si)
                        del held[k]
                elif r[0] == "sync":
                    at_sync[si] = True
            break
        assert advanced, ("scheduler deadlock", pending, held, at_sync)
    if dry:
        return wseq, seg_counts
    assert wstate["cur"] == len(wseq), (wstate, len(wseq))
    op("sp", None, reads=out_keys[-16:])
    return wseq, dbg_names


def build_nc(nt=SEQ // T, dbg=False):
    nc = bass.Bass("TRN2", target_bir_lowering=False)
    xT = nc.dram_tensor("xT", [BPC, D, SEQ], F32, kind="ExternalInput").ap()
    wA = nc.dram_tensor("wA", [DEPTH, 30, 128, 4096], F32, kind="ExternalInput").ap()
    wB = nc.dram_tensor("wB", [DEPTH, 16, 128, 2816], F32, kind="ExternalInput").ap()
    wada = nc.dram_tensor("wada", [DEPTH, 18, 128, 4096], F32, kind="ExternalInput").ap()
    vecsT = nc.dram_tensor("vecsT", [128, NV], F32, kind="ExternalInput").ap()
    wspT = nc.dram_tensor("wspT", [DEPTH, 128, 512], F32, kind="ExternalInput").ap()
    bsp = nc.dram_tensor("bsp", [DEPTH, 512], F32, kind="ExternalInput").ap()
    outT = nc.dram_tensor("outT", [BPC, D, SEQ], F32, kind="ExternalOutput").ap()
    wsA = nc.dram_tensor("wsA", [DEPTH, 30, 128, 4096], BF16, kind="Internal").ap()
    wsB = nc.dram_tensor("wsB", [DEPTH, 16, 128, 2816], BF16, kind="Internal").ap()
    D_ = (xT, wA, wB, wada, vecsT, wspT, bsp, outT, wsA, wsB)
    _, counts = _program(nc, None, Sched(nc, None, dry=True), nt, None, dbg, D_)
    wseq, _ = _program(nc, None, Sched(nc, None, dry=True), nt, None, dbg, D_, counts=counts)
    print("segment step counts", [c[:6] for c in counts])
    with ExitStack() as es:
        S = Sched(nc, es)
        _, dbg_names = _program(nc, es, S, nt, wseq, dbg, D_, counts=counts)
        print("sbuf bytes remaining", nc.sbuf_bytes_remaining, "ops", {k: len(v) for k, v in S.ops.items()})
        S.emit()
    nc._dbg_names = dbg_names
    return nc


def _blkA(W):
    return W.reshape(8, 128, 512).transpose(1, 0, 2).reshape(128, 4096)


def _blkB(W):
    return W.reshape(22, 128, 128).transpose(1, 0, 2).reshape(128, 2816)


def _prep_shared(w_ada, b_ada, norm_gain, ffn1_w13, ffn1_w2, w_in, hg_lb_logits, hg_gnorm, gm_ln_gain,
                 gm_w_spatial, gm_b_spatial, gm_out_gain, w_out, ffn2_w13, ffn2_w2, final_gain):
    f = lambda a: np.asarray(a, dtype=np.float32)
    wA = np.empty((DEPTH, 30, 128, 4096), np.float32)
    wB = np.empty((DEPTH, 16, 128, 2816), np.float32)
    wada = np.empty((DEPTH, 18, 128, 4096), np.float32)
    for l in range(DEPTH):
        for fi, (w13, w2) in enumerate(((f(ffn1_w13)[l], f(ffn1_w2)[l]), (f(ffn2_w13)[l], f(ffn2_w2)[l]))):
            for i in range(11):
                blk = np.concatenate([w13[:, 256 * i:256 * (i + 1)], w13[:, DFF + 256 * i:DFF + 256 * (i + 1)]], axis=1)
                wA[l, fi * 11 + i] = _blkA(blk)
            for i in range(8):
                wB[l, fi * 8 + i] = _blkB(w2[:, 128 * i:128 * (i + 1)])
        for i in range(6):
            wA[l, 22 + i] = _blkA(f(w_in)[l][:, 512 * i:512 * (i + 1)])
        for i in range(2):
            wA[l, 28 + i] = _blkA(f(w_out)[l][:, 512 * i:512 * (i + 1)])
        for i in range(18):
            wada[l, i] = _blkA(f(w_ada)[l][:, 512 * i:512 * (i + 1)])
    vecs = np.zeros((128, NV), np.float32)
    for l in range(DEPTH):
        vecs[:, C_BADA + l * 72:C_BADA + (l + 1) * 72] = f(b_ada)[l].reshape(72, 128).T
        for sub in range(3):
            c0 = C_NG + (l * 3 + sub) * 8
            vecs[:, c0:c0 + 8] = f(norm_gain)[l, sub].reshape(8, 128).T
        vecs[:, C_LB + l * 4:C_LB + l * 4 + 4] = f(hg_lb_logits)[l].reshape(4, 128).T
        vecs[:, C_GN + l * 4:C_GN + l * 4 + 4] = f(hg_gnorm)[l].reshape(4, 128).T
        vecs[:, C_LNG + l * 4:C_LNG + l * 4 + 4] = f(gm_ln_gain)[l].reshape(4, 128).T
        vecs[:, C_OG + l * 4:C_OG + l * 4 + 4] = f(gm_out_gain)[l].reshape(4, 128).T
    vecs[:, C_FG:C_FG + 8] = f(final_gain).reshape(8, 128).T
    wspT = np.ascontiguousarray(f(gm_w_spatial).transpose(0, 3, 1, 2).reshape(DEPTH, 128, 512))
    bspv = np.ascontiguousarray(f(gm_b_spatial).reshape(DEPTH, 512))
    return wA, wB, wada, vecs, wspT, bspv


def kernel(x, c, w_ada, b_ada, norm_gain, ffn1_w13, ffn1_w2, w_in, hg_lb_logits, hg_gnorm, gm_ln_gain,
           gm_w_spatial, gm_b_spatial, gm_out_gain, w_out, ffn2_w13, ffn2_w2, final_gain, _nt=SEQ // T,
           _cores=NCORES, _dbg=None):
    x = np.asarray(x, dtype=np.float32)
    c = np.asarray(c, dtype=np.float32)
    wA, wB, wada, vecs, wspT, bspv = _prep_shared(
        w_ada, b_ada, norm_gain, ffn1_w13, ffn1_w2, w_in, hg_lb_logits, hg_gnorm, gm_ln_gain, gm_w_spatial,
        gm_b_spatial, gm_out_gain, w_out, ffn2_w13, ffn2_w2, final_gain)
    in_maps = []
    for core in range(_cores):
        xs = x[core * BPC:(core + 1) * BPC]
        xTc = np.ascontiguousarray(xs.transpose(0, 2, 1))
        v = vecs.copy()
        cs = c[core * BPC:(core + 1) * BPC]
        v[:, C_C:C_C + 16] = cs.reshape(BPC, 8, 128).transpose(2, 1, 0).reshape(128, 16)
        in_maps.append({"xT": xTc, "wA": wA, "wB": wB, "wada": wada, "vecsT": v, "wspT": wspT, "bsp": bspv})
    nc = build_nc(_nt, dbg=_dbg is not None)
    res = run_bass_kernel_spmd(nc, in_maps, core_ids=list(range(_cores)))
    out = np.empty((NCORES * BPC, SEQ, D), np.float32)
    if _dbg is not None:
        for k in nc._dbg_names:
            _dbg[k] = np.asarray(res.results[0][k])
    for core in range(_cores):
        out[core * BPC:(core + 1) * BPC] = res.results[core]["outT"].transpose(0, 2, 1)
    return out
```

```python
from contextlib import ExitStack
import numpy as np
import concourse.bass as bass
import concourse.mybir as mybir
from concourse.bass_utils import run_bass_kernel_spmd

F32 = mybir.dt.float32
BF16 = mybir.dt.bfloat16
AF = mybir.ActivationFunctionType
ALU = mybir.AluOpType
AX = mybir.AxisListType

D = 1024
SEQ = 2048
DEPTH = 2
DFF = 2816
T = 512
NCORES = 8
BPC = 2
RMS_EPS = 1e-6
LN_EPS = 1e-5
NSLOT = 4

C_BADA = 0
C_NG = 144
C_LB = 192
C_GN = 200
C_LNG = 208
C_OG = 216
C_FG = 224
C_C = 232
NV = 256


class Sched:
    ENGS = ("pe", "act", "dve", "pool", "sp")

    def __init__(self, nc, es, dry=False):
        self.nc = nc
        self.es = es
        self.dry = dry
        self.ops = {e: [] for e in self.ENGS}
        self.sem = {}
        self.cnt = {}
        self.seen = {e: {} for e in self.ENGS}
        self.bufs = {}
        for e in self.ENGS:
            self._mk("E_" + e)

    def _mk(self, name):
        if name not in self.cnt:
            if not self.dry:
                self.sem[name] = self.es.enter_context(self.nc.semaphore(name))
            self.cnt[name] = 0
        return name

    def op(self, eng, fn, reads=(), writes=(), dma=None):
        if self.dry:
            return None
        deps = {}
        for k in reads:
            b = self.bufs.get(k)
            if b and b["w"]:
                s, v = b["w"]
                deps[s] = max(deps.get(s, 0), v)
        for k in writes:
            b = self.bufs.get(k)
            if b:
                if b["w"]:
                    s, v = b["w"]
                    deps[s] = max(deps.get(s, 0), v)
                for s, v in b["r"]:
                    deps[s] = max(deps.get(s, 0), v)
        own = "E_" + eng
        waits = []
        for s, v in deps.items():
            if s == own and eng == "pe":
                continue
            if self.seen[eng].get(s, 0) >= v:
                continue
            self.seen[eng][s] = v
            waits.append((s, v))
        if dma is None:
            sname, inc = own, 1
        else:
            sname, inc = self._mk("D_" + dma), 16
        self.cnt[sname] += inc
        tok = (sname, self.cnt[sname])
        self.ops[eng].append((waits, fn, sname, inc))
        for k in reads:
            b = self.bufs.setdefault(k, {"w": None, "r": []})
            b["r"].append(tok)
        for k in writes:
            b = self.bufs.setdefault(k, {"w": None, "r": []})
            b["w"] = tok
            b["r"] = []
        return tok

    def emit(self):
        S = self
        with self.nc.Block() as block:
            def run(name):
                def body(e):
                    for waits, fn, sname, inc in S.ops[name]:
                        for s, v in waits:
                            e.wait_ge(S.sem[s], v)
                        if fn is None:
                            e.nop().then_inc(S.sem[sname], inc)
                            continue
                        fn(e).then_inc(S.sem[sname], inc)
                return body
            block.tensor(run("pe"))
            block.scalar(run("act"))
            block.vector(run("dve"))
            block.gpsimd(run("pool"))
            block.sync(run("sp"))


def _program(nc, es, S, nt, wseq_known, dbg, D_, counts=None):
    xT, wA, wB, wada, vecsT, wspT, bsp, outT, wsA, wsB = D_
    op = S.op
    dry = S.dry

    def sb(name, shape, dt=F32):
        return es.enter_context(nc.sbuf_tensor(name, shape, dt))

    if not dry:
        hT = [sb(f"hT{i}", [128, 8, T]) for i in range(2)]
        yTf = sb("yTf", [128, 8, T], BF16)
        yTm = sb("yTm", [128, 8, T], BF16)
        gT = sb("gT", [128, 22, T], BF16)
        ssq = [sb(f"ssq{i}", [128, T]) for i in range(2)]
        ssqb = [sb(f"ssqb{i}", [128, T], BF16) for i in range(2)]
        tmpsq = sb("tmpsq", [128, T])
        slots = [sb(f"slot{i}", [128, 4096], BF16) for i in range(NSLOT)]
        tmpf = [sb(f"tmpf{i}", [128, T]) for i in range(2)]
        rstd = sb("rstd", [128, T])
        sact = [sb(f"sact{i}", [128, T]) for i in range(2)]
        vec = sb("vec", [128, NV])
        cact = sb("cact", [128, 16])
        cactb = sb("cactb", [128, 16], BF16)
        modT = sb("modT", [128, DEPTH, 72, 2])
        modA = sb("modA", [128, DEPTH, 3, 2, 8])
        modG = sb("modG", [128, DEPTH, 3, 2, 8])
        lbt = sb("lbt", [128, DEPTH, 4])
        omlt = sb("omlt", [128, DEPTH, 4])
        nomlt = sb("nomlt", [128, DEPTH, 4])
        lbd = sb("lbd", [128, 4])
        identb = sb("identb", [128, 128], BF16)
        identf = sb("identf", [128, 128])
        onesb = sb("onesb", [128, 128], BF16)
        cmask = sb("cmask", [128, 128])
        gmask = sb("gmask", [128, 128])
        rmask = sb("rmask", [128, T])
        WmT = sb("WmT", [128, DEPTH, 4, 128], BF16)
        bspb = sb("bspb", [128, DEPTH, 512])
        epsr = sb("epsr", [128, 1])
        epsl = sb("epsl", [128, 1])
        dummy = sb("dummyt", [128, 1])
        epraw = sb("epraw", [128, 4096], BF16)
        Ep = epraw[:, 0:2048].rearrange("p (h t) -> p h t", h=4)
        Ep2 = epraw[:, 2048:4096].rearrange("p (h t) -> p h t", h=4)
        epf = epraw[:].bitcast(F32)
        osb, o1, zz, z1 = epf[:, 0:512], epf[:, 512:1024], epf[:, 1024:1536], epf[:, 1536:2048]
        z2 = sb("z2", [128, 512])
        wspf = z2
        vg4 = epf[:, 0:2048].rearrange("p (a t) -> p a t", a=4)
        ktT = sb("ktT", [128, 4, T], BF16)
        kdT = sb("kdT", [128, 4, T], BF16)
        qtT = sb("qtT", [128, 4, T], BF16)
        qdT = sb("qdT", [128, 4, T], BF16)
        sgT = sb("sgT", [128, 4, T], BF16)
        ugT = sb("ugT", [128, 4, T], BF16)
        vi = sb("vi", [128, 4, 512], BF16)
        vhat = sb("vhat", [128, 4, 512], BF16)
        mixraw = sb("mixraw", [128, 8 * T], BF16)
        mixT = mixraw[:].rearrange("p (k t) -> p k t", k=8)
        mixf = mixraw[:].bitcast(F32)
        sig, logf, kk, bcs = mixf[:, 0:512], mixf[:, 512:1024], mixf[:, 1024:1536], mixf[:, 1536:2048]
        br = sb("br", [128, T])
        sigx = sb("sigx", [128, 3, T])
        Em = sb("Em", [128, T])
        er = sb("er", [128, 4, 8])
        wc = sb("wc", [128, 4, 8])
        dl = sb("dl", [128, 4, 8])
        Sst = sb("Sst", [128, DEPTH, 4, 128])
        Shat = sb("Shat", [128, 2, 4, 128], BF16)
        dst = sb("dst", [128, 4, 128])
        ktok = sb("ktok", [128, 4, 128], BF16)
        ATm = sb("ATm", [128, 4, 128], BF16)
        osq = sb("osq", [128, 512], BF16)
        zsq = sb("zsq", [128, 512], BF16)
        st1 = sb("st1", [128, 4, 4])
        st2 = sb("st2", [128, 4, 4])
        mean = sb("mean", [128, 4, 4])
        msq = sb("msq", [128, 4, 4])
        var = sb("var", [128, 4, 4])
        lrs = sb("lrs", [128, 4, 4])
        ps = [es.enter_context(nc.psum_tensor(f"ps{i}", [128, 512], F32)) for i in range(8)]

    def P(i):
        return ("ps", i)

    dbg_names = []

    wseq = [] if wseq_known is None else wseq_known
    wstate = {"issued": 0, "cur": 0, "seen": set()}

    def slot_key(i):
        return ("slot", i % NSLOT)

    def prefetch(upto):
        while wstate["issued"] <= min(upto, len(wseq) - 1):
            i = wstate["issued"]
            kind, l, blk = wseq[i]
            sl = slots[i % NSLOT]
            n = 2816 if kind == "B" else 4096
            src = {"A": wA, "B": wB, "ada": wada}[kind]
            scr = {"A": wsA, "B": wsB}.get(kind)
            bkey = (kind, l, blk)
            if scr is None or bkey not in wstate["seen"]:
                op("pool", lambda e, sl=sl, l=l, blk=blk, n=n, src=src: e.dma_start(out=sl[:, 0:n], in_=src[l, blk]),
                   writes=[slot_key(i)], dma=f"w{i % NSLOT}")
                if scr is not None:
                    wstate["seen"].add(bkey)
                    op("sp", lambda e, sl=sl, l=l, blk=blk, n=n, scr=scr: e.dma_start(out=scr[l, blk], in_=sl[:, 0:n]),
                       reads=[slot_key(i)], writes=[("scr",) + bkey], dma=f"sw{i % NSLOT}")
            else:
                op("sp", lambda e, sl=sl, l=l, blk=blk, n=n, scr=scr: e.dma_start(out=sl[:, 0:n], in_=scr[l, blk]),
                   reads=[("scr",) + bkey], writes=[slot_key(i)], dma=f"w{i % NSLOT}")
            wstate["issued"] += 1

    def next_w(spec):
        i = wstate["cur"]
        wstate["cur"] += 1
        if dry:
            wseq.append(spec)
            return None, None
        assert wseq[i] == spec, (wseq[i], spec)
        prefetch(i + NSLOT - 1)
        return slots[i % NSLOT], slot_key(i)

    if not dry:
        op("sp", lambda e: e.dma_start(out=vec[:], in_=vecsT), writes=["vec"], dma="vec")
        for l in range(DEPTH):
            op("sp", lambda e, l=l: e.dma_start(out=bspb[:, l, :], in_=bsp[l].partition_broadcast(128)),
               writes=[("bspb", l)], dma=f"bspb{l}")
        op("dve", lambda e: e.memset(epsr[:], RMS_EPS), writes=["epsr"])
        op("dve", lambda e: e.memset(epsl[:], LN_EPS), writes=["epsl"])
        op("pool", lambda e: e.memset(identf[:], 0.0), writes=["identf"])
        op("pool", lambda e: e.affine_select(out=identf[:], in_=identf[:], pattern=[[-1, 128]],
                                             compare_op=ALU.not_equal, fill=1.0, base=0, channel_multiplier=1),
           reads=["identf"], writes=["identf"])
        op("dve", lambda e: e.tensor_copy(out=identb[:], in_=identf[:]), reads=["identf"], writes=["identb"])
        op("dve", lambda e: e.memset(onesb[:], 1.0), writes=["onesb"])
        op("pool", lambda e: e.memset(cmask[:], 1.0), writes=["cmask"])
        op("pool", lambda e: e.affine_select(out=cmask[:], in_=cmask[:], pattern=[[1, 128]], compare_op=ALU.is_ge,
                                             fill=0.0, base=0, channel_multiplier=-1),
           reads=["cmask"], writes=["cmask"])
        for i_ in range(3):
            op("pool", lambda e, i_=i_: e.memset(cmask[i_ * 32:(i_ + 1) * 32, (i_ + 1) * 32:128], 0.0),
               reads=["cmask"], writes=["cmask"])
        op("pool", lambda e: e.memset(gmask[:], 1.0), writes=["gmask"])
        op("pool", lambda e: e.memset(gmask[64:128, 0:64], 0.0), reads=["gmask"], writes=["gmask"])
        op("dve", lambda e: e.memset(rmask[:], 1.0), writes=["rmask"])
        op("dve", lambda e: e.memset(rmask[:].rearrange("p (c t) -> p c t", t=64)[:, :, 0:1], 0.0),
           reads=["rmask"], writes=["rmask"])
        for l in range(DEPTH):
            op("sp", lambda e, l=l: e.dma_start(out=wspf[:], in_=wspT[l]), writes=["z2"], dma="wsp")
            op("dve", lambda e, l=l: e.tensor_tensor(
                out=WmT[:, l, :, :], in0=wspf[:].rearrange("p (h t) -> p h t", h=4),
                in1=gmask[:].unsqueeze(1).to_broadcast([128, 4, 128]), op=ALU.mult),
               reads=["z2", "gmask"], writes=[("WmT", l)])
        op("dve", lambda e: e.memset(lbt[:, 0, :], 0.0), writes=["lb0"])
        op("dve", lambda e: e.memset(omlt[:, 0, :], 1.0), writes=["oml0"])
        op("dve", lambda e: e.memset(nomlt[:, 0, :], -1.0), writes=["noml0"])
        op("dve", lambda e: e.tensor_tensor(out=lbd[:], in0=vec[:, C_LB + 4:C_LB + 8], in1=vec[:, C_LB:C_LB + 4],
                                            op=ALU.subtract), reads=["vec"], writes=["lbd"])
        op("act", lambda e: e.activation(out=lbt[:, 1, :], in_=lbd[:], func=AF.Sigmoid), reads=["lbd"], writes=["lb1"])
        op("act", lambda e: e.activation(out=omlt[:, 1, :], in_=lbd[:], func=AF.Sigmoid, scale=-1.0),
           reads=["lbd"], writes=["oml1"])
        op("dve", lambda e: e.tensor_scalar(out=nomlt[:, 1, :], in0=omlt[:, 1, :], scalar1=-1.0, scalar2=None,
                                            op0=ALU.mult), reads=["oml1"], writes=["noml1"])
        op("act", lambda e: e.activation(out=cact[:], in_=vec[:, C_C:C_C + 16], func=AF.Silu),
           reads=["vec"], writes=["cact"])
        op("dve", lambda e: e.tensor_copy(out=cactb[:], in_=cact[:]), reads=["cact"], writes=["cactb"])

    def dump(name, ap, keys):
        if not dbg or dry:
            return
        shp = [int(x) for x in ap.shape]
        dt_ = nc.dram_tensor("dbg_" + name, shp, ap.dtype, kind="ExternalOutput").ap()
        dbg_names.append("dbg_" + name)
        op("sp", lambda e: e.dma_start(out=dt_, in_=ap), reads=keys, writes=[("dbg", name)], dma="dbg_" + name)
        op("sp", None, reads=[("dbg", name)])

    def adaln(l):
        def mm_blk(cb, sl, sk):
            slv = sl[:, 0:4096].rearrange("p (k n) -> p k n", k=8)
            r = cb % 2

            def mm(e):
                for k in range(8):
                    ins = e.matmul(ps[r][0:2, :], lhsT=cactb[:, 2 * k:2 * k + 2], rhs=slv[:, k, :],
                                   start=(k == 0), stop=(k == 7))
                return ins
            op("pe", mm, reads=[sk, "cactb"], writes=[P(r)])
            op("act", lambda e: e.copy(out=sact[r][0:2, :], in_=ps[r][0:2, :]), reads=[P(r)], writes=[("sact", r)])

        def tr_blk(cb):
            r = cb % 2

            def tr(e):
                for q in range(4):
                    col = (cb * 4 + q) * 2
                    ins = e.transpose(out=ps[2][:, col:col + 2], in_=sact[r][0:2, q * 128:(q + 1) * 128],
                                      identity=identf[0:2, 0:2])
                return ins
            op("pe", tr, reads=[("sact", r), "identf"], writes=[P(2)])

        for cb in range(18):
            sl, sk = next_w(("ada", l, cb))
            if not dry:
                mm_blk(cb, sl, sk)
                if cb > 0:
                    tr_blk(cb - 1)
            if cb % 2 == 1:
                yield None
        if not dry:
            tr_blk(17)
        if dry:
            return
        pm = ps[2]
        op("dve", lambda e: e.tensor_tensor(
            out=modT[:, l, :, :], in0=pm[:, 0:144].rearrange("p (c b) -> p c b", b=2),
            in1=vec[:, C_BADA + l * 72:C_BADA + (l + 1) * 72].unsqueeze(2).to_broadcast([128, 72, 2]),
            op=ALU.add), reads=[P(2), "vec"], writes=[("modT", l)])
        for sub in range(3):
            for b in range(BPC):
                ng = vec[:, C_NG + (l * 3 + sub) * 8:C_NG + (l * 3 + sub) * 8 + 8]
                op("dve", lambda e, sub=sub, b=b, ng=ng: e.scalar_tensor_tensor(
                    out=modA[:, l, sub, b, :], in0=modT[:, l, (3 * sub + 1) * 8:(3 * sub + 2) * 8, b], scalar=1.0,
                    in1=ng, op0=ALU.add, op1=ALU.mult), reads=[("modT", l), "vec"], writes=[("modA", l, sub, b)])
                gs = 1.0 if sub == 1 else 0.5
                op("dve", lambda e, sub=sub, b=b, gs=gs: e.tensor_scalar(
                    out=modG[:, l, sub, b, :], in0=modT[:, l, (3 * sub + 2) * 8:(3 * sub + 3) * 8, b], scalar1=gs,
                    scalar2=None, op0=ALU.mult), reads=[("modT", l)], writes=[("modG", l, sub, b)])

    def rsqrt_psum(dst_t, dkey, pb, scale):
        op("act", lambda e: e.activation(out=dst_t[:], in_=ps[pb][:], func=AF.Ln, scale=scale, bias=epsr[:]),
           reads=[P(pb), "epsr"], writes=[dkey])
        op("act", lambda e: e.activation(out=dst_t[:], in_=dst_t[:], func=AF.Exp, scale=-0.5),
           reads=[dkey], writes=[dkey])

    def sq_accum(h, hkeys, si, dc):
        if dc == 0:
            op("dve", lambda e: e.tensor_tensor(out=ssq[si][:], in0=h[:, 0, :], in1=h[:, 0, :], op=ALU.mult),
               reads=[hkeys[0]], writes=[("ssq", si)])
            return
        op("dve", lambda e: e.tensor_tensor(out=tmpsq[:], in0=h[:, dc, :], in1=h[:, dc, :], op=ALU.mult),
           reads=[hkeys[dc]], writes=["tmpsq"])
        if dc < 7:
            op("dve", lambda e: e.tensor_tensor(out=ssq[si][:], in0=ssq[si][:], in1=tmpsq[:], op=ALU.add),
               reads=[("ssq", si), "tmpsq"], writes=[("ssq", si)])
        else:
            op("dve", lambda e: e.tensor_tensor(out=ssqb[si][:], in0=ssq[si][:], in1=tmpsq[:], op=ALU.add),
               reads=[("ssq", si), "tmpsq"], writes=[("ssqb", si)])

    def rms_squares(h, hkeys, si):
        for dc in range(8):
            sq_accum(h, hkeys, si, dc)

    def rms_rstd(si, pb):
        op("pe", lambda e: e.matmul(ps[pb][:], lhsT=onesb[:], rhs=ssqb[si][:], start=True, stop=True),
           reads=[("ssqb", si), "onesb"], writes=[P(pb)])
        rsqrt_psum(rstd, "rstd", pb, 1.0 / D)

    def norm_mod(h, hkeys, y, ykeys, l, sub, b, pb):
        for dc in range(8):
            tf = tmpf[dc % 2]
            op("dve", lambda e, dc=dc, tf=tf: e.scalar_tensor_tensor(
                out=tf[:], in0=h[:, dc, :], scalar=modA[:, l, sub, b, dc:dc + 1], in1=rstd[:],
                op0=ALU.mult, op1=ALU.mult),
               reads=[hkeys[dc], "rstd", ("modA", l, sub, b)], writes=[("tmpf", dc % 2)])
            op("act", lambda e, dc=dc, tf=tf: e.activation(
                out=y[:, dc, :], in_=tf[:], func=AF.Identity,
                bias=modT[:, l, 3 * sub * 8 + dc, b:b + 1], scale=1.0),
               reads=[("tmpf", dc % 2), ("modT", l)], writes=[ykeys[dc]])

    ykf = [("yf", dc) for dc in range(8)]
    ykm = [("ym", dc) for dc in range(8)]
    pctr = {"ab": 0, "o": 0, "mo": 0}

    def ffn(h, hkeys, l, f, b, si, need_sq=False):
        sub = 0 if f == 0 else 2
        yield ("acq", ("N", "Fy"), None)
        if not dry and need_sq:
            rms_squares(h, hkeys, si)
        yield None
        if not dry:
            rms_rstd(si, 0)
        yield None
        if not dry:
            norm_mod(h, hkeys, yTf, ykf, l, sub, b, 0)
        yield ("rel", ("N",))
        yield ("acq", ("Fg",), None)
        for blk in range(11):
            sl, sk = next_w(("A", l, f * 11 + blk))
            if not dry:
                slv = sl[:, 0:4096].rearrange("p (k n) -> p k n", k=8)
                for ch in range(2):
                    r = pctr["ab"] % 2
                    pctr["ab"] += 1
                    pa, pb = r, 2 + r
                    hc = blk * 2 + ch

                    def mm(e, slv=slv, ch=ch, pa=pa, pb=pb):
                        for k in range(8):
                            e.matmul(ps[pa][:], lhsT=slv[:, k, ch * 128:(ch + 1) * 128], rhs=yTf[:, k, :],
                                     start=(k == 0), stop=(k == 7))
                        for k in range(8):
                            ins = e.matmul(ps[pb][:], lhsT=slv[:, k, 256 + ch * 128:256 + (ch + 1) * 128],
                                           rhs=yTf[:, k, :], start=(k == 0), stop=(k == 7))
                        return ins
                    op("pe", mm, reads=[sk] + ykf, writes=[P(pa), P(pb)])
                    op("act", lambda e, pa=pa, r=r: e.activation(out=sact[r][:], in_=ps[pa][:], func=AF.Silu),
                       reads=[P(pa)], writes=[("sact", r)])
                    op("dve", lambda e, pb=pb, r=r, hc=hc: e.tensor_tensor(out=gT[:, hc, :], in0=sact[r][:],
                                                                           in1=ps[pb][:], op=ALU.mult),
                       reads=[("sact", r), P(pb)], writes=[("g", hc)])
            yield None
        yield ("rel", ("Fy",))
        gkeys = [("g", i) for i in range(22)]
        for dc in range(8):
            sl, sk = next_w(("B", l, f * 8 + dc))
            if not dry:
                slv = sl[:, 0:2816].rearrange("p (k n) -> p k n", k=22)
                po = 2 + pctr["o"] % 2
                pctr["o"] += 1

                def mm(e, slv=slv, po=po):
                    for k in range(22):
                        ins = e.matmul(ps[po][:], lhsT=slv[:, k, :], rhs=gT[:, k, :], start=(k == 0), stop=(k == 21))
                    return ins
                op("pe", mm, reads=[sk] + gkeys, writes=[P(po)])
                op("dve", lambda e, dc=dc, po=po: e.scalar_tensor_tensor(
                    out=h[:, dc, :], in0=ps[po][:], scalar=modG[:, l, sub, b, dc:dc + 1], in1=h[:, dc, :],
                    op0=ALU.mult, op1=ALU.add),
                   reads=[P(po), ("modG", l, sub, b), hkeys[dc]], writes=[hkeys[dc]])
                sq_accum(h, hkeys, si, dc)
            if dc % 2 == 1:
                yield None
        yield ("rel", ("Fg",))

    def c3(ap):
        return ap.rearrange("p (c t) -> p c t", t=64)

    def c32(ap):
        return ap.rearrange("p (c t) -> p c t", t=32)

    def h4(ap):
        return ap.rearrange("p (h t) -> p h t", h=4)

    mdone = [0] * DEPTH

    def mixer(h, hkeys, l, b, j, ti, si):
        yield ("acq", ("N", "M"), (lambda: mdone[l] == ti))
        yield None
        if not dry:
            rms_rstd(si, 4)
        yield None
        if not dry:
            norm_mod(h, hkeys, yTm, ykm, l, 1, b, 4)
        yield ("rel", ("N",))
        if j == 0 and not dry:
            op("dve", lambda e: e.memset(Sst[:, l, :, :], 0.0), writes=[("S", l)])

        def proj_fm(slv, hd, pb):
            def mm(e):
                for k in range(8):
                    ins = e.matmul(ps[pb][:], lhsT=slv[:, k, hd * 128:(hd + 1) * 128], rhs=yTm[:, k, :],
                                   start=(k == 0), stop=(k == 7))
                return ins
            return mm

        def proj_tm(slv, s, pb):
            def mm(e):
                for k in range(8):
                    ins = e.matmul(ps[pb][:], lhsT=yTm[:, k, s * 128:(s + 1) * 128], rhs=slv[:, k, :],
                                   start=(k == 0), stop=(k == 7))
                return ins
            return mm

        A_ = lambda sl: sl[:, 0:4096].rearrange("p (k n) -> p k n", k=8)
        sigs = None if dry else [(sig, "sig0"), (sigx[:, 0, :], "sig1"), (sigx[:, 1, :], "sig2"), (sigx[:, 2, :], "sig3")]
        sl, sk = next_w(("A", l, 23))
        if not dry:
            slv = A_(sl)
            for hd in range(4):
                op("pe", proj_fm(slv, hd, 4 + hd), reads=[sk] + ykm, writes=[P(4 + hd)])
            for hd in range(4):
                sb_, skey = sigs[hd]
                op("act", lambda e, hd=hd, sb_=sb_: e.activation(out=sb_, in_=ps[4 + hd][:], func=AF.Sigmoid),
                   reads=[P(4 + hd)], writes=[skey])
        yield None
        sl, sk = next_w(("A", l, 25))
        if not dry:
            slv = A_(sl)
            for hd in range(4):
                op("pe", proj_fm(slv, hd, 4 + hd), reads=[sk] + ykm, writes=[P(4 + hd)])
            for hd in range(4):
                tb, tk = (br, "br") if hd % 2 == 0 else (Em, "Em")
                op("act", lambda e, hd=hd, tb=tb: e.activation(out=tb[:], in_=ps[4 + hd][:], func=AF.Silu),
                   reads=[P(4 + hd)], writes=[tk])
                op("dve", lambda e, hd=hd, tb=tb: e.tensor_scalar(
                    out=sgT[:, hd, :], in0=tb[:], scalar1=vec[:, C_GN + l * 4 + hd:C_GN + l * 4 + hd + 1],
                    scalar2=None, op0=ALU.mult), reads=[tk, "vec"], writes=[("sg", hd)])
        yield None
        sl, sk = next_w(("A", l, 26))
        if not dry:
            slv = A_(sl)
            for hd in range(4):
                op("pe", proj_fm(slv, hd, 4 + hd), reads=[sk] + ykm, writes=[P(4 + hd)])
            for hd in range(4):
                op("act", lambda e, hd=hd: e.activation(out=ugT[:, hd, :], in_=ps[4 + hd][:],
                                                        func=AF.Gelu_apprx_tanh),
                   reads=[P(4 + hd)], writes=[("ug", hd)])
        yield None
        sl, sk = next_w(("A", l, 24))
        if not dry:
            slv = A_(sl)
            for s in range(4):
                op("pe", proj_tm(slv, s, 4 + s), reads=[sk] + ykm, writes=[P(4 + s)])
            for s in range(4):
                op("act", lambda e, s=s: e.copy(out=vi[:, s, :], in_=ps[4 + s][:]), reads=[P(4 + s)],
                   writes=[("vi", s)])
        yield None
        sl, sk = next_w(("A", l, 27))
        vgk4 = ["vg0", "vg1", "vg2", "vg3"]
        if not dry:
            slv = A_(sl)
            for s in range(4):
                op("pe", proj_tm(slv, s, 4 + s), reads=[sk] + ykm, writes=[P(4 + s)])
            for s in range(4):
                op("act", lambda e, s=s: e.activation(out=vg4[:, s, :], in_=ps[4 + s][:], func=AF.Gelu_apprx_tanh),
                   reads=[P(4 + s)], writes=[vgk4[s]])
            for s in range(4):
                op("dve", lambda e, s=s: e.tensor_reduce(out=st1[:, s, :], in_=h4(vg4[:, s, :]), axis=AX.X, op=ALU.add),
                   reads=[vgk4[s]], writes=["st1"])
                op("dve", lambda e, s=s: e.tensor_tensor(out=z2[:], in0=vg4[:, s, :], in1=vg4[:, s, :], op=ALU.mult),
                   reads=[vgk4[s]], writes=["z2"])
                op("dve", lambda e, s=s: e.tensor_reduce(out=st2[:, s, :], in_=h4(z2[:]), axis=AX.X, op=ALU.add),
                   reads=["z2"], writes=["st2"])
            op("dve", lambda e: e.tensor_scalar(out=mean[:], in0=st1[:], scalar1=1.0 / 128, scalar2=None,
                                                op0=ALU.mult), reads=["st1"], writes=["mean"])
            op("dve", lambda e: e.tensor_tensor(out=msq[:], in0=mean[:], in1=mean[:], op=ALU.mult),
               reads=["mean"], writes=["msq"])
            op("dve", lambda e: e.scalar_tensor_tensor(out=var[:], in0=st2[:], scalar=1.0 / 128, in1=msq[:],
                                                       op0=ALU.mult, op1=ALU.subtract),
               reads=["st2", "msq"], writes=["var"])
            op("act", lambda e: e.activation(out=lrs[:], in_=var[:], func=AF.Ln, scale=1.0, bias=epsl[:]),
               reads=["var", "epsl"], writes=["lrs"])
            op("act", lambda e: e.activation(out=lrs[:], in_=lrs[:], func=AF.Exp, scale=-0.5),
               reads=["lrs"], writes=["lrs"])
        yield None
        if not dry:
            for s in range(4):
                for hd in range(4):
                    op("dve", lambda e, s=s, hd=hd: e.tensor_scalar(
                        out=vhat[:, s, hd * 128:(hd + 1) * 128], in0=vg4[:, s, hd * 128:(hd + 1) * 128],
                        scalar1=mean[:, s, hd:hd + 1], scalar2=lrs[:, s, hd:hd + 1], op0=ALU.subtract, op1=ALU.mult),
                       reads=[vgk4[s], "mean", "lrs"], writes=[("vhat", s, hd)])
        yield None
        for hd in range(4):
            if not dry:
                sb_, skey = sigs[hd]
                op("act", lambda e, hd=hd, sb_=sb_: e.activation(out=logf[:], in_=sb_, func=AF.Ln,
                                                                  scale=omlt[:, l, hd:hd + 1], bias=lbt[:, l, hd:hd + 1]),
                   reads=[skey, f"oml{l}", f"lb{l}"], writes=["logf"])
                op("dve", lambda e, hd=hd, sb_=sb_: e.tensor_scalar(out=kk[:], in0=sb_, scalar1=nomlt[:, l, hd:hd + 1],
                                                                    scalar2=omlt[:, l, hd:hd + 1], op0=ALU.mult, op1=ALU.add),
                   reads=[skey, f"oml{l}", f"noml{l}"], writes=["kk"])
                op("dve", lambda e: e.tensor_tensor_scan(out=bcs[:], data0=rmask[:], data1=logf[:], initial=0.0,
                                                         op0=ALU.mult, op1=ALU.add),
                   reads=["logf", "rmask"], writes=["bcs"])
                op("dve", lambda e: e.tensor_tensor(out=c3(br[:]), in0=c3(bcs[:]),
                                                    in1=c3(bcs[:])[:, :, 31:32].to_broadcast([128, 8, 64]),
                                                    op=ALU.subtract), reads=["bcs"], writes=["br"])
                op("act", lambda e, hd=hd: e.activation(out=Ep[:, hd, :], in_=br[:], func=AF.Exp),
                   reads=["br"], writes=[("Ep", hd), "vg0", "vg1", "vg2", "vg3"])
                op("act", lambda e: e.activation(out=Em[:], in_=br[:], func=AF.Exp, scale=-1.0),
                   reads=["br"], writes=["Em"])
                op("act", lambda e, hd=hd: e.activation(out=dl[:, hd, :], in_=c3(br[:])[:, :, 63], func=AF.Exp),
                   reads=["br"], writes=[("dl", hd)])
                op("act", lambda e, hd=hd: e.activation(out=er[:, hd, :], in_=c3(bcs[:])[:, :, 31], func=AF.Exp),
                   reads=["bcs"], writes=[("er", hd)])
                op("act", lambda e, hd=hd: e.activation(out=wc[:, hd, :], in_=c3(bcs[:])[:, :, 63], func=AF.Exp),
                   reads=["bcs"], writes=[("wc", hd)])
                op("dve", lambda e, hd=hd: e.tensor_tensor(out=ktT[:, hd, :], in0=kk[:], in1=Em[:], op=ALU.mult),
                   reads=["kk", "Em"], writes=[("kt", hd)])
                op("dve", lambda e: e.tensor_tensor(out=c32(br[:]), in0=c32(bcs[:]),
                                                    in1=c32(bcs[:])[:, :, 15:16].to_broadcast([128, 16, 32]),
                                                    op=ALU.subtract), reads=["bcs"], writes=["br"])
                op("act", lambda e, hd=hd: e.activation(out=Ep2[:, hd, :], in_=br[:], func=AF.Exp),
                   reads=["br"], writes=[("Ep2", hd), "vg0", "vg1", "vg2", "vg3"])
                op("act", lambda e: e.activation(out=Em[:], in_=br[:], func=AF.Exp, scale=-1.0),
                   reads=["br"], writes=["Em"])
                op("dve", lambda e, hd=hd: e.tensor_tensor(out=kdT[:, hd, :], in0=kk[:], in1=Em[:], op=ALU.mult),
                   reads=["kk", "Em"], writes=[("kd", hd)])
            yield None
        sl, sk = next_w(("A", l, 22))
        if not dry:
            slv = A_(sl)
            for hd in range(4):
                op("pe", proj_fm(slv, hd, 4 + hd), reads=[sk] + ykm, writes=[P(4 + hd)])
            for hd in range(4):
                pb = 4 + hd
                op("dve", lambda e, hd=hd, pb=pb: e.tensor_tensor(out=qtT[:, hd, :], in0=ps[pb][:], in1=Ep[:, hd, :],
                                                                  op=ALU.mult),
                   reads=[P(pb), ("Ep", hd)], writes=[("qt", hd)])
                op("dve", lambda e, hd=hd, pb=pb: e.tensor_tensor(out=qdT[:, hd, :], in0=ps[pb][:], in1=Ep2[:, hd, :],
                                                                  op=ALU.mult),
                   reads=[P(pb), ("Ep2", hd)], writes=[("qd", hd)])
        yield None
        if not dry:
            op("dve", lambda e: e.memset(dummy[:], 0.0),
               reads=[("Ep", hd) for hd in range(4)] + [("Ep2", hd) for hd in range(4)],
               writes=["osb", "o1", "zz", "z1", "dummy"])

        def subtile_steps(s):
            ts = slice(s * 128, (s + 1) * 128)
            kts = [("kt", hd) for hd in range(4)]
            qts = [("qt", hd) for hd in range(4)]
            ptr = ps[4][:].bitcast(BF16)[:, 0:512].rearrange("p (h k) -> p h k", h=4)
            pat = h4(ps[5][:])
            pat2 = h4(ps[6][:])
            pdb = (4, 7)
            po = h4(ps[6][:])
            pmm = h4(ps[7][:])

            def st_a():
                def tr(e):
                    for hd in range(4):
                        ins = e.transpose(out=ptr[:, hd, :], in_=ktT[:, hd, ts], identity=identb[:])
                    return ins
                op("pe", tr, reads=kts + ["identb"], writes=[P(4)])

                def at(e):
                    for hd in range(4):
                        ins = e.matmul(pat[:, hd, :], lhsT=kdT[:, hd, ts], rhs=qdT[:, hd, ts], start=True, stop=True)
                    return ins
                op("pe", at, reads=[("kd", hd) for hd in range(4)] + [("qd", hd) for hd in range(4)], writes=[P(5)])

                def at2(e):
                    for hd in range(4):
                        for cc in range(2):
                            r0 = cc * 64
                            t0 = s * 128 + cc * 64
                            ins = e.matmul(pat2[r0:r0 + 32, hd, r0 + 32:r0 + 64], lhsT=ktT[:, hd, t0:t0 + 32],
                                           rhs=qtT[:, hd, t0 + 32:t0 + 64], start=True, stop=True)
                    return ins
                op("pe", at2, reads=kts + qts, writes=[P(6)])

                def gm(e):
                    for hd in range(4):
                        ins = e.matmul(pmm[:, hd, :], lhsT=vhat[:, s, hd * 128:(hd + 1) * 128], rhs=WmT[:, l, hd, :],
                                       start=True, stop=True)
                    return ins
                op("pe", gm, reads=[("vhat", s, hd) for hd in range(4)] + [("WmT", l)], writes=[P(7)])
                op("act", lambda e: e.copy(out=ktok[:], in_=ptr), reads=[P(4)], writes=["ktok"])
                op("dve", lambda e: e.tensor_tensor(out=ATm[:], in0=pat,
                                                    in1=cmask[:].unsqueeze(1).to_broadcast([128, 4, 128]),
                                                    op=ALU.mult), reads=[P(5), "cmask"], writes=["ATm"])
                for cc in range(2):
                    r0 = cc * 64
                    op("act", lambda e, r0=r0: e.copy(out=ATm[r0:r0 + 32, :, r0 + 32:r0 + 64],
                                                      in_=pat2[r0:r0 + 32, :, r0 + 32:r0 + 64]),
                       reads=[P(6)], writes=["ATm"])
                for hd in range(4):
                    op("dve", lambda e, hd=hd: e.scalar_tensor_tensor(
                        out=z1[:, hd * 128:(hd + 1) * 128], in0=pmm[:, hd, :],
                        scalar=vec[:, C_LNG + l * 4 + hd:C_LNG + l * 4 + hd + 1],
                        in1=bspb[:, l, hd * 128:(hd + 1) * 128], op0=ALU.mult, op1=ALU.add),
                       reads=[P(7), "vec", ("bspb", l)], writes=["z1"])
                op("dve", lambda e: e.tensor_tensor(out=h4(zz[:]), in0=h4(z1[:]), in1=ugT[:, :, ts], op=ALU.mult),
                   reads=["z1"] + [("ug", hd) for hd in range(4)], writes=["zz"])
                op("act", lambda e: e.activation(out=zsq[:], in_=zz[:], func=AF.Square), reads=["zz"], writes=["zsq"])

            def st_b():
                for cc in range(2):
                    pds = h4(ps[pdb[cc]][:])
                    rows = slice(cc * 64, (cc + 1) * 64)

                    def dsm(e, pds=pds, rows=rows):
                        for hd in range(4):
                            ins = e.matmul(pds[:, hd, :], lhsT=ktok[rows, hd, :], rhs=vi[rows, s, hd * 128:(hd + 1) * 128],
                                           start=True, stop=True)
                        return ins
                    op("pe", dsm, reads=["ktok", ("vi", s)], writes=[P(pdb[cc])])
                op("pe", lambda e: e.matmul(ps[5][:], lhsT=onesb[:], rhs=zsq[:], start=True, stop=True),
                   reads=["zsq", "onesb"], writes=[P(5)])
                for cc in range(2):
                    c = 2 * s + cc
                    pds = h4(ps[pdb[cc]][:])

                    def bc(t, c=c):
                        return t[:, :, c:c + 1].to_broadcast([128, 4, 128])
                    op("dve", lambda e, cc=cc, bc=bc: e.tensor_tensor(out=Shat[:, cc, :, :], in0=Sst[:, l, :, :],
                                                                      in1=bc(er), op=ALU.mult),
                       reads=[("S", l)] + [("er", hd) for hd in range(4)], writes=[("Shat", cc)])
                    op("dve", lambda e, pds=pds, bc=bc: e.tensor_tensor(out=dst[:], in0=pds, in1=bc(dl), op=ALU.mult),
                       reads=[P(pdb[cc])] + [("dl", hd) for hd in range(4)], writes=["dst"])
                    op("dve", lambda e, bc=bc: e.tensor_tensor(out=Sst[:, l, :, :], in0=Sst[:, l, :, :], in1=bc(wc),
                                                               op=ALU.mult),
                       reads=[("S", l)] + [("wc", hd) for hd in range(4)], writes=[("S", l)])
                    op("dve", lambda e: e.tensor_tensor(out=Sst[:, l, :, :], in0=Sst[:, l, :, :], in1=dst[:],
                                                        op=ALU.add), reads=[("S", l), "dst"], writes=[("S", l)])
                rsqrt_psum(z2, "z2", 5, 1.0 / 128)
                op("dve", lambda e: e.tensor_tensor(out=z2[:], in0=z2[:], in1=zz[:], op=ALU.mult),
                   reads=["z2", "zz"], writes=["z2"])
                for hd in range(4):
                    op("dve", lambda e, hd=hd: e.tensor_scalar(
                        out=mixT[:, 4 + hd, ts], in0=z2[:, hd * 128:(hd + 1) * 128],
                        scalar1=vec[:, C_OG + l * 4 + hd:C_OG + l * 4 + hd + 1], scalar2=None, op0=ALU.mult),
                       reads=["z2", "vec"], writes=[("mixg", s, hd)])

            def st_c():
                def om(e):
                    for hd in range(4):
                        e.matmul(po[:, hd, :], lhsT=vi[:, s, hd * 128:(hd + 1) * 128], rhs=ATm[:, hd, :],
                                 start=True, stop=False)
                        for cc in range(2):
                            tc = slice(s * 128 + cc * 64, s * 128 + (cc + 1) * 64)
                            ins = e.matmul(po[:, hd, cc * 64:(cc + 1) * 64], lhsT=Shat[:, cc, hd, :],
                                           rhs=qtT[:, hd, tc], start=False, stop=(cc == 1))
                    return ins
                op("pe", om, reads=[("vi", s), "ATm", ("Shat", 0), ("Shat", 1)] + qts, writes=[P(6)])
                op("act", lambda e: e.copy(out=osb[:], in_=ps[6][:]), reads=[P(6)], writes=["osb"])
                op("act", lambda e: e.activation(out=osq[:], in_=ps[6][:], func=AF.Square), reads=[P(6)],
                   writes=["osq"])

            def st_d():
                op("pe", lambda e: e.matmul(ps[5][:], lhsT=onesb[:], rhs=osq[:], start=True, stop=True),
                   reads=["osq", "onesb"], writes=[P(5)])
                rsqrt_psum(o1, "o1", 5, 1.0 / 128)
                op("dve", lambda e: e.tensor_tensor(out=o1[:], in0=o1[:], in1=osb[:], op=ALU.mult),
                   reads=["o1", "osb"], writes=["o1"])
                op("dve", lambda e: e.tensor_tensor(out=mixT[:, 0:4, ts], in0=h4(o1[:]), in1=sgT[:, :, ts],
                                                    op=ALU.mult),
                   reads=["o1"] + [("sg", hd) for hd in range(4)], writes=[("mix", s)])
            return [st_a, st_b, st_c, st_d]

        for s in range(4):
            steps = [None] * 4 if dry else subtile_steps(s)
            for st in steps:
                if st is not None:
                    st()
                yield None
        mkeys = [("mix", s) for s in range(4)] + [("mixg", s, hd) for s in range(4) for hd in range(4)]
        for wb in range(2):
            sl, sk = next_w(("A", l, 28 + wb))
            if not dry:
                slv = sl[:, 0:4096].rearrange("p (k n) -> p k n", k=8)
                for dcl in range(4):
                    dc = wb * 4 + dcl
                    po = 4 + pctr["mo"] % 4
                    pctr["mo"] += 1

                    def mm(e, slv=slv, dcl=dcl, po=po):
                        for k in range(8):
                            ins = e.matmul(ps[po][:], lhsT=slv[:, k, dcl * 128:(dcl + 1) * 128], rhs=mixT[:, k, :],
                                           start=(k == 0), stop=(k == 7))
                        return ins
                    op("pe", mm, reads=[sk] + mkeys, writes=[P(po)])
                    op("dve", lambda e, dc=dc, po=po: e.scalar_tensor_tensor(
                        out=h[:, dc, :], in0=ps[po][:], scalar=modG[:, l, 1, b, dc:dc + 1], in1=h[:, dc, :],
                        op0=ALU.mult, op1=ALU.add),
                       reads=[P(po), ("modG", l, 1, b), hkeys[dc]], writes=[hkeys[dc]])
                    sq_accum(h, hkeys, si, dc)
            yield None
        mdone[l] = ti + 1
        yield ("rel", ("M",))

    tile_list = [(b, j) for b in range(BPC) for j in range(nt)]
    out_keys = []

    def load_tile(ti, hb):
        if dry:
            return
        b, j = tile_list[ti]
        src = xT[b].rearrange("(dc p) t -> p dc t", p=128)[:, :, j * T:(j + 1) * T]
        op("pool", lambda e, hb=hb, src=src: e.dma_start(out=hT[hb][:], in_=src),
           writes=[("h", hb, dc) for dc in range(8)], dma=f"x{hb}")

    def stream(si):
        hb = si
        h = None if dry else hT[hb]
        hkeys = [("h", hb, dc) for dc in range(8)]
        if si == 1:
            yield ("sync",)
        for ti in range(si, len(tile_list), 2):
            b, j = tile_list[ti]
            load_tile(ti, hb)
            for l in range(DEPTH):
                dd = (ti == 0 and l == 0)
                yield from ffn(h, hkeys, l, 0, b, si, need_sq=(l == 0))
                if dd:
                    dump("h_ffn1", None if dry else h[:], hkeys)
                yield ("sync",)
                yield from mixer(h, hkeys, l, b, j, ti, si)
                yield ("sync",)
                if dd:
                    dump("h_mix", None if dry else h[:], hkeys)
                if ti == 0 and l == 0:
                    for r in ffn(h, hkeys, l, 1, b, si):
                        if isinstance(r, tuple) and r[0] == "rel" and r[1] == ("Fg",):
                            yield from adaln(1)
                        yield r
                else:
                    yield from ffn(h, hkeys, l, 1, b, si)
                if dd:
                    dump("h_ffn2", None if dry else h[:], hkeys)
            yield ("acq", ("N",), None)
            yield None
            if not dry:
                rms_rstd(si, 0)
                for dc in range(8):
                    og = tmpf[dc % 2]
                    op("dve", lambda e, dc=dc, og=og: e.scalar_tensor_tensor(
                        out=og[:], in0=h[:, dc, :], scalar=vec[:, C_FG + dc:C_FG + dc + 1], in1=rstd[:],
                        op0=ALU.mult, op1=ALU.mult), reads=[hkeys[dc], "vec", "rstd"], writes=[("tmpf", dc % 2)])
                    ok = ("out", ti, dc)
                    op("pool", lambda e, dc=dc, og=og, b=b, j=j: e.dma_start(
                        out=outT[b, dc * 128:(dc + 1) * 128, j * T:(j + 1) * T], in_=og[:]),
                       reads=[("tmpf", dc % 2)], writes=[ok], dma=f"o{dc % 2}")
                    out_keys.append(ok)
            yield ("rel", ("N",))

    for _ in adaln(0):
        pass

    gens = [stream(0), stream(1)]
    active = [True, len(tile_list) > 1]
    pending = [None, None]
    at_sync = [False, False]
    held = {}
    seg_idx = [0, 0]
    done = [0, 0]
    seg_counts = [[0], [0]]

    def frac(si):
        if counts is None or seg_idx[si] >= len(counts[si]):
            return done[si]
        return done[si] / max(counts[si][seg_idx[si]], 1)

    while any(active):
        cands = [si for si in (0, 1) if active[si] and not at_sync[si]]
        if not cands:
            for si in (0, 1):
                if at_sync[si]:
                    at_sync[si] = False
                    seg_idx[si] += 1
                    done[si] = 0
                    seg_counts[si].append(0)
            continue
        cands.sort(key=lambda si: (frac(si), si))
        advanced = False
        for si in cands:
            if pending[si] is not None:
                locks, cond = pending[si]
                if all(k not in held for k in locks) and (cond is None or cond()):
                    for k in locks:
                        held[k] = si
                    pending[si] = None
                else:
                    continue
            advanced = True
            try:
                r = next(gens[si])
            except StopIteration:
                active[si] = False
                break
            done[si] += 1
            seg_counts[si][-1] += 1
            if isinstance(r, tuple):
                if r[0] == "acq":
                    pending[si] = (r[1], r[2])
                elif r[0] == "rel":
                    for k in r[1]:
                        assert held.get(k) == si, (k, held, si)
                        del held[k]
                elif r[0] == "sync":
                    at_sync[si] = True
            break
        assert advanced, ("scheduler deadlock", pending, held, at_sync)
    if dry:
        return wseq, seg_counts
    assert wstate["cur"] == len(wseq), (wstate, len(wseq))
    op("sp", None, reads=out_keys[-16:])
    return wseq, dbg_names


def build_nc(nt=SEQ // T, dbg=False):
    nc = bass.Bass("TRN2", target_bir_lowering=False)
    xT = nc.dram_tensor("xT", [BPC, D, SEQ], F32, kind="ExternalInput").ap()
    wA = nc.dram_tensor("wA", [DEPTH, 30, 128, 4096], F32, kind="ExternalInput").ap()
    wB = nc.dram_tensor("wB", [DEPTH, 16, 128, 2816], F32, kind="ExternalInput").ap()
    wada = nc.dram_tensor("wada", [DEPTH, 18, 128, 4096], F32, kind="ExternalInput").ap()
    vecsT = nc.dram_tensor("vecsT", [128, NV], F32, kind="ExternalInput").ap()
    wspT = nc.dram_tensor("wspT", [DEPTH, 128, 512], F32, kind="ExternalInput").ap()
    bsp = nc.dram_tensor("bsp", [DEPTH, 512], F32, kind="ExternalInput").ap()
    outT = nc.dram_tensor("outT", [BPC, D, SEQ], F32, kind="ExternalOutput").ap()
    wsA = nc.dram_tensor("wsA", [DEPTH, 30, 128, 4096], BF16, kind="Internal").ap()
    wsB = nc.dram_tensor("wsB", [DEPTH, 16, 128, 2816], BF16, kind="Internal").ap()
    D_ = (xT, wA, wB, wada, vecsT, wspT, bsp, outT, wsA, wsB)
    _, counts = _program(nc, None, Sched(nc, None, dry=True), nt, None, dbg, D_)
    wseq, _ = _program(nc, None, Sched(nc, None, dry=True), nt, None, dbg, D_, counts=counts)
    print("segment step counts", [c[:6] for c in counts])
    with ExitStack() as es:
        S = Sched(nc, es)
        _, dbg_names = _program(nc, es, S, nt, wseq, dbg, D_, counts=counts)
        print("sbuf bytes remaining", nc.sbuf_bytes_remaining, "ops", {k: len(v) for k, v in S.ops.items()})
        S.emit()
    nc._dbg_names = dbg_names
    return nc


def _blkA(W):
    return W.reshape(8, 128, 512).transpose(1, 0, 2).reshape(128, 4096)


def _blkB(W):
    return W.reshape(22, 128, 128).transpose(1, 0, 2).reshape(128, 2816)


def _prep_shared(w_ada, b_ada, norm_gain, ffn1_w13, ffn1_w2, w_in, hg_lb_logits, hg_gnorm, gm_ln_gain,
                 gm_w_spatial, gm_b_spatial, gm_out_gain, w_out, ffn2_w13, ffn2_w2, final_gain):
    f = lambda a: np.asarray(a, dtype=np.float32)
    wA = np.empty((DEPTH, 30, 128, 4096), np.float32)
    wB = np.empty((DEPTH, 16, 128, 2816), np.float32)
    wada = np.empty((DEPTH, 18, 128, 4096), np.float32)
    for l in range(DEPTH):
        for fi, (w13, w2) in enumerate(((f(ffn1_w13)[l], f(ffn1_w2)[l]), (f(ffn2_w13)[l], f(ffn2_w2)[l]))):
            for i in range(11):
                blk = np.concatenate([w13[:, 256 * i:256 * (i + 1)], w13[:, DFF + 256 * i:DFF + 256 * (i + 1)]], axis=1)
                wA[l, fi * 11 + i] = _blkA(blk)
            for i in range(8):
                wB[l, fi * 8 + i] = _blkB(w2[:, 128 * i:128 * (i + 1)])
        for i in range(6):
            wA[l, 22 + i] = _blkA(f(w_in)[l][:, 512 * i:512 * (i + 1)])
        for i in range(2):
            wA[l, 28 + i] = _blkA(f(w_out)[l][:, 512 * i:512 * (i + 1)])
        for i in range(18):
            wada[l, i] = _blkA(f(w_ada)[l][:, 512 * i:512 * (i + 1)])
    vecs = np.zeros((128, NV), np.float32)
    for l in range(DEPTH):
        vecs[:, C_BADA + l * 72:C_BADA + (l + 1) * 72] = f(b_ada)[l].reshape(72, 128).T
        for sub in range(3):
            c0 = C_NG + (l * 3 + sub) * 8
            vecs[:, c0:c0 + 8] = f(norm_gain)[l, sub].reshape(8, 128).T
        vecs[:, C_LB + l * 4:C_LB + l * 4 + 4] = f(hg_lb_logits)[l].reshape(4, 128).T
        vecs[:, C_GN + l * 4:C_GN + l * 4 + 4] = f(hg_gnorm)[l].reshape(4, 128).T
        vecs[:, C_LNG + l * 4:C_LNG + l * 4 + 4] = f(gm_ln_gain)[l].reshape(4, 128).T
        vecs[:, C_OG + l * 4:C_OG + l * 4 + 4] = f(gm_out_gain)[l].reshape(4, 128).T
    vecs[:, C_FG:C_FG + 8] = f(final_gain).reshape(8, 128).T
    wspT = np.ascontiguousarray(f(gm_w_spatial).transpose(0, 3, 1, 2).reshape(DEPTH, 128, 512))
    bspv = np.ascontiguousarray(f(gm_b_spatial).reshape(DEPTH, 512))
    return wA, wB, wada, vecs, wspT, bspv


def kernel(x, c, w_ada, b_ada, norm_gain, ffn1_w13, ffn1_w2, w_in, hg_lb_logits, hg_gnorm, gm_ln_gain,
           gm_w_spatial, gm_b_spatial, gm_out_gain, w_out, ffn2_w13, ffn2_w2, final_gain, _nt=SEQ // T,
           _cores=NCORES, _dbg=None):
    x = np.asarray(x, dtype=np.float32)
    c = np.asarray(c, dtype=np.float32)
    wA, wB, wada, vecs, wspT, bspv = _prep_shared(
        w_ada, b_ada, norm_gain, ffn1_w13, ffn1_w2, w_in, hg_lb_logits, hg_gnorm, gm_ln_gain, gm_w_spatial,
        gm_b_spatial, gm_out_gain, w_out, ffn2_w13, ffn2_w2, final_gain)
    in_maps = []
    for core in range(_cores):
        xs = x[core * BPC:(core + 1) * BPC]
        xTc = np.ascontiguousarray(xs.transpose(0, 2, 1))
        v = vecs.copy()
        cs = c[core * BPC:(core + 1) * BPC]
        v[:, C_C:C_C + 16] = cs.reshape(BPC, 8, 128).transpose(2, 1, 0).reshape(128, 16)
        in_maps.append({"xT": xTc, "wA": wA, "wB": wB, "wada": wada, "vecsT": v, "wspT": wspT, "bsp": bspv})
    nc = build_nc(_nt, dbg=_dbg is not None)
    res = run_bass_kernel_spmd(nc, in_maps, core_ids=list(range(_cores)))
    out = np.empty((NCORES * BPC, SEQ, D), np.float32)
    if _dbg is not None:
        for k in nc._dbg_names:
            _dbg[k] = np.asarray(res.results[0][k])
    for core in range(_cores):
        out[core * BPC:(core + 1) * BPC] = res.results[core]["outT"].transpose(0, 2, 1)
    return out
```

```python
from contextlib import ExitStack
import numpy as np
import concourse.bass as bass
import concourse.mybir as mybir
from concourse.bass_utils import run_bass_kernel_spmd

F32 = mybir.dt.float32
BF16 = mybir.dt.bfloat16
AF = mybir.ActivationFunctionType
ALU = mybir.AluOpType
AX = mybir.AxisListType

D = 1024
SEQ = 2048
DEPTH = 2
DFF = 2816
T = 512
NCORES = 8
BPC = 2
RMS_EPS = 1e-6
LN_EPS = 1e-5
NSLOT = 4

C_BADA = 0
C_NG = 144
C_LB = 192
C_GN = 200
C_LNG = 208
C_OG = 216
C_FG = 224
C_C = 232
NV = 256


class Sched:
    ENGS = ("pe", "act", "dve", "pool", "sp")

    def __init__(self, nc, es, dry=False):
        self.nc = nc
        self.es = es
        self.dry = dry
        self.ops = {e: [] for e in self.ENGS}
        self.sem = {}
        self.cnt = {}
        self.seen = {e: {} for e in self.ENGS}
        self.bufs = {}
        for e in self.ENGS:
            self._mk("E_" + e)

    def _mk(self, name):
        if name not in self.cnt:
            if not self.dry:
                self.sem[name] = self.es.enter_context(self.nc.semaphore(name))
            self.cnt[name] = 0
        return name

    def op(self, eng, fn, reads=(), writes=(), dma=None):
        if self.dry:
            return None
        deps = {}
        for k in reads:
            b = self.bufs.get(k)
            if b and b["w"]:
                s, v = b["w"]
                deps[s] = max(deps.get(s, 0), v)
        for k in writes:
            b = self.bufs.get(k)
            if b:
                if b["w"]:
                    s, v = b["w"]
                    deps[s] = max(deps.get(s, 0), v)
                for s, v in b["r"]:
                    deps[s] = max(deps.get(s, 0), v)
        own = "E_" + eng
        waits = []
        for s, v in deps.items():
            if s == own and eng == "pe":
                continue
            if self.seen[eng].get(s, 0) >= v:
                continue
            self.seen[eng][s] = v
            waits.append((s, v))
        if dma is None:
            sname, inc = own, 1
        else:
            sname, inc = self._mk("D_" + dma), 16
        self.cnt[sname] += inc
        tok = (sname, self.cnt[sname])
        self.ops[eng].append((waits, fn, sname, inc))
        for k in reads:
            b = self.bufs.setdefault(k, {"w": None, "r": []})
            b["r"].append(tok)
        for k in writes:
            b = self.bufs.setdefault(k, {"w": None, "r": []})
            b["w"] = tok
            b["r"] = []
        return tok

    def emit(self):
        S = self
        with self.nc.Block() as block:
            def run(name):
                def body(e):
                    for waits, fn, sname, inc in S.ops[name]:
                        for s, v in waits:
                            e.wait_ge(S.sem[s], v)
                        if fn is None:
                            e.nop().then_inc(S.sem[sname], inc)
                            continue
                        fn(e).then_inc(S.sem[sname], inc)
                return body
            block.tensor(run("pe"))
            block.scalar(run("act"))
            block.vector(run("dve"))
            block.gpsimd(run("pool"))
            block.sync(run("sp"))


def _program(nc, es, S, nt, wseq_known, dbg, D_, counts=None):
    xT, wA, wB, wada, vecsT, wspT, bsp, outT, wsA, wsB = D_
    op = S.op
    dry = S.dry

    def sb(name, shape, dt=F32):
        return es.enter_context(nc.sbuf_tensor(name, shape, dt))

    if not dry:
        hT = [sb(f"hT{i}", [128, 8, T]) for i in range(2)]
        yTf = sb("yTf", [128, 8, T], BF16)
        yTm = sb("yTm", [128, 8, T], BF16)
        gT = sb("gT", [128, 22, T], BF16)
        ssq = [sb(f"ssq{i}", [128, T]) for i in range(2)]
        ssqb = [sb(f"ssqb{i}", [128, T], BF16) for i in range(2)]
        tmpsq = sb("tmpsq", [128, T])
        slots = [sb(f"slot{i}", [128, 4096], BF16) for i in range(NSLOT)]
        tmpf = [sb(f"tmpf{i}", [128, T]) for i in range(2)]
        rstd = sb("rstd", [128, T])
        sact = [sb(f"sact{i}", [128, T]) for i in range(2)]
        vec = sb("vec", [128, NV])
        cact = sb("cact", [128, 16])
        cactb = sb("cactb", [128, 16], BF16)
        modT = sb("modT", [128, DEPTH, 72, 2])
        modA = sb("modA", [128, DEPTH, 3, 2, 8])
        modG = sb("modG", [128, DEPTH, 3, 2, 8])
        lbt = sb("lbt", [128, DEPTH, 4])
        omlt = sb("omlt", [128, DEPTH, 4])
        nomlt = sb("nomlt", [128, DEPTH, 4])
        lbd = sb("lbd", [128, 4])
        identb = sb("identb", [128, 128], BF16)
        identf = sb("identf", [128, 128])
        onesb = sb("onesb", [128, 128], BF16)
        cmask = sb("cmask", [128, 128])
        gmask = sb("gmask", [128, 128])
        rmask = sb("rmask", [128, T])
        WmT = sb("WmT", [128, DEPTH, 4, 128], BF16)
        bspb = sb("bspb", [128, DEPTH, 512])
        epsr = sb("epsr", [128, 1])
        epsl = sb("epsl", [128, 1])
        dummy = sb("dummyt", [128, 1])
        epraw = sb("epraw", [128, 4096], BF16)
        Ep = epraw[:, 0:2048].rearrange("p (h t) -> p h t", h=4)
        Ep2 = epraw[:, 2048:4096].rearrange("p (h t) -> p h t", h=4)
        epf = epraw[:].bitcast(F32)
        osb, o1, zz, z1 = epf[:, 0:512], epf[:, 512:1024], epf[:, 1024:1536], epf[:, 1536:2048]
        z2 = sb("z2", [128, 512])
        wspf = z2
        vg4 = epf[:, 0:2048].rearrange("p (a t) -> p a t", a=4)
        ktT = sb("ktT", [128, 4, T], BF16)
        kdT = sb("kdT", [128, 4, T], BF16)
        qtT = sb("qtT", [128, 4, T], BF16)
        qdT = sb("qdT", [128, 4, T], BF16)
        sgT = sb("sgT", [128, 4, T], BF16)
        ugT = sb("ugT", [128, 4, T], BF16)
        vi = sb("vi", [128, 4, 512], BF16)
        vhat = sb("vhat", [128, 4, 512], BF16)
        mixraw = sb("mixraw", [128, 8 * T], BF16)
        mixT = mixraw[:].rearrange("p (k t) -> p k t", k=8)
        mixf = mixraw[:].bitcast(F32)
        sig, logf, kk, bcs = mixf[:, 0:512], mixf[:, 512:1024], mixf[:, 1024:1536], mixf[:, 1536:2048]
        br = sb("br", [128, T])
        sigx = sb("sigx", [128, 3, T])
        Em = sb("Em", [128, T])
        er = sb("er", [128, 4, 8])
        wc = sb("wc", [128, 4, 8])
        dl = sb("dl", [128, 4, 8])
        Sst = sb("Sst", [128, DEPTH, 4, 128])
        Shat = sb("Shat", [128, 2, 4, 128], BF16)
        dst = sb("dst", [128, 4, 128])
        ktok = sb("ktok", [128, 4, 128], BF16)
        ATm = sb("ATm", [128, 4, 128], BF16)
        osq = sb("osq", [128, 512], BF16)
        zsq = sb("zsq", [128, 512], BF16)
        st1 = sb("st1", [128, 4, 4])
        st2 = sb("st2", [128, 4, 4])
        mean = sb("mean", [128, 4, 4])
        msq = sb("msq", [128, 4, 4])
        var = sb("var", [128, 4, 4])
        lrs = sb("lrs", [128, 4, 4])
        ps = [es.enter_context(nc.psum_tensor(f"ps{i}", [128, 512], F32)) for i in range(8)]

    def P(i):
        return ("ps", i)

    dbg_names = []

    wseq = [] if wseq_known is None else wseq_known
    wstate = {"issued": 0, "cur": 0, "seen": set()}

    def slot_key(i):
        return ("slot", i % NSLOT)

    def prefetch(upto):
        while wstate["issued"] <= min(upto, len(wseq) - 1):
            i = wstate["issued"]
            kind, l, blk = wseq[i]
            sl = slots[i % NSLOT]
            n = 2816 if kind == "B" else 4096
            src = {"A": wA, "B": wB, "ada": wada}[kind]
            scr = {"A": wsA, "B": wsB}.get(kind)
            bkey = (kind, l, blk)
            if scr is None or bkey not in wstate["seen"]:
                op("pool", lambda e, sl=sl, l=l, blk=blk, n=n, src=src: e.dma_start(out=sl[:, 0:n], in_=src[l, blk]),
                   writes=[slot_key(i)], dma=f"w{i % NSLOT}")
                if scr is not None:
                    wstate["seen"].add(bkey)
                    op("sp", lambda e, sl=sl, l=l, blk=blk, n=n, scr=scr: e.dma_start(out=scr[l, blk], in_=sl[:, 0:n]),
                       reads=[slot_key(i)], writes=[("scr",) + bkey], dma=f"sw{i % NSLOT}")
            else:
                op("sp", lambda e, sl=sl, l=l, blk=blk, n=n, scr=scr: e.dma_start(out=sl[:, 0:n], in_=scr[l, blk]),
                   reads=[("scr",) + bkey], writes=[slot_key(i)], dma=f"w{i % NSLOT}")
            wstate["issued"] += 1

    def next_w(spec):
        i = wstate["cur"]
        wstate["cur"] += 1
        if dry:
            wseq.append(spec)
            return None, None
        assert wseq[i] == spec, (wseq[i], spec)
        prefetch(i + NSLOT - 1)
        return slots[i % NSLOT], slot_key(i)

    if not dry:
        op("sp", lambda e: e.dma_start(out=vec[:], in_=vecsT), writes=["vec"], dma="vec")
        for l in range(DEPTH):
            op("sp", lambda e, l=l: e.dma_start(out=bspb[:, l, :], in_=bsp[l].partition_broadcast(128)),
               writes=[("bspb", l)], dma=f"bspb{l}")
        op("dve", lambda e: e.memset(epsr[:], RMS_EPS), writes=["epsr"])
        op("dve", lambda e: e.memset(epsl[:], LN_EPS), writes=["epsl"])
        op("pool", lambda e: e.memset(identf[:], 0.0), writes=["identf"])
        op("pool", lambda e: e.affine_select(out=identf[:], in_=identf[:], pattern=[[-1, 128]],
                                             compare_op=ALU.not_equal, fill=1.0, base=0, channel_multiplier=1),
           reads=["identf"], writes=["identf"])
        op("dve", lambda e: e.tensor_copy(out=identb[:], in_=identf[:]), reads=["identf"], writes=["identb"])
        op("dve", lambda e: e.memset(onesb[:], 1.0), writes=["onesb"])
        op("pool", lambda e: e.memset(cmask[:], 1.0), writes=["cmask"])
        op("pool", lambda e: e.affine_select(out=cmask[:], in_=cmask[:], pattern=[[1, 128]], compare_op=ALU.is_ge,
                                             fill=0.0, base=0, channel_multiplier=-1),
           reads=["cmask"], writes=["cmask"])
        for i_ in range(3):
            op("pool", lambda e, i_=i_: e.memset(cmask[i_ * 32:(i_ + 1) * 32, (i_ + 1) * 32:128], 0.0),
               reads=["cmask"], writes=["cmask"])
        op("pool", lambda e: e.memset(gmask[:], 1.0), writes=["gmask"])
        op("pool", lambda e: e.memset(gmask[64:128, 0:64], 0.0), reads=["gmask"], writes=["gmask"])
        op("dve", lambda e: e.memset(rmask[:], 1.0), writes=["rmask"])
        op("dve", lambda e: e.memset(rmask[:].rearrange("p (c t) -> p c t", t=64)[:, :, 0:1], 0.0),
           reads=["rmask"], writes=["rmask"])
        for l in range(DEPTH):
            op("sp", lambda e, l=l: e.dma_start(out=wspf[:], in_=wspT[l]), writes=["z2"], dma="wsp")
            op("dve", lambda e, l=l: e.tensor_tensor(
                out=WmT[:, l, :, :], in0=wspf[:].rearrange("p (h t) -> p h t", h=4),
                in1=gmask[:].unsqueeze(1).to_broadcast([128, 4, 128]), op=ALU.mult),
               reads=["z2", "gmask"], writes=[("WmT", l)])
        op("dve", lambda e: e.memset(lbt[:, 0, :], 0.0), writes=["lb0"])
        op("dve", lambda e: e.memset(omlt[:, 0, :], 1.0), writes=["oml0"])
        op("dve", lambda e: e.memset(nomlt[:, 0, :], -1.0), writes=["noml0"])
        op("dve", lambda e: e.tensor_tensor(out=lbd[:], in0=vec[:, C_LB + 4:C_LB + 8], in1=vec[:, C_LB:C_LB + 4],
                                            op=ALU.subtract), reads=["vec"], writes=["lbd"])
        op("act", lambda e: e.activation(out=lbt[:, 1, :], in_=lbd[:], func=AF.Sigmoid), reads=["lbd"], writes=["lb1"])
        op("act", lambda e: e.activation(out=omlt[:, 1, :], in_=lbd[:], func=AF.Sigmoid, scale=-1.0),
           reads=["lbd"], writes=["oml1"])
        op("dve", lambda e: e.tensor_scalar(out=nomlt[:, 1, :], in0=omlt[:, 1, :], scalar1=-1.0, scalar2=None,
                                            op0=ALU.mult), reads=["oml1"], writes=["noml1"])
        op("act", lambda e: e.activation(out=cact[:], in_=vec[:, C_C:C_C + 16], func=AF.Silu),
           reads=["vec"], writes=["cact"])
        op("dve", lambda e: e.tensor_copy(out=cactb[:], in_=cact[:]), reads=["cact"], writes=["cactb"])

    def dump(name, ap, keys):
        if not dbg or dry:
            return
        shp = [int(x) for x in ap.shape]
        dt_ = nc.dram_tensor("dbg_" + name, shp, ap.dtype, kind="ExternalOutput").ap()
        dbg_names.append("dbg_" + name)
        op("sp", lambda e: e.dma_start(out=dt_, in_=ap), reads=keys, writes=[("dbg", name)], dma="dbg_" + name)
        op("sp", None, reads=[("dbg", name)])

    def adaln(l):
        def mm_blk(cb, sl, sk):
            slv = sl[:, 0:4096].rearrange("p (k n) -> p k n", k=8)
            r = cb % 2

            def mm(e):
                for k in range(8):
                    ins = e.matmul(ps[r][0:2, :], lhsT=cactb[:, 2 * k:2 * k + 2], rhs=slv[:, k, :],
                                   start=(k == 0), stop=(k == 7))
                return ins
            op("pe", mm, reads=[sk, "cactb"], writes=[P(r)])
            op("act", lambda e: e.copy(out=sact[r][0:2, :], in_=ps[r][0:2, :]), reads=[P(r)], writes=[("sact", r)])

        def tr_blk(cb):
            r = cb % 2

            def tr(e):
                for q in range(4):
                    col = (cb * 4 + q) * 2
                    ins = e.transpose(out=ps[2][:, col:col + 2], in_=sact[r][0:2, q * 128:(q + 1) * 128],
                                      identity=identf[0:2, 0:2])
                return ins
            op("pe", tr, reads=[("sact", r), "identf"], writes=[P(2)])

        for cb in range(18):
            sl, sk = next_w(("ada", l, cb))
            if not dry:
                mm_blk(cb, sl, sk)
                if cb > 0:
                    tr_blk(cb - 1)
            if cb % 2 == 1:
                yield None
        if not dry:
            tr_blk(17)
        if dry:
            return
        pm = ps[2]
        op("dve", lambda e: e.tensor_tensor(
            out=modT[:, l, :, :], in0=pm[:, 0:144].rearrange("p (c b) -> p c b", b=2),
            in1=vec[:, C_BADA + l * 72:C_BADA + (l + 1) * 72].unsqueeze(2).to_broadcast([128, 72, 2]),
            op=ALU.add), reads=[P(2), "vec"], writes=[("modT", l)])
        for sub in range(3):
            for b in range(BPC):
                ng = vec[:, C_NG + (l * 3 + sub) * 8:C_NG + (l * 3 + sub) * 8 + 8]
                op("dve", lambda e, sub=sub, b=b, ng=ng: e.scalar_tensor_tensor(
                    out=modA[:, l, sub, b, :], in0=modT[:, l, (3 * sub + 1) * 8:(3 * sub + 2) * 8, b], scalar=1.0,
                    in1=ng, op0=ALU.add, op1=ALU.mult), reads=[("modT", l), "vec"], writes=[("modA", l, sub, b)])
                gs = 1.0 if sub == 1 else 0.5
                op("dve", lambda e, sub=sub, b=b, gs=gs: e.tensor_scalar(
                    out=modG[:, l, sub, b, :], in0=modT[:, l, (3 * sub + 2) * 8:(3 * sub + 3) * 8, b], scalar1=gs,
                    scalar2=None, op0=ALU.mult), reads=[("modT", l)], writes=[("modG", l, sub, b)])

    def rsqrt_psum(dst_t, dkey, pb, scale):
        op("act", lambda e: e.activation(out=dst_t[:], in_=ps[pb][:], func=AF.Ln, scale=scale, bias=epsr[:]),
           reads=[P(pb), "epsr"], writes=[dkey])
        op("act", lambda e: e.activation(out=dst_t[:], in_=dst_t[:], func=AF.Exp, scale=-0.5),
           reads=[dkey], writes=[dkey])

    def sq_accum(h, hkeys, si, dc):
        if dc == 0:
            op("dve", lambda e: e.tensor_tensor(out=ssq[si][:], in0=h[:, 0, :], in1=h[:, 0, :], op=ALU.mult),
               reads=[hkeys[0]], writes=[("ssq", si)])
            return
        op("dve", lambda e: e.tensor_tensor(out=tmpsq[:], in0=h[:, dc, :], in1=h[:, dc, :], op=ALU.mult),
           reads=[hkeys[dc]], writes=["tmpsq"])
        if dc < 7:
            op("dve", lambda e: e.tensor_tensor(out=ssq[si][:], in0=ssq[si][:], in1=tmpsq[:], op=ALU.add),
               reads=[("ssq", si), "tmpsq"], writes=[("ssq", si)])
        else:
            op("dve", lambda e: e.tensor_tensor(out=ssqb[si][:], in0=ssq[si][:], in1=tmpsq[:], op=ALU.add),
               reads=[("ssq", si), "tmpsq"], writes=[("ssqb", si)])

    def rms_squares(h, hkeys, si):
        for dc in range(8):
            sq_accum(h, hkeys, si, dc)

    def rms_rstd(si, pb):
        op("pe", lambda e: e.matmul(ps[pb][:], lhsT=onesb[:], rhs=ssqb[si][:], start=True, stop=True),
           reads=[("ssqb", si), "onesb"], writes=[P(pb)])
        rsqrt_psum(rstd, "rstd", pb, 1.0 / D)

    def norm_mod(h, hkeys, y, ykeys, l, sub, b, pb):
        for dc in range(8):
            tf = tmpf[dc % 2]
            op("dve", lambda e, dc=dc, tf=tf: e.scalar_tensor_tensor(
                out=tf[:], in0=h[:, dc, :], scalar=modA[:, l, sub, b, dc:dc + 1], in1=rstd[:],
                op0=ALU.mult, op1=ALU.mult),
               reads=[hkeys[dc], "rstd", ("modA", l, sub, b)], writes=[("tmpf", dc % 2)])
            op("act", lambda e, dc=dc, tf=tf: e.activation(
                out=y[:, dc, :], in_=tf[:], func=AF.Identity,
                bias=modT[:, l, 3 * sub * 8 + dc, b:b + 1], scale=1.0),
               reads=[("tmpf", dc % 2), ("modT", l)], writes=[ykeys[dc]])

    ykf = [("yf", dc) for dc in range(8)]
    ykm = [("ym", dc) for dc in range(8)]
    pctr = {"ab": 0, "o": 0, "mo": 0}

    def ffn(h, hkeys, l, f, b, si, need_sq=False):
        sub = 0 if f == 0 else 2
        yield ("acq", ("N", "Fy"), None)
        if not dry and need_sq:
            rms_squares(h, hkeys, si)
        yield None
        if not dry:
            rms_rstd(si, 0)
        yield None
        if not dry:
            norm_mod(h, hkeys, yTf, ykf, l, sub, b, 0)
        yield ("rel", ("N",))
        yield ("acq", ("Fg",), None)
        for blk in range(11):
            sl, sk = next_w(("A", l, f * 11 + blk))
            if not dry:
                slv = sl[:, 0:4096].rearrange("p (k n) -> p k n", k=8)
                for ch in range(2):
                    r = pctr["ab"] % 2
                    pctr["ab"] += 1
                    pa, pb = r, 2 + r
                    hc = blk * 2 + ch

                    def mm(e, slv=slv, ch=ch, pa=pa, pb=pb):
                        for k in range(8):
                            e.matmul(ps[pa][:], lhsT=slv[:, k, ch * 128:(ch + 1) * 128], rhs=yTf[:, k, :],
                                     start=(k == 0), stop=(k == 7))
                        for k in range(8):
                            ins = e.matmul(ps[pb][:], lhsT=slv[:, k, 256 + ch * 128:256 + (ch + 1) * 128],
                                           rhs=yTf[:, k, :], start=(k == 0), stop=(k == 7))
                        return ins
                    op("pe", mm, reads=[sk] + ykf, writes=[P(pa), P(pb)])
                    op("act", lambda e, pa=pa, r=r: e.activation(out=sact[r][:], in_=ps[pa][:], func=AF.Silu),
                       reads=[P(pa)], writes=[("sact", r)])
                    op("dve", lambda e, pb=pb, r=r, hc=hc: e.tensor_tensor(out=gT[:, hc, :], in0=sact[r][:],
                                                                           in1=ps[pb][:], op=ALU.mult),
                       reads=[("sact", r), P(pb)], writes=[("g", hc)])
            yield None
        yield ("rel", ("Fy",))
        gkeys = [("g", i) for i in range(22)]
        for dc in range(8):
            sl, sk = next_w(("B", l, f * 8 + dc))
            if not dry:
                slv = sl[:, 0:2816].rearrange("p (k n) -> p k n", k=22)
                po = 2 + pctr["o"] % 2
                pctr["o"] += 1

                def mm(e, slv=slv, po=po):
                    for k in range(22):
                        ins = e.matmul(ps[po][:], lhsT=slv[:, k, :], rhs=gT[:, k, :], start=(k == 0), stop=(k == 21))
                    return ins
                op("pe", mm, reads=[sk] + gkeys, writes=[P(po)])
                op("dve", lambda e, dc=dc, po=po: e.scalar_tensor_tensor(
                    out=h[:, dc, :], in0=ps[po][:], scalar=modG[:, l, sub, b, dc:dc + 1], in1=h[:, dc, :],
                    op0=ALU.mult, op1=ALU.add),
                   reads=[P(po), ("modG", l, sub, b), hkeys[dc]], writes=[hkeys[dc]])
                sq_accum(h, hkeys, si, dc)
            if dc % 2 == 1:
                yield None
        yield ("rel", ("Fg",))

    def c3(ap):
        return ap.rearrange("p (c t) -> p c t", t=64)

    def c32(ap):
        return ap.rearrange("p (c t) -> p c t", t=32)

    def h4(ap):
        return ap.rearrange("p (h t) -> p h t", h=4)

    mdone = [0] * DEPTH

    def mixer(h, hkeys, l, b, j, ti, si):
        yield ("acq", ("N", "M"), (lambda: mdone[l] == ti))
        yield None
        if not dry:
            rms_rstd(si, 4)
        yield None
        if not dry:
            norm_mod(h, hkeys, yTm, ykm, l, 1, b, 4)
        yield ("rel", ("N",))
        if j == 0 and not dry:
            op("dve", lambda e: e.memset(Sst[:, l, :, :], 0.0), writes=[("S", l)])

        def proj_fm(slv, hd, pb):
            def mm(e):
                for k in range(8):
                    ins = e.matmul(ps[pb][:], lhsT=slv[:, k, hd * 128:(hd + 1) * 128], rhs=yTm[:, k, :],
                                   start=(k == 0), stop=(k == 7))
                return ins
            return mm

        def proj_tm(slv, s, pb):
            def mm(e):
                for k in range(8):
                    ins = e.matmul(ps[pb][:], lhsT=yTm[:, k, s * 128:(s + 1) * 128], rhs=slv[:, k, :],
                                   start=(k == 0), stop=(k == 7))
                return ins
            return mm

        A_ = lambda sl: sl[:, 0:4096].rearrange("p (k n) -> p k n", k=8)
        sigs = None if dry else [(sig, "sig0"), (sigx[:, 0, :], "sig1"), (sigx[:, 1, :], "sig2"), (sigx[:, 2, :], "sig3")]
        sl, sk = next_w(("A", l, 23))
        if not dry:
            slv = A_(sl)
            for hd in range(4):
                op("pe", proj_fm(slv, hd, 4 + hd), reads=[sk] + ykm, writes=[P(4 + hd)])
            for hd in range(4):
                sb_, skey = sigs[hd]
                op("act", lambda e, hd=hd, sb_=sb_: e.activation(out=sb_, in_=ps[4 + hd][:], func=AF.Sigmoid),
                   reads=[P(4 + hd)], writes=[skey])
        yield None
        sl, sk = next_w(("A", l, 25))
        if not dry:
            slv = A_(sl)
            for hd in range(4):
                op("pe", proj_fm(slv, hd, 4 + hd), reads=[sk] + ykm, writes=[P(4 + hd)])
            for hd in range(4):
                tb, tk = (br, "br") if hd % 2 == 0 else (Em, "Em")
                op("act", lambda e, hd=hd, tb=tb: e.activation(out=tb[:], in_=ps[4 + hd][:], func=AF.Silu),
                   reads=[P(4 + hd)], writes=[tk])
                op("dve", lambda e, hd=hd, tb=tb: e.tensor_scalar(
                    out=sgT[:, hd, :], in0=tb[:], scalar1=vec[:, C_GN + l * 4 + hd:C_GN + l * 4 + hd + 1],
                    scalar2=None, op0=ALU.mult), reads=[tk, "vec"], writes=[("sg", hd)])
        yield None
        sl, sk = next_w(("A", l, 26))
        if not dry:
            slv = A_(sl)
            for hd in range(4):
                op("pe", proj_fm(slv, hd, 4 + hd), reads=[sk] + ykm, writes=[P(4 + hd)])
            for hd in range(4):
                op("act", lambda e, hd=hd: e.activation(out=ugT[:, hd, :], in_=ps[4 + hd][:],
                                                        func=AF.Gelu_apprx_tanh),
                   reads=[P(4 + hd)], writes=[("ug", hd)])
        yield None
        sl, sk = next_w(("A", l, 24))
        if not dry:
            slv = A_(sl)
            for s in range(4):
                op("pe", proj_tm(slv, s, 4 + s), reads=[sk] + ykm, writes=[P(4 + s)])
            for s in range(4):
                op("act", lambda e, s=s: e.copy(out=vi[:, s, :], in_=ps[4 + s][:]), reads=[P(4 + s)],
                   writes=[("vi", s)])
        yield None
        sl, sk = next_w(("A", l, 27))
        vgk4 = ["vg0", "vg1", "vg2", "vg3"]
        if not dry:
            slv = A_(sl)
            for s in range(4):
                op("pe", proj_tm(slv, s, 4 + s), reads=[sk] + ykm, writes=[P(4 + s)])
            for s in range(4):
                op("act", lambda e, s=s: e.activation(out=vg4[:, s, :], in_=ps[4 + s][:], func=AF.Gelu_apprx_tanh),
                   reads=[P(4 + s)], writes=[vgk4[s]])
            for s in range(4):
                op("dve", lambda e, s=s: e.tensor_reduce(out=st1[:, s, :], in_=h4(vg4[:, s, :]), axis=AX.X, op=ALU.add),
                   reads=[vgk4[s]], writes=["st1"])
                op("dve", lambda e, s=s: e.tensor_tensor(out=z2[:], in0=vg4[:, s, :], in1=vg4[:, s, :], op=ALU.mult),
                   reads=[vgk4[s]], writes=["z2"])
                op("dve", lambda e, s=s: e.tensor_reduce(out=st2[:, s, :], in_=h4(z2[:]), axis=AX.X, op=ALU.add),
                   reads=["z2"], writes=["st2"])
            op("dve", lambda e: e.tensor_scalar(out=mean[:], in0=st1[:], scalar1=1.0 / 128, scalar2=None,
                                                op0=ALU.mult), reads=["st1"], writes=["mean"])
            op("dve", lambda e: e.tensor_tensor(out=msq[:], in0=mean[:], in1=mean[:], op=ALU.mult),
               reads=["mean"], writes=["msq"])
            op("dve", lambda e: e.scalar_tensor_tensor(out=var[:], in0=st2[:], scalar=1.0 / 128, in1=msq[:],
                                                       op0=ALU.mult, op1=ALU.subtract),
               reads=["st2", "msq"], writes=["var"])
            op("act", lambda e: e.activation(out=lrs[:], in_=var[:], func=AF.Ln, scale=1.0, bias=epsl[:]),
               reads=["var", "epsl"], writes=["lrs"])
            op("act", lambda e: e.activation(out=lrs[:], in_=lrs[:], func=AF.Exp, scale=-0.5),
               reads=["lrs"], writes=["lrs"])
        yield None
        if not dry:
            for s in range(4):
                for hd in range(4):
                    op("dve", lambda e, s=s, hd=hd: e.tensor_scalar(
                        out=vhat[:, s, hd * 128:(hd + 1) * 128], in0=vg4[:, s, hd * 128:(hd + 1) * 128],
                        scalar1=mean[:, s, hd:hd + 1], scalar2=lrs[:, s, hd:hd + 1], op0=ALU.subtract, op1=ALU.mult),
                       reads=[vgk4[s], "mean", "lrs"], writes=[("vhat", s, hd)])
        yield None
        for hd in range(4):
            if not dry:
                sb_, skey = sigs[hd]
                op("act", lambda e, hd=hd, sb_=sb_: e.activation(out=logf[:], in_=sb_, func=AF.Ln,
                                                                  scale=omlt[:, l, hd:hd + 1], bias=lbt[:, l, hd:hd + 1]),
                   reads=[skey, f"oml{l}", f"lb{l}"], writes=["logf"])
                op("dve", lambda e, hd=hd, sb_=sb_: e.tensor_scalar(out=kk[:], in0=sb_, scalar1=nomlt[:, l, hd:hd + 1],
                                                                    scalar2=omlt[:, l, hd:hd + 1], op0=ALU.mult, op1=ALU.add),
                   reads=[skey, f"oml{l}", f"noml{l}"], writes=["kk"])
                op("dve", lambda e: e.tensor_tensor_scan(out=bcs[:], data0=rmask[:], data1=logf[:], initial=0.0,
                                                         op0=ALU.mult, op1=ALU.add),
                   reads=["logf", "rmask"], writes=["bcs"])
                op("dve", lambda e: e.tensor_tensor(out=c3(br[:]), in0=c3(bcs[:]),
                                                    in1=c3(bcs[:])[:, :, 31:32].to_broadcast([128, 8, 64]),
                                                    op=ALU.subtract), reads=["bcs"], writes=["br"])
                op("act", lambda e, hd=hd: e.activation(out=Ep[:, hd, :], in_=br[:], func=AF.Exp),
                   reads=["br"], writes=[("Ep", hd), "vg0", "vg1", "vg2", "vg3"])
                op("act", lambda e: e.activation(out=Em[:], in_=br[:], func=AF.Exp, scale=-1.0),
                   reads=["br"], writes=["Em"])
                op("act", lambda e, hd=hd: e.activation(out=dl[:, hd, :], in_=c3(br[:])[:, :, 63], func=AF.Exp),
                   reads=["br"], writes=[("dl", hd)])
                op("act", lambda e, hd=hd: e.activation(out=er[:, hd, :], in_=c3(bcs[:])[:, :, 31], func=AF.Exp),
                   reads=["bcs"], writes=[("er", hd)])
                op("act", lambda e, hd=hd: e.activation(out=wc[:, hd, :], in_=c3(bcs[:])[:, :, 63], func=AF.Exp),
                   reads=["bcs"], writes=[("wc", hd)])
                op("dve", lambda e, hd=hd: e.tensor_tensor(out=ktT[:, hd, :], in0=kk[:], in1=Em[:], op=ALU.mult),
                   reads=["kk", "Em"], writes=[("kt", hd)])
                op("dve", lambda e: e.tensor_tensor(out=c32(br[:]), in0=c32(bcs[:]),
                                                    in1=c32(bcs[:])[:, :, 15:16].to_broadcast([128, 16, 32]),
                                                    op=ALU.subtract), reads=["bcs"], writes=["br"])
                op("act", lambda e, hd=hd: e.activation(out=Ep2[:, hd, :], in_=br[:], func=AF.Exp),
                   reads=["br"], writes=[("Ep2", hd), "vg0", "vg1", "vg2", "vg3"])
                op("act", lambda e: e.activation(out=Em[:], in_=br[:], func=AF.Exp, scale=-1.0),
                   reads=["br"], writes=["Em"])
                op("dve", lambda e, hd=hd: e.tensor_tensor(out=kdT[:, hd, :], in0=kk[:], in1=Em[:], op=ALU.mult),
                   reads=["kk", "Em"], writes=[("kd", hd)])
            yield None
        sl, sk = next_w(("A", l, 22))
        if not dry:
            slv = A_(sl)
            for hd in range(4):
                op("pe", proj_fm(slv, hd, 4 + hd), reads=[sk] + ykm, writes=[P(4 + hd)])
            for hd in range(4):
                pb = 4 + hd
                op("dve", lambda e, hd=hd, pb=pb: e.tensor_tensor(out=qtT[:, hd, :], in0=ps[pb][:], in1=Ep[:, hd, :],
                                                                  op=ALU.mult),
                   reads=[P(pb), ("Ep", hd)], writes=[("qt", hd)])
                op("dve", lambda e, hd=hd, pb=pb: e.tensor_tensor(out=qdT[:, hd, :], in0=ps[pb][:], in1=Ep2[:, hd, :],
                                                                  op=ALU.mult),
                   reads=[P(pb), ("Ep2", hd)], writes=[("qd", hd)])
        yield None
        if not dry:
            op("dve", lambda e: e.memset(dummy[:], 0.0),
               reads=[("Ep", hd) for hd in range(4)] + [("Ep2", hd) for hd in range(4)],
               writes=["osb", "o1", "zz", "z1", "dummy"])

        def subtile_steps(s):
            ts = slice(s * 128, (s + 1) * 128)
            kts = [("kt", hd) for hd in range(4)]
            qts = [("qt", hd) for hd in range(4)]
            ptr = ps[4][:].bitcast(BF16)[:, 0:512].rearrange("p (h k) -> p h k", h=4)
            pat = h4(ps[5][:])
            pat2 = h4(ps[6][:])
            pdb = (4, 7)
            po = h4(ps[6][:])
            pmm = h4(ps[7][:])

            def st_a():
                def tr(e):
                    for hd in range(4):
                        ins = e.transpose(out=ptr[:, hd, :], in_=ktT[:, hd, ts], identity=identb[:])
                    return ins
                op("pe", tr, reads=kts + ["identb"], writes=[P(4)])

                def at(e):
                    for hd in range(4):
                        ins = e.matmul(pat[:, hd, :], lhsT=kdT[:, hd, ts], rhs=qdT[:, hd, ts], start=True, stop=True)
                    return ins
                op("pe", at, reads=[("kd", hd) for hd in range(4)] + [("qd", hd) for hd in range(4)], writes=[P(5)])

                def at2(e):
                    for hd in range(4):
                        for cc in range(2):
                            r0 = cc * 64
                            t0 = s * 128 + cc * 64
                            ins = e.matmul(pat2[r0:r0 + 32, hd, r0 + 32:r0 + 64], lhsT=ktT[:, hd, t0:t0 + 32],
                                           rhs=qtT[:, hd, t0 + 32:t0 + 64], start=True, stop=True)
                    return ins
                op("pe", at2, reads=kts + qts, writes=[P(6)])

                def gm(e):
                    for hd in range(4):
                        ins = e.matmul(pmm[:, hd, :], lhsT=vhat[:, s, hd * 128:(hd + 1) * 128], rhs=WmT[:, l, hd, :],
                                       start=True, stop=True)
                    return ins
                op("pe", gm, reads=[("vhat", s, hd) for hd in range(4)] + [("WmT", l)], writes=[P(7)])
                op("act", lambda e: e.copy(out=ktok[:], in_=ptr), reads=[P(4)], writes=["ktok"])
                op("dve", lambda e: e.tensor_tensor(out=ATm[:], in0=pat,
                                                    in1=cmask[:].unsqueeze(1).to_broadcast([128, 4, 128]),
                                                    op=ALU.mult), reads=[P(5), "cmask"], writes=["ATm"])
                for cc in range(2):
                    r0 = cc * 64
                    op("act", lambda e, r0=r0: e.copy(out=ATm[r0:r0 + 32, :, r0 + 32:r0 + 64],
                                                      in_=pat2[r0:r0 + 32, :, r0 + 32:r0 + 64]),
                       reads=[P(6)], writes=["ATm"])
                for hd in range(4):
                    op("dve", lambda e, hd=hd: e.scalar_tensor_tensor(
                        out=z1[:, hd * 128:(hd + 1) * 128], in0=pmm[:, hd, :],
                        scalar=vec[:, C_LNG + l * 4 + hd:C_LNG + l * 4 + hd + 1],
                        in1=bspb[:, l, hd * 128:(hd + 1) * 128], op0=ALU.mult, op1=ALU.add),
                       reads=[P(7), "vec", ("bspb", l)], writes=["z1"])
                op("dve", lambda e: e.tensor_tensor(out=h4(zz[:]), in0=h4(z1[:]), in1=ugT[:, :, ts], op=ALU.mult),
                   reads=["z1"] + [("ug", hd) for hd in range(4)], writes=["zz"])
                op("act", lambda e: e.activation(out=zsq[:], in_=zz[:], func=AF.Square), reads=["zz"], writes=["zsq"])

            def st_b():
                for cc in range(2):
                    pds = h4(ps[pdb[cc]][:])
                    rows = slice(cc * 64, (cc + 1) * 64)

                    def dsm(e, pds=pds, rows=rows):
                        for hd in range(4):
                            ins = e.matmul(pds[:, hd, :], lhsT=ktok[rows, hd, :], rhs=vi[rows, s, hd * 128:(hd + 1) * 128],
                                           start=True, stop=True)
                        return ins
                    op("pe", dsm, reads=["ktok", ("vi", s)], writes=[P(pdb[cc])])
                op("pe", lambda e: e.matmul(ps[5][:], lhsT=onesb[:], rhs=zsq[:], start=True, stop=True),
                   reads=["zsq", "onesb"], writes=[P(5)])
                for cc in range(2):
                    c = 2 * s + cc
                    pds = h4(ps[pdb[cc]][:])

                    def bc(t, c=c):
                        return t[:, :, c:c + 1].to_broadcast([128, 4, 128])
                    op("dve", lambda e, cc=cc, bc=bc: e.tensor_tensor(out=Shat[:, cc, :, :], in0=Sst[:, l, :, :],
                                                                      in1=bc(er), op=ALU.mult),
                       reads=[("S", l)] + [("er", hd) for hd in range(4)], writes=[("Shat", cc)])
                    op("dve", lambda e, pds=pds, bc=bc: e.tensor_tensor(out=dst[:], in0=pds, in1=bc(dl), op=ALU.mult),
                       reads=[P(pdb[cc])] + [("dl", hd) for hd in range(4)], writes=["dst"])
                    op("dve", lambda e, bc=bc: e.tensor_tensor(out=Sst[:, l, :, :], in0=Sst[:, l, :, :], in1=bc(wc),
                                                               op=ALU.mult),
                       reads=[("S", l)] + [("wc", hd) for hd in range(4)], writes=[("S", l)])
                    op("dve", lambda e: e.tensor_tensor(out=Sst[:, l, :, :], in0=Sst[:, l, :, :], in1=dst[:],
                                                        op=ALU.add), reads=[("S", l), "dst"], writes=[("S", l)])
            def st_b2():
                rsqrt_psum(z2, "z2", 5, 1.0 / 128)
                op("dve", lambda e: e.tensor_tensor(out=z2[:], in0=z2[:], in1=zz[:], op=ALU.mult),
                   reads=["z2", "zz"], writes=["z2"])
                for hd in range(4):
                    op("dve", lambda e, hd=hd: e.tensor_scalar(
                        out=mixT[:, 4 + hd, ts], in0=z2[:, hd * 128:(hd + 1) * 128],
                        scalar1=vec[:, C_OG + l * 4 + hd:C_OG + l * 4 + hd + 1], scalar2=None, op0=ALU.mult),
                       reads=["z2", "vec"], writes=[("mixg", s, hd)])

            def st_c():
                def om(e):
                    for hd in range(4):
                        e.matmul(po[:, hd, :], lhsT=vi[:, s, hd * 128:(hd + 1) * 128], rhs=ATm[:, hd, :],
                                 start=True, stop=False)
                        for cc in range(2):
                            tc = slice(s * 128 + cc * 64, s * 128 + (cc + 1) * 64)
                            ins = e.matmul(po[:, hd, cc * 64:(cc + 1) * 64], lhsT=Shat[:, cc, hd, :],
                                           rhs=qtT[:, hd, tc], start=False, stop=(cc == 1))
                    return ins
                op("pe", om, reads=[("vi", s), "ATm", ("Shat", 0), ("Shat", 1)] + qts, writes=[P(6)])
                op("act", lambda e: e.copy(out=osb[:], in_=ps[6][:]), reads=[P(6)], writes=["osb"])
                op("act", lambda e: e.activation(out=osq[:], in_=ps[6][:], func=AF.Square), reads=[P(6)],
                   writes=["osq"])

            def st_d():
                op("pe", lambda e: e.matmul(ps[5][:], lhsT=onesb[:], rhs=osq[:], start=True, stop=True),
                   reads=["osq", "onesb"], writes=[P(5)])
                rsqrt_psum(o1, "o1", 5, 1.0 / 128)
                op("dve", lambda e: e.tensor_tensor(out=o1[:], in0=o1[:], in1=osb[:], op=ALU.mult),
                   reads=["o1", "osb"], writes=["o1"])
                op("dve", lambda e: e.tensor_tensor(out=mixT[:, 0:4, ts], in0=h4(o1[:]), in1=sgT[:, :, ts],
                                                    op=ALU.mult),
                   reads=["o1"] + [("sg", hd) for hd in range(4)], writes=[("mix", s)])
            return [st_a, st_b, st_b2, st_c, st_d]

        for s in range(4):
            steps = [None] * 5 if dry else subtile_steps(s)
            for st in steps:
                if st is not None:
                    st()
                yield None
        mkeys = [("mix", s) for s in range(4)] + [("mixg", s, hd) for s in range(4) for hd in range(4)]
        for wb in range(2):
            sl, sk = next_w(("A", l, 28 + wb))
            if not dry:
                slv = sl[:, 0:4096].rearrange("p (k n) -> p k n", k=8)
                for dcl in range(4):
                    dc = wb * 4 + dcl
                    po = 4 + pctr["mo"] % 4
                    pctr["mo"] += 1

                    def mm(e, slv=slv, dcl=dcl, po=po):
                        for k in range(8):
                            ins = e.matmul(ps[po][:], lhsT=slv[:, k, dcl * 128:(dcl + 1) * 128], rhs=mixT[:, k, :],
                                           start=(k == 0), stop=(k == 7))
                        return ins
                    op("pe", mm, reads=[sk] + mkeys, writes=[P(po)])
                    op("dve", lambda e, dc=dc, po=po: e.scalar_tensor_tensor(
                        out=h[:, dc, :], in0=ps[po][:], scalar=modG[:, l, 1, b, dc:dc + 1], in1=h[:, dc, :],
                        op0=ALU.mult, op1=ALU.add),
                       reads=[P(po), ("modG", l, 1, b), hkeys[dc]], writes=[hkeys[dc]])
                    sq_accum(h, hkeys, si, dc)
            yield None
        mdone[l] = ti + 1
        yield ("rel", ("M",))

    tile_list = [(b, j) for b in range(BPC) for j in range(nt)]
    out_keys = []

    def load_tile(ti, hb):
        if dry:
            return
        b, j = tile_list[ti]
        src = xT[b].rearrange("(dc p) t -> p dc t", p=128)[:, :, j * T:(j + 1) * T]
        op("pool", lambda e, hb=hb, src=src: e.dma_start(out=hT[hb][:], in_=src),
           writes=[("h", hb, dc) for dc in range(8)], dma=f"x{hb}")

    def stream(si):
        hb = si
        h = None if dry else hT[hb]
        hkeys = [("h", hb, dc) for dc in range(8)]
        if si == 1:
            yield ("sync",)
        for ti in range(si, len(tile_list), 2):
            b, j = tile_list[ti]
            load_tile(ti, hb)
            for l in range(DEPTH):
                dd = (ti == 0 and l == 0)
                yield from ffn(h, hkeys, l, 0, b, si, need_sq=(l == 0))
                if dd:
                    dump("h_ffn1", None if dry else h[:], hkeys)
                yield ("sync",)
                yield from mixer(h, hkeys, l, b, j, ti, si)
                yield ("sync",)
                if dd:
                    dump("h_mix", None if dry else h[:], hkeys)
                if ti == 0 and l == 0:
                    for r in ffn(h, hkeys, l, 1, b, si):
                        if isinstance(r, tuple) and r[0] == "rel" and r[1] == ("Fg",):
                            yield from adaln(1)
                        yield r
                else:
                    yield from ffn(h, hkeys, l, 1, b, si)
                if dd:
                    dump("h_ffn2", None if dry else h[:], hkeys)
            yield ("acq", ("N",), None)
            yield None
            if not dry:
                rms_rstd(si, 0)
                for dc in range(8):
                    og = tmpf[dc % 2]
                    op("dve", lambda e, dc=dc, og=og: e.scalar_tensor_tensor(
                        out=og[:], in0=h[:, dc, :], scalar=vec[:, C_FG + dc:C_FG + dc + 1], in1=rstd[:],
                        op0=ALU.mult, op1=ALU.mult), reads=[hkeys[dc], "vec", "rstd"], writes=[("tmpf", dc % 2)])
                    ok = ("out", ti, dc)
                    op("pool", lambda e, dc=dc, og=og, b=b, j=j: e.dma_start(
                        out=outT[b, dc * 128:(dc + 1) * 128, j * T:(j + 1) * T], in_=og[:]),
                       reads=[("tmpf", dc % 2)], writes=[ok], dma=f"o{dc % 2}")
                    out_keys.append(ok)
            yield ("rel", ("N",))

    for _ in adaln(0):
        pass

    gens = [stream(0), stream(1)]
    active = [True, len(tile_list) > 1]
    pending = [None, None]
    at_sync = [False, False]
    held = {}
    seg_idx = [0, 0]
    done = [0, 0]
    seg_counts = [[0], [0]]

    def frac(si):
        if counts is None or seg_idx[si] >= len(counts[si]):
            return done[si]
        return done[si] / max(counts[si][seg_idx[si]], 1)

    while any(active):
        cands = [si for si in (0, 1) if active[si] and not at_sync[si]]
        if not cands:
            for si in (0, 1):
                if at_sync[si]:
                    at_sync[si] = False
                    seg_idx[si] += 1
                    done[si] = 0
                    seg_counts[si].append(0)
            continue
        cands.sort(key=lambda si: (frac(si), si))
        advanced = False
        for si in cands:
            if pending[si] is not None:
                locks, cond = pending[si]
                if all(k not in held for k in locks) and (cond is None or cond()):
                    for k in locks:
                        held[k] = si
                    pending[si] = None
                else:
                    continue
            advanced = True
            try:
                r = next(gens[si])
            except StopIteration:
                active[si] = False
                break
            done[si] += 1
            seg_counts[si][-1] += 1
            if isinstance(r, tuple):
                if r[0] == "acq":
                    pending[si] = (r[1], r[2])
                elif r[0] == "rel":
                    for k in r[1]:
                        assert held.get(k) == si, (k, held, si)
                        del held[k]
                elif r[0] == "sync":
                    at_sync[si] = True
            break
        assert advanced, ("scheduler deadlock", pending, held, at_sync)
    if dry:
        return wseq, seg_counts
    assert wstate["cur"] == len(wseq), (wstate, len(wseq))
    op("sp", None, reads=out_keys[-16:])
    return wseq, dbg_names


def build_nc(nt=SEQ // T, dbg=False):
    nc = bass.Bass("TRN2", target_bir_lowering=False)
    xT = nc.dram_tensor("xT", [BPC, D, SEQ], F32, kind="ExternalInput").ap()
    wA = nc.dram_tensor("wA", [DEPTH, 30, 128, 4096], F32, kind="ExternalInput").ap()
    wB = nc.dram_tensor("wB", [DEPTH, 16, 128, 2816], F32, kind="ExternalInput").ap()
    wada = nc.dram_tensor("wada", [DEPTH, 18, 128, 4096], F32, kind="ExternalInput").ap()
    vecsT = nc.dram_tensor("vecsT", [128, NV], F32, kind="ExternalInput").ap()
    wspT = nc.dram_tensor("wspT", [DEPTH, 128, 512], F32, kind="ExternalInput").ap()
    bsp = nc.dram_tensor("bsp", [DEPTH, 512], F32, kind="ExternalInput").ap()
    outT = nc.dram_tensor("outT", [BPC, D, SEQ], F32, kind="ExternalOutput").ap()
    wsA = nc.dram_tensor("wsA", [DEPTH, 30, 128, 4096], BF16, kind="Internal").ap()
    wsB = nc.dram_tensor("wsB", [DEPTH, 16, 128, 2816], BF16, kind="Internal").ap()
    D_ = (xT, wA, wB, wada, vecsT, wspT, bsp, outT, wsA, wsB)
    _, counts = _program(nc, None, Sched(nc, None, dry=True), nt, None, dbg, D_)
    wseq, _ = _program(nc, None, Sched(nc, None, dry=True), nt, None, dbg, D_, counts=counts)
    print("segment step counts", [c[:6] for c in counts])
    with ExitStack() as es:
        S = Sched(nc, es)
        _, dbg_names = _program(nc, es, S, nt, wseq, dbg, D_, counts=counts)
        print("sbuf bytes remaining", nc.sbuf_bytes_remaining, "ops", {k: len(v) for k, v in S.ops.items()})
        S.emit()
    nc._dbg_names = dbg_names
    return nc


def _blkA(W):
    return W.reshape(8, 128, 512).transpose(1, 0, 2).reshape(128, 4096)


def _blkB(W):
    return W.reshape(22, 128, 128).transpose(1, 0, 2).reshape(128, 2816)


def _prep_shared(w_ada, b_ada, norm_gain, ffn1_w13, ffn1_w2, w_in, hg_lb_logits, hg_gnorm, gm_ln_gain,
                 gm_w_spatial, gm_b_spatial, gm_out_gain, w_out, ffn2_w13, ffn2_w2, final_gain):
    f = lambda a: np.asarray(a, dtype=np.float32)
    wA = np.empty((DEPTH, 30, 128, 4096), np.float32)
    wB = np.empty((DEPTH, 16, 128, 2816), np.float32)
    wada = np.empty((DEPTH, 18, 128, 4096), np.float32)
    for l in range(DEPTH):
        for fi, (w13, w2) in enumerate(((f(ffn1_w13)[l], f(ffn1_w2)[l]), (f(ffn2_w13)[l], f(ffn2_w2)[l]))):
            for i in range(11):
                blk = np.concatenate([w13[:, 256 * i:256 * (i + 1)], w13[:, DFF + 256 * i:DFF + 256 * (i + 1)]], axis=1)
                wA[l, fi * 11 + i] = _blkA(blk)
            for i in range(8):
                wB[l, fi * 8 + i] = _blkB(w2[:, 128 * i:128 * (i + 1)])
        for i in range(6):
            wA[l, 22 + i] = _blkA(f(w_in)[l][:, 512 * i:512 * (i + 1)])
        for i in range(2):
            wA[l, 28 + i] = _blkA(f(w_out)[l][:, 512 * i:512 * (i + 1)])
        for i in range(18):
            wada[l, i] = _blkA(f(w_ada)[l][:, 512 * i:512 * (i + 1)])
    vecs = np.zeros((128, NV), np.float32)
    for l in range(DEPTH):
        vecs[:, C_BADA + l * 72:C_BADA + (l + 1) * 72] = f(b_ada)[l].reshape(72, 128).T
        for sub in range(3):
            c0 = C_NG + (l * 3 + sub) * 8
            vecs[:, c0:c0 + 8] = f(norm_gain)[l, sub].reshape(8, 128).T
        vecs[:, C_LB + l * 4:C_LB + l * 4 + 4] = f(hg_lb_logits)[l].reshape(4, 128).T
        vecs[:, C_GN + l * 4:C_GN + l * 4 + 4] = f(hg_gnorm)[l].reshape(4, 128).T
        vecs[:, C_LNG + l * 4:C_LNG + l * 4 + 4] = f(gm_ln_gain)[l].reshape(4, 128).T
        vecs[:, C_OG + l * 4:C_OG + l * 4 + 4] = f(gm_out_gain)[l].reshape(4, 128).T
    vecs[:, C_FG:C_FG + 8] = f(final_gain).reshape(8, 128).T
    wspT = np.ascontiguousarray(f(gm_w_spatial).transpose(0, 3, 1, 2).reshape(DEPTH, 128, 512))
    bspv = np.ascontiguousarray(f(gm_b_spatial).reshape(DEPTH, 512))
    return wA, wB, wada, vecs, wspT, bspv


def kernel(x, c, w_ada, b_ada, norm_gain, ffn1_w13, ffn1_w2, w_in, hg_lb_logits, hg_gnorm, gm_ln_gain,
           gm_w_spatial, gm_b_spatial, gm_out_gain, w_out, ffn2_w13, ffn2_w2, final_gain, _nt=SEQ // T,
           _cores=NCORES, _dbg=None):
    x = np.asarray(x, dtype=np.float32)
    c = np.asarray(c, dtype=np.float32)
    wA, wB, wada, vecs, wspT, bspv = _prep_shared(
        w_ada, b_ada, norm_gain, ffn1_w13, ffn1_w2, w_in, hg_lb_logits, hg_gnorm, gm_ln_gain, gm_w_spatial,
        gm_b_spatial, gm_out_gain, w_out, ffn2_w13, ffn2_w2, final_gain)
    in_maps = []
    for core in range(_cores):
        xs = x[core * BPC:(core + 1) * BPC]
        xTc = np.ascontiguousarray(xs.transpose(0, 2, 1))
        v = vecs.copy()
        cs = c[core * BPC:(core + 1) * BPC]
        v[:, C_C:C_C + 16] = cs.reshape(BPC, 8, 128).transpose(2, 1, 0).reshape(128, 16)
        in_maps.append({"xT": xTc, "wA": wA, "wB": wB, "wada": wada, "vecsT": v, "wspT": wspT, "bsp": bspv})
    nc = build_nc(_nt, dbg=_dbg is not None)
    res = run_bass_kernel_spmd(nc, in_maps, core_ids=list(range(_cores)))
    out = np.empty((NCORES * BPC, SEQ, D), np.float32)
    if _dbg is not None:
        for k in nc._dbg_names:
            _dbg[k] = np.asarray(res.results[0][k])
    for core in range(_cores):
        out[core * BPC:(core + 1) * BPC] = res.results[core]["outT"].transpose(0, 2, 1)
    return out
```

```python
from contextlib import ExitStack
import numpy as np
import concourse.bass as bass
import concourse.mybir as mybir
from concourse.bass_utils import run_bass_kernel_spmd

F32 = mybir.dt.float32
BF16 = mybir.dt.bfloat16
AF = mybir.ActivationFunctionType
ALU = mybir.AluOpType
AX = mybir.AxisListType

D = 1024
SEQ = 2048
DEPTH = 2
DFF = 2816
T = 512
NCORES = 8
BPC = 2
RMS_EPS = 1e-6
LN_EPS = 1e-5
NSLOT = 4

C_BADA = 0
C_NG = 144
C_LB = 192
C_GN = 200
C_LNG = 208
C_OG = 216
C_FG = 224
C_C = 232
NV = 256


class Sched:
    ENGS = ("pe", "act", "dve", "pool", "sp")

    def __init__(self, nc, es, dry=False):
        self.nc = nc
        self.es = es
        self.dry = dry
        self.ops = {e: [] for e in self.ENGS}
        self.sem = {}
        self.cnt = {}
        self.seen = {e: {} for e in self.ENGS}
        self.bufs = {}
        for e in self.ENGS:
            self._mk("E_" + e)

    def _mk(self, name):
        if name not in self.cnt:
            if not self.dry:
                self.sem[name] = self.es.enter_context(self.nc.semaphore(name))
            self.cnt[name] = 0
        return name

    def op(self, eng, fn, reads=(), writes=(), dma=None):
        if self.dry:
            return None
        deps = {}
        for k in reads:
            b = self.bufs.get(k)
            if b and b["w"]:
                s, v = b["w"]
                deps[s] = max(deps.get(s, 0), v)
        for k in writes:
            b = self.bufs.get(k)
            if b:
                if b["w"]:
                    s, v = b["w"]
                    deps[s] = max(deps.get(s, 0), v)
                for s, v in b["r"]:
                    deps[s] = max(deps.get(s, 0), v)
        own = "E_" + eng
        waits = []
        for s, v in deps.items():
            if s == own and eng == "pe":
                continue
            if self.seen[eng].get(s, 0) >= v:
                continue
            self.seen[eng][s] = v
            waits.append((s, v))
        if dma is None:
            sname, inc = own, 1
        else:
            sname, inc = self._mk("D_" + dma), 16
        self.cnt[sname] += inc
        tok = (sname, self.cnt[sname])
        self.ops[eng].append((waits, fn, sname, inc))
        for k in reads:
            b = self.bufs.setdefault(k, {"w": None, "r": []})
            b["r"].append(tok)
        for k in writes:
            b = self.bufs.setdefault(k, {"w": None, "r": []})
            b["w"] = tok
            b["r"] = []
        return tok

    def emit(self):
        S = self
        with self.nc.Block() as block:
            def run(name):
                def body(e):
                    for waits, fn, sname, inc in S.ops[name]:
                        for s, v in waits:
                            e.wait_ge(S.sem[s], v)
                        if fn is None:
                            e.nop().then_inc(S.sem[sname], inc)
                            continue
                        fn(e).then_inc(S.sem[sname], inc)
                return body
            block.tensor(run("pe"))
            block.scalar(run("act"))
            block.vector(run("dve"))
            block.gpsimd(run("pool"))
            block.sync(run("sp"))


def _program(nc, es, S, nt, wseq_known, dbg, D_, counts=None):
    xT, wA, wB, wada, vecsT, wspT, bsp, outT, wsA, wsB = D_
    op = S.op
    dry = S.dry

    def sb(name, shape, dt=F32):
        return es.enter_context(nc.sbuf_tensor(name, shape, dt))

    if not dry:
        hT = [sb(f"hT{i}", [128, 8, T]) for i in range(2)]
        yTf = sb("yTf", [128, 8, T], BF16)
        yTm = sb("yTm", [128, 8, T], BF16)
        gT = sb("gT", [128, 22, T], BF16)
        ssq = [sb(f"ssq{i}", [128, T]) for i in range(2)]
        ssqb = [sb(f"ssqb{i}", [128, T], BF16) for i in range(2)]
        tmpsq = sb("tmpsq", [128, T])
        slots = [sb(f"slot{i}", [128, 4096], BF16) for i in range(NSLOT)]
        tmpf = [sb(f"tmpf{i}", [128, T]) for i in range(2)]
        rstd = sb("rstd", [128, T])
        sact = [sb(f"sact{i}", [128, T]) for i in range(2)]
        vec = sb("vec", [128, NV])
        cact = sb("cact", [128, 16])
        cactb = sb("cactb", [128, 16], BF16)
        modT = sb("modT", [128, DEPTH, 72, 2])
        modA = sb("modA", [128, DEPTH, 3, 2, 8])
        modG = sb("modG", [128, DEPTH, 3, 2, 8])
        lbt = sb("lbt", [128, DEPTH, 4])
        omlt = sb("omlt", [128, DEPTH, 4])
        nomlt = sb("nomlt", [128, DEPTH, 4])
        lbd = sb("lbd", [128, 4])
        identb = sb("identb", [128, 128], BF16)
        identf = sb("identf", [128, 128])
        onesb = sb("onesb", [128, 128], BF16)
        cmask = sb("cmask", [128, 128])
        gmask = sb("gmask", [128, 128])
        rmask = sb("rmask", [128, T])
        WmT = sb("WmT", [128, DEPTH, 4, 128], BF16)
        bspb = sb("bspb", [128, DEPTH, 512])
        epsr = sb("epsr", [128, 1])
        epsl = sb("epsl", [128, 1])
        dummy = sb("dummyt", [128, 1])
        epraw = sb("epraw", [128, 4096], BF16)
        Ep = epraw[:, 0:2048].rearrange("p (h t) -> p h t", h=4)
        Ep2 = epraw[:, 2048:4096].rearrange("p (h t) -> p h t", h=4)
        epf = epraw[:].bitcast(F32)
        osb, o1, zz, z1 = epf[:, 0:512], epf[:, 512:1024], epf[:, 1024:1536], epf[:, 1536:2048]
        z2 = sb("z2", [128, 512])
        wspf = z2
        vg4 = epf[:, 0:2048].rearrange("p (a t) -> p a t", a=4)
        ktT = sb("ktT", [128, 4, T], BF16)
        kdT = sb("kdT", [128, 4, T], BF16)
        qtT = sb("qtT", [128, 4, T], BF16)
        qdT = sb("qdT", [128, 4, T], BF16)
        sgT = sb("sgT", [128, 4, T], BF16)
        ugT = sb("ugT", [128, 4, T], BF16)
        vi = sb("vi", [128, 4, 512], BF16)
        vhat = sb("vhat", [128, 4, 512], BF16)
        mixraw = sb("mixraw", [128, 8 * T], BF16)
        mixT = mixraw[:].rearrange("p (k t) -> p k t", k=8)
        mixf = mixraw[:].bitcast(F32)
        sig, logf, kk, bcs = mixf[:, 0:512], mixf[:, 512:1024], mixf[:, 1024:1536], mixf[:, 1536:2048]
        br = sb("br", [128, T])
        sigx = sb("sigx", [128, 3, T])
        Em = sb("Em", [128, T])
        er = sb("er", [128, 4, 8])
        wc = sb("wc", [128, 4, 8])
        dl = sb("dl", [128, 4, 8])
        Sst = sb("Sst", [128, DEPTH, 4, 128])
        Shat = sb("Shat", [128, 2, 4, 128], BF16)
        dst = sb("dst", [128, 4, 128])
        ktok = sb("ktok", [128, 4, 128], BF16)
        ATm = sb("ATm", [128, 4, 128], BF16)
        osq = sb("osq", [128, 512], BF16)
        zsq = sb("zsq", [128, 512], BF16)
        st1 = sb("st1", [128, 4, 4])
        st2 = sb("st2", [128, 4, 4])
        mean = sb("mean", [128, 4, 4])
        msq = sb("msq", [128, 4, 4])
        var = sb("var", [128, 4, 4])
        lrs = sb("lrs", [128, 4, 4])
        ps = [es.enter_context(nc.psum_tensor(f"ps{i}", [128, 512], F32)) for i in range(8)]

    def P(i):
        return ("ps", i)

    dbg_names = []

    wseq = [] if wseq_known is None else wseq_known
    wstate = {"issued": 0, "cur": 0, "seen": set()}

    def slot_key(i):
        return ("slot", i % NSLOT)

    def prefetch(upto):
        while wstate["issued"] <= min(upto, len(wseq) - 1):
            i = wstate["issued"]
            kind, l, blk = wseq[i]
            sl = slots[i % NSLOT]
            n = 2816 if kind == "B" else 4096
            src = {"A": wA, "B": wB, "ada": wada}[kind]
            scr = {"A": wsA, "B": wsB}.get(kind)
            bkey = (kind, l, blk)
            if scr is None or bkey not in wstate["seen"]:
                op("pool", lambda e, sl=sl, l=l, blk=blk, n=n, src=src: e.dma_start(out=sl[:, 0:n], in_=src[l, blk]),
                   writes=[slot_key(i)], dma=f"w{i % NSLOT}")
                if scr is not None:
                    wstate["seen"].add(bkey)
                    op("sp", lambda e, sl=sl, l=l, blk=blk, n=n, scr=scr: e.dma_start(out=scr[l, blk], in_=sl[:, 0:n]),
                       reads=[slot_key(i)], writes=[("scr",) + bkey], dma=f"sw{i % NSLOT}")
            else:
                op("sp", lambda e, sl=sl, l=l, blk=blk, n=n, scr=scr: e.dma_start(out=sl[:, 0:n], in_=scr[l, blk]),
                   reads=[("scr",) + bkey], writes=[slot_key(i)], dma=f"w{i % NSLOT}")
            wstate["issued"] += 1

    def next_w(spec):
        i = wstate["cur"]
        wstate["cur"] += 1
        if dry:
            wseq.append(spec)
            return None, None
        assert wseq[i] == spec, (wseq[i], spec)
        prefetch(i + NSLOT - 1)
        return slots[i % NSLOT], slot_key(i)

    if not dry:
        op("sp", lambda e: e.dma_start(out=vec[:], in_=vecsT), writes=["vec"], dma="vec")
        for l in range(DEPTH):
            op("sp", lambda e, l=l: e.dma_start(out=bspb[:, l, :], in_=bsp[l].partition_broadcast(128)),
               writes=[("bspb", l)], dma=f"bspb{l}")
        op("dve", lambda e: e.memset(epsr[:], RMS_EPS), writes=["epsr"])
        op("dve", lambda e: e.memset(epsl[:], LN_EPS), writes=["epsl"])
        op("pool", lambda e: e.memset(identf[:], 0.0), writes=["identf"])
        op("pool", lambda e: e.affine_select(out=identf[:], in_=identf[:], pattern=[[-1, 128]],
                                             compare_op=ALU.not_equal, fill=1.0, base=0, channel_multiplier=1),
           reads=["identf"], writes=["identf"])
        op("dve", lambda e: e.tensor_copy(out=identb[:], in_=identf[:]), reads=["identf"], writes=["identb"])
        op("dve", lambda e: e.memset(onesb[:], 1.0), writes=["onesb"])
        op("pool", lambda e: e.memset(cmask[:], 1.0), writes=["cmask"])
        op("pool", lambda e: e.affine_select(out=cmask[:], in_=cmask[:], pattern=[[1, 128]], compare_op=ALU.is_ge,
                                             fill=0.0, base=0, channel_multiplier=-1),
           reads=["cmask"], writes=["cmask"])
        for i_ in range(3):
            op("pool", lambda e, i_=i_: e.memset(cmask[i_ * 32:(i_ + 1) * 32, (i_ + 1) * 32:128], 0.0),
               reads=["cmask"], writes=["cmask"])
        op("pool", lambda e: e.memset(gmask[:], 1.0), writes=["gmask"])
        op("pool", lambda e: e.memset(gmask[64:128, 0:64], 0.0), reads=["gmask"], writes=["gmask"])
        op("dve", lambda e: e.memset(rmask[:], 1.0), writes=["rmask"])
        op("dve", lambda e: e.memset(rmask[:].rearrange("p (c t) -> p c t", t=64)[:, :, 0:1], 0.0),
           reads=["rmask"], writes=["rmask"])
        for l in range(DEPTH):
            op("sp", lambda e, l=l: e.dma_start(out=wspf[:], in_=wspT[l]), writes=["z2"], dma="wsp")
            op("dve", lambda e, l=l: e.tensor_tensor(
                out=WmT[:, l, :, :], in0=wspf[:].rearrange("p (h t) -> p h t", h=4),
                in1=gmask[:].unsqueeze(1).to_broadcast([128, 4, 128]), op=ALU.mult),
               reads=["z2", "gmask"], writes=[("WmT", l)])
        op("dve", lambda e: e.memset(lbt[:, 0, :], 0.0), writes=["lb0"])
        op("dve", lambda e: e.memset(omlt[:, 0, :], 1.0), writes=["oml0"])
        op("dve", lambda e: e.memset(nomlt[:, 0, :], -1.0), writes=["noml0"])
        op("dve", lambda e: e.tensor_tensor(out=lbd[:], in0=vec[:, C_LB + 4:C_LB + 8], in1=vec[:, C_LB:C_LB + 4],
                                            op=ALU.subtract), reads=["vec"], writes=["lbd"])
        op("act", lambda e: e.activation(out=lbt[:, 1, :], in_=lbd[:], func=AF.Sigmoid), reads=["lbd"], writes=["lb1"])
        op("act", lambda e: e.activation(out=omlt[:, 1, :], in_=lbd[:], func=AF.Sigmoid, scale=-1.0),
           reads=["lbd"], writes=["oml1"])
        op("dve", lambda e: e.tensor_scalar(out=nomlt[:, 1, :], in0=omlt[:, 1, :], scalar1=-1.0, scalar2=None,
                                            op0=ALU.mult), reads=["oml1"], writes=["noml1"])
        op("act", lambda e: e.activation(out=cact[:], in_=vec[:, C_C:C_C + 16], func=AF.Silu),
           reads=["vec"], writes=["cact"])
        op("dve", lambda e: e.tensor_copy(out=cactb[:], in_=cact[:]), reads=["cact"], writes=["cactb"])

    def dump(name, ap, keys):
        if not dbg or dry:
            return
        shp = [int(x) for x in ap.shape]
        dt_ = nc.dram_tensor("dbg_" + name, shp, ap.dtype, kind="ExternalOutput").ap()
        dbg_names.append("dbg_" + name)
        op("sp", lambda e: e.dma_start(out=dt_, in_=ap), reads=keys, writes=[("dbg", name)], dma="dbg_" + name)
        op("sp", None, reads=[("dbg", name)])

    def adaln(l):
        def mm_blk(cb, sl, sk):
            slv = sl[:, 0:4096].rearrange("p (k n) -> p k n", k=8)
            r = cb % 2

            def mm(e):
                for k in range(8):
                    ins = e.matmul(ps[r][0:2, :], lhsT=cactb[:, 2 * k:2 * k + 2], rhs=slv[:, k, :],
                                   start=(k == 0), stop=(k == 7))
                return ins
            op("pe", mm, reads=[sk, "cactb"], writes=[P(r)])
            op("act", lambda e: e.copy(out=sact[r][0:2, :], in_=ps[r][0:2, :]), reads=[P(r)], writes=[("sact", r)])

        def tr_blk(cb):
            r = cb % 2

            def tr(e):
                for q in range(4):
                    col = (cb * 4 + q) * 2
                    ins = e.transpose(out=ps[2][:, col:col + 2], in_=sact[r][0:2, q * 128:(q + 1) * 128],
                                      identity=identf[0:2, 0:2])
                return ins
            op("pe", tr, reads=[("sact", r), "identf"], writes=[P(2)])

        for cb in range(18):
            sl, sk = next_w(("ada", l, cb))
            if not dry:
                mm_blk(cb, sl, sk)
                if cb > 0:
                    tr_blk(cb - 1)
            if cb % 2 == 1:
                yield None
        if not dry:
            tr_blk(17)
        if dry:
            return
        pm = ps[2]
        op("dve", lambda e: e.tensor_tensor(
            out=modT[:, l, :, :], in0=pm[:, 0:144].rearrange("p (c b) -> p c b", b=2),
            in1=vec[:, C_BADA + l * 72:C_BADA + (l + 1) * 72].unsqueeze(2).to_broadcast([128, 72, 2]),
            op=ALU.add), reads=[P(2), "vec"], writes=[("modT", l)])
        for sub in range(3):
            for b in range(BPC):
                ng = vec[:, C_NG + (l * 3 + sub) * 8:C_NG + (l * 3 + sub) * 8 + 8]
                op("dve", lambda e, sub=sub, b=b, ng=ng: e.scalar_tensor_tensor(
                    out=modA[:, l, sub, b, :], in0=modT[:, l, (3 * sub + 1) * 8:(3 * sub + 2) * 8, b], scalar=1.0,
                    in1=ng, op0=ALU.add, op1=ALU.mult), reads=[("modT", l), "vec"], writes=[("modA", l, sub, b)])
                gs = 1.0 if sub == 1 else 0.5
                op("dve", lambda e, sub=sub, b=b, gs=gs: e.tensor_scalar(
                    out=modG[:, l, sub, b, :], in0=modT[:, l, (3 * sub + 2) * 8:(3 * sub + 3) * 8, b], scalar1=gs,
                    scalar2=None, op0=ALU.mult), reads=[("modT", l)], writes=[("modG", l, sub, b)])

    def rsqrt_psum(dst_t, dkey, pb, scale):
        op("act", lambda e: e.activation(out=dst_t[:], in_=ps[pb][:], func=AF.Ln, scale=scale, bias=epsr[:]),
           reads=[P(pb), "epsr"], writes=[dkey])
        op("act", lambda e: e.activation(out=dst_t[:], in_=dst_t[:], func=AF.Exp, scale=-0.5),
           reads=[dkey], writes=[dkey])

    def sq_accum(h, hkeys, si, dc):
        if dc == 0:
            op("dve", lambda e: e.tensor_tensor(out=ssq[si][:], in0=h[:, 0, :], in1=h[:, 0, :], op=ALU.mult),
               reads=[hkeys[0]], writes=[("ssq", si)])
            return
        op("dve", lambda e: e.tensor_tensor(out=tmpsq[:], in0=h[:, dc, :], in1=h[:, dc, :], op=ALU.mult),
           reads=[hkeys[dc]], writes=["tmpsq"])
        if dc < 7:
            op("dve", lambda e: e.tensor_tensor(out=ssq[si][:], in0=ssq[si][:], in1=tmpsq[:], op=ALU.add),
               reads=[("ssq", si), "tmpsq"], writes=[("ssq", si)])
        else:
            op("dve", lambda e: e.tensor_tensor(out=ssqb[si][:], in0=ssq[si][:], in1=tmpsq[:], op=ALU.add),
               reads=[("ssq", si), "tmpsq"], writes=[("ssqb", si)])

    def rms_squares(h, hkeys, si):
        for dc in range(8):
            sq_accum(h, hkeys, si, dc)

    def rms_rstd(si, pb):
        op("pe", lambda e: e.matmul(ps[pb][:], lhsT=onesb[:], rhs=ssqb[si][:], start=True, stop=True),
           reads=[("ssqb", si), "onesb"], writes=[P(pb)])
        rsqrt_psum(rstd, "rstd", pb, 1.0 / D)

    def norm_mod(h, hkeys, y, ykeys, l, sub, b, pb):
        for dc in range(8):
            tf = tmpf[dc % 2]
            op("dve", lambda e, dc=dc, tf=tf: e.scalar_tensor_tensor(
                out=tf[:], in0=h[:, dc, :], scalar=modA[:, l, sub, b, dc:dc + 1], in1=rstd[:],
                op0=ALU.mult, op1=ALU.mult),
               reads=[hkeys[dc], "rstd", ("modA", l, sub, b)], writes=[("tmpf", dc % 2)])
            op("act", lambda e, dc=dc, tf=tf: e.activation(
                out=y[:, dc, :], in_=tf[:], func=AF.Identity,
                bias=modT[:, l, 3 * sub * 8 + dc, b:b + 1], scale=1.0),
               reads=[("tmpf", dc % 2), ("modT", l)], writes=[ykeys[dc]])

    ykf = [("yf", dc) for dc in range(8)]
    ykm = [("ym", dc) for dc in range(8)]
    pctr = {"ab": 0, "o": 0, "mo": 0}

    fy_done = [False, False]

    def ffn_norm(h, hkeys, l, f, b, si, need_sq=False, pb=0, cond=None):
        sub = 0 if f == 0 else 2
        yield ("acq", ("N", "Fy"), cond)
        if not dry and need_sq:
            rms_squares(h, hkeys, si)
        yield None
        if not dry:
            rms_rstd(si, pb)
        yield None
        if not dry:
            norm_mod(h, hkeys, yTf, ykf, l, sub, b, pb)
        yield ("rel", ("N",))

    def ffn_body(h, hkeys, l, f, b, si, last_in_sp):
        sub = 0 if f == 0 else 2
        yield ("acq", ("Fg",), None)
        for blk in range(11):
            sl, sk = next_w(("A", l, f * 11 + blk))
            if not dry:
                slv = sl[:, 0:4096].rearrange("p (k n) -> p k n", k=8)
                for ch in range(2):
                    r = pctr["ab"] % 2
                    pctr["ab"] += 1
                    pa, pb = r, 2 + r
                    hc = blk * 2 + ch

                    def mm(e, slv=slv, ch=ch, pa=pa, pb=pb):
                        for k in range(8):
                            e.matmul(ps[pa][:], lhsT=slv[:, k, ch * 128:(ch + 1) * 128], rhs=yTf[:, k, :],
                                     start=(k == 0), stop=(k == 7))
                        for k in range(8):
                            ins = e.matmul(ps[pb][:], lhsT=slv[:, k, 256 + ch * 128:256 + (ch + 1) * 128],
                                           rhs=yTf[:, k, :], start=(k == 0), stop=(k == 7))
                        return ins
                    op("pe", mm, reads=[sk] + ykf, writes=[P(pa), P(pb)])
                    op("act", lambda e, pa=pa, r=r: e.activation(out=sact[r][:], in_=ps[pa][:], func=AF.Silu),
                       reads=[P(pa)], writes=[("sact", r)])
                    op("dve", lambda e, pb=pb, r=r, hc=hc: e.tensor_tensor(out=gT[:, hc, :], in0=sact[r][:],
                                                                           in1=ps[pb][:], op=ALU.mult),
                       reads=[("sact", r), P(pb)], writes=[("g", hc)])
            yield None
        yield ("rel", ("Fy",))
        if last_in_sp:
            fy_done[si] = True
        gkeys = [("g", i) for i in range(22)]
        for dc in range(8):
            sl, sk = next_w(("B", l, f * 8 + dc))
            if not dry:
                slv = sl[:, 0:2816].rearrange("p (k n) -> p k n", k=22)
                po = 2 + pctr["o"] % 2
                pctr["o"] += 1

                def mm(e, slv=slv, po=po):
                    for k in range(22):
                        ins = e.matmul(ps[po][:], lhsT=slv[:, k, :], rhs=gT[:, k, :], start=(k == 0), stop=(k == 21))
                    return ins
                op("pe", mm, reads=[sk] + gkeys, writes=[P(po)])
                op("dve", lambda e, dc=dc, po=po: e.scalar_tensor_tensor(
                    out=h[:, dc, :], in0=ps[po][:], scalar=modG[:, l, sub, b, dc:dc + 1], in1=h[:, dc, :],
                    op0=ALU.mult, op1=ALU.add),
                   reads=[P(po), ("modG", l, sub, b), hkeys[dc]], writes=[hkeys[dc]])
                sq_accum(h, hkeys, si, dc)
            if dc % 2 == 1:
                yield None
        yield ("rel", ("Fg",))

    def c3(ap):
        return ap.rearrange("p (c t) -> p c t", t=64)

    def c32(ap):
        return ap.rearrange("p (c t) -> p c t", t=32)

    def h4(ap):
        return ap.rearrange("p (h t) -> p h t", h=4)

    mdone = [0] * DEPTH

    def mixer_norm(h, hkeys, l, b, si):
        yield ("acq", ("N", "Y"), None)
        if not dry:
            rms_rstd(si, 0)
        yield None
        if not dry:
            norm_mod(h, hkeys, yTm, ykm, l, 1, b, 0)
        yield ("rel", ("N",))

    def mixer(h, hkeys, l, b, j, ti, si):
        yield ("acq", ("M",), (lambda: mdone[l] == ti))
        if j == 0 and not dry:
            op("dve", lambda e: e.memset(Sst[:, l, :, :], 0.0), writes=[("S", l)])

        def proj_fm(slv, hd, pb):
            def mm(e):
                for k in range(8):
                    ins = e.matmul(ps[pb][:], lhsT=slv[:, k, hd * 128:(hd + 1) * 128], rhs=yTm[:, k, :],
                                   start=(k == 0), stop=(k == 7))
                return ins
            return mm

        def proj_tm(slv, s, pb):
            def mm(e):
                for k in range(8):
                    ins = e.matmul(ps[pb][:], lhsT=yTm[:, k, s * 128:(s + 1) * 128], rhs=slv[:, k, :],
                                   start=(k == 0), stop=(k == 7))
                return ins
            return mm

        A_ = lambda sl: sl[:, 0:4096].rearrange("p (k n) -> p k n", k=8)
        sigs = None if dry else [(sig, "sig0"), (sigx[:, 0, :], "sig1"), (sigx[:, 1, :], "sig2"), (sigx[:, 2, :], "sig3")]
        sl, sk = next_w(("A", l, 23))
        if not dry:
            slv = A_(sl)
            for hd in range(4):
                op("pe", proj_fm(slv, hd, 4 + hd), reads=[sk] + ykm, writes=[P(4 + hd)])
            for hd in range(4):
                sb_, skey = sigs[hd]
                op("act", lambda e, hd=hd, sb_=sb_: e.activation(out=sb_, in_=ps[4 + hd][:], func=AF.Sigmoid),
                   reads=[P(4 + hd)], writes=[skey])
        yield None
        sl, sk = next_w(("A", l, 25))
        if not dry:
            slv = A_(sl)
            for hd in range(4):
                op("pe", proj_fm(slv, hd, 4 + hd), reads=[sk] + ykm, writes=[P(4 + hd)])
            for hd in range(4):
                tb, tk = (br, "br") if hd % 2 == 0 else (Em, "Em")
                op("act", lambda e, hd=hd, tb=tb: e.activation(out=tb[:], in_=ps[4 + hd][:], func=AF.Silu),
                   reads=[P(4 + hd)], writes=[tk])
                op("dve", lambda e, hd=hd, tb=tb: e.tensor_scalar(
                    out=sgT[:, hd, :], in0=tb[:], scalar1=vec[:, C_GN + l * 4 + hd:C_GN + l * 4 + hd + 1],
                    scalar2=None, op0=ALU.mult), reads=[tk, "vec"], writes=[("sg", hd)])
        yield None
        sl, sk = next_w(("A", l, 26))
        if not dry:
            slv = A_(sl)
            for hd in range(4):
                op("pe", proj_fm(slv, hd, 4 + hd), reads=[sk] + ykm, writes=[P(4 + hd)])
            for hd in range(4):
                op("act", lambda e, hd=hd: e.activation(out=ugT[:, hd, :], in_=ps[4 + hd][:],
                                                        func=AF.Gelu_apprx_tanh),
                   reads=[P(4 + hd)], writes=[("ug", hd)])
        yield None
        sl, sk = next_w(("A", l, 24))
        if not dry:
            slv = A_(sl)
            for s in range(4):
                op("pe", proj_tm(slv, s, 4 + s), reads=[sk] + ykm, writes=[P(4 + s)])
            for s in range(4):
                op("act", lambda e, s=s: e.copy(out=vi[:, s, :], in_=ps[4 + s][:]), reads=[P(4 + s)],
                   writes=[("vi", s)])
        yield None
        sl, sk = next_w(("A", l, 27))
        vgk4 = ["vg0", "vg1", "vg2", "vg3"]
        if not dry:
            slv = A_(sl)
            for s in range(4):
                op("pe", proj_tm(slv, s, 4 + s), reads=[sk] + ykm, writes=[P(4 + s)])
            for s in range(4):
                op("act", lambda e, s=s: e.activation(out=vg4[:, s, :], in_=ps[4 + s][:], func=AF.Gelu_apprx_tanh),
                   reads=[P(4 + s)], writes=[vgk4[s]])
            for s in range(4):
                op("dve", lambda e, s=s: e.tensor_reduce(out=st1[:, s, :], in_=h4(vg4[:, s, :]), axis=AX.X, op=ALU.add),
                   reads=[vgk4[s]], writes=["st1"])
                op("dve", lambda e, s=s: e.tensor_tensor(out=z2[:], in0=vg4[:, s, :], in1=vg4[:, s, :], op=ALU.mult),
                   reads=[vgk4[s]], writes=["z2"])
                op("dve", lambda e, s=s: e.tensor_reduce(out=st2[:, s, :], in_=h4(z2[:]), axis=AX.X, op=ALU.add),
                   reads=["z2"], writes=["st2"])
            op("dve", lambda e: e.tensor_scalar(out=mean[:], in0=st1[:], scalar1=1.0 / 128, scalar2=None,
                                                op0=ALU.mult), reads=["st1"], writes=["mean"])
            op("dve", lambda e: e.tensor_tensor(out=msq[:], in0=mean[:], in1=mean[:], op=ALU.mult),
               reads=["mean"], writes=["msq"])
            op("dve", lambda e: e.scalar_tensor_tensor(out=var[:], in0=st2[:], scalar=1.0 / 128, in1=msq[:],
                                                       op0=ALU.mult, op1=ALU.subtract),
               reads=["st2", "msq"], writes=["var"])
            op("act", lambda e: e.activation(out=lrs[:], in_=var[:], func=AF.Ln, scale=1.0, bias=epsl[:]),
               reads=["var", "epsl"], writes=["lrs"])
            op("act", lambda e: e.activation(out=lrs[:], in_=lrs[:], func=AF.Exp, scale=-0.5),
               reads=["lrs"], writes=["lrs"])
        yield None
        if not dry:
            for s in range(4):
                for hd in range(4):
                    op("dve", lambda e, s=s, hd=hd: e.tensor_scalar(
                        out=vhat[:, s, hd * 128:(hd + 1) * 128], in0=vg4[:, s, hd * 128:(hd + 1) * 128],
                        scalar1=mean[:, s, hd:hd + 1], scalar2=lrs[:, s, hd:hd + 1], op0=ALU.subtract, op1=ALU.mult),
                       reads=[vgk4[s], "mean", "lrs"], writes=[("vhat", s, hd)])
        yield None
        for hd in range(4):
            if not dry:
                sb_, skey = sigs[hd]
                op("act", lambda e, hd=hd, sb_=sb_: e.activation(out=logf[:], in_=sb_, func=AF.Ln,
                                                                  scale=omlt[:, l, hd:hd + 1], bias=lbt[:, l, hd:hd + 1]),
                   reads=[skey, f"oml{l}", f"lb{l}"], writes=["logf"])
                op("dve", lambda e, hd=hd, sb_=sb_: e.tensor_scalar(out=kk[:], in0=sb_, scalar1=nomlt[:, l, hd:hd + 1],
                                                                    scalar2=omlt[:, l, hd:hd + 1], op0=ALU.mult, op1=ALU.add),
                   reads=[skey, f"oml{l}", f"noml{l}"], writes=["kk"])
                op("dve", lambda e: e.tensor_tensor_scan(out=bcs[:], data0=rmask[:], data1=logf[:], initial=0.0,
                                                         op0=ALU.mult, op1=ALU.add),
                   reads=["logf", "rmask"], writes=["bcs"])
                op("dve", lambda e: e.tensor_tensor(out=c3(br[:]), in0=c3(bcs[:]),
                                                    in1=c3(bcs[:])[:, :, 31:32].to_broadcast([128, 8, 64]),
                                                    op=ALU.subtract), reads=["bcs"], writes=["br"])
                op("act", lambda e, hd=hd: e.activation(out=Ep[:, hd, :], in_=br[:], func=AF.Exp),
                   reads=["br"], writes=[("Ep", hd), "vg0", "vg1", "vg2", "vg3"])
                op("act", lambda e: e.activation(out=Em[:], in_=br[:], func=AF.Exp, scale=-1.0),
                   reads=["br"], writes=["Em"])
                op("act", lambda e, hd=hd: e.activation(out=dl[:, hd, :], in_=c3(br[:])[:, :, 63], func=AF.Exp),
                   reads=["br"], writes=[("dl", hd)])
                op("act", lambda e, hd=hd: e.activation(out=er[:, hd, :], in_=c3(bcs[:])[:, :, 31], func=AF.Exp),
                   reads=["bcs"], writes=[("er", hd)])
                op("act", lambda e, hd=hd: e.activation(out=wc[:, hd, :], in_=c3(bcs[:])[:, :, 63], func=AF.Exp),
                   reads=["bcs"], writes=[("wc", hd)])
                op("dve", lambda e, hd=hd: e.tensor_tensor(out=ktT[:, hd, :], in0=kk[:], in1=Em[:], op=ALU.mult),
                   reads=["kk", "Em"], writes=[("kt", hd)])
                op("dve", lambda e: e.tensor_tensor(out=c32(br[:]), in0=c32(bcs[:]),
                                                    in1=c32(bcs[:])[:, :, 15:16].to_broadcast([128, 16, 32]),
                                                    op=ALU.subtract), reads=["bcs"], writes=["br"])
                op("act", lambda e, hd=hd: e.activation(out=Ep2[:, hd, :], in_=br[:], func=AF.Exp),
                   reads=["br"], writes=[("Ep2", hd), "vg0", "vg1", "vg2", "vg3"])
                op("act", lambda e: e.activation(out=Em[:], in_=br[:], func=AF.Exp, scale=-1.0),
                   reads=["br"], writes=["Em"])
                op("dve", lambda e, hd=hd: e.tensor_tensor(out=kdT[:, hd, :], in0=kk[:], in1=Em[:], op=ALU.mult),
                   reads=["kk", "Em"], writes=[("kd", hd)])
            yield None
        sl, sk = next_w(("A", l, 22))
        if not dry:
            slv = A_(sl)
            for hd in range(4):
                op("pe", proj_fm(slv, hd, 4 + hd), reads=[sk] + ykm, writes=[P(4 + hd)])
            for hd in range(4):
                pb = 4 + hd
                op("dve", lambda e, hd=hd, pb=pb: e.tensor_tensor(out=qtT[:, hd, :], in0=ps[pb][:], in1=Ep[:, hd, :],
                                                                  op=ALU.mult),
                   reads=[P(pb), ("Ep", hd)], writes=[("qt", hd)])
                op("dve", lambda e, hd=hd, pb=pb: e.tensor_tensor(out=qdT[:, hd, :], in0=ps[pb][:], in1=Ep2[:, hd, :],
                                                                  op=ALU.mult),
                   reads=[P(pb), ("Ep2", hd)], writes=[("qd", hd)])
        yield ("rel", ("Y",))
        if not dry:
            op("dve", lambda e: e.memset(dummy[:], 0.0),
               reads=[("Ep", hd) for hd in range(4)] + [("Ep2", hd) for hd in range(4)],
               writes=["osb", "o1", "zz", "z1", "dummy"])

        def subtile_steps(s):
            ts = slice(s * 128, (s + 1) * 128)
            kts = [("kt", hd) for hd in range(4)]
            qts = [("qt", hd) for hd in range(4)]
            ptr = ps[4][:].bitcast(BF16)[:, 0:512].rearrange("p (h k) -> p h k", h=4)
            pat = h4(ps[5][:])
            pat2 = h4(ps[6][:])
            pdb = (4, 7)
            po = h4(ps[6][:])
            pmm = h4(ps[7][:])

            def st_a():
                def tr(e):
                    for hd in range(4):
                        ins = e.transpose(out=ptr[:, hd, :], in_=ktT[:, hd, ts], identity=identb[:])
                    return ins
                op("pe", tr, reads=kts + ["identb"], writes=[P(4)])

                def at(e):
                    for hd in range(4):
                        ins = e.matmul(pat[:, hd, :], lhsT=kdT[:, hd, ts], rhs=qdT[:, hd, ts], start=True, stop=True)
                    return ins
                op("pe", at, reads=[("kd", hd) for hd in range(4)] + [("qd", hd) for hd in range(4)], writes=[P(5)])

                def at2(e):
                    for hd in range(4):
                        for cc in range(2):
                            r0 = cc * 64
                            t0 = s * 128 + cc * 64
                            ins = e.matmul(pat2[r0:r0 + 32, hd, r0 + 32:r0 + 64], lhsT=ktT[:, hd, t0:t0 + 32],
                                           rhs=qtT[:, hd, t0 + 32:t0 + 64], start=True, stop=True)
                    return ins
                op("pe", at2, reads=kts + qts, writes=[P(6)])

                def gm(e):
                    for hd in range(4):
                        ins = e.matmul(pmm[:, hd, :], lhsT=vhat[:, s, hd * 128:(hd + 1) * 128], rhs=WmT[:, l, hd, :],
                                       start=True, stop=True)
                    return ins
                op("pe", gm, reads=[("vhat", s, hd) for hd in range(4)] + [("WmT", l)], writes=[P(7)])
                op("act", lambda e: e.copy(out=ktok[:], in_=ptr), reads=[P(4)], writes=["ktok"])
                op("dve", lambda e: e.tensor_tensor(out=ATm[:], in0=pat,
                                                    in1=cmask[:].unsqueeze(1).to_broadcast([128, 4, 128]),
                                                    op=ALU.mult), reads=[P(5), "cmask"], writes=["ATm"])
                for cc in range(2):
                    r0 = cc * 64
                    op("act", lambda e, r0=r0: e.copy(out=ATm[r0:r0 + 32, :, r0 + 32:r0 + 64],
                                                      in_=pat2[r0:r0 + 32, :, r0 + 32:r0 + 64]),
                       reads=[P(6)], writes=["ATm"])
                for hd in range(4):
                    op("dve", lambda e, hd=hd: e.scalar_tensor_tensor(
                        out=z1[:, hd * 128:(hd + 1) * 128], in0=pmm[:, hd, :],
                        scalar=vec[:, C_LNG + l * 4 + hd:C_LNG + l * 4 + hd + 1],
                        in1=bspb[:, l, hd * 128:(hd + 1) * 128], op0=ALU.mult, op1=ALU.add),
                       reads=[P(7), "vec", ("bspb", l)], writes=["z1"])
                op("dve", lambda e: e.tensor_tensor(out=h4(zz[:]), in0=h4(z1[:]), in1=ugT[:, :, ts], op=ALU.mult),
                   reads=["z1"] + [("ug", hd) for hd in range(4)], writes=["zz"])
                op("act", lambda e: e.activation(out=zsq[:], in_=zz[:], func=AF.Square), reads=["zz"], writes=["zsq"])

            def st_b():
                for cc in range(2):
                    pds = h4(ps[pdb[cc]][:])
                    rows = slice(cc * 64, (cc + 1) * 64)

                    def dsm(e, pds=pds, rows=rows):
                        for hd in range(4):
                            ins = e.matmul(pds[:, hd, :], lhsT=ktok[rows, hd, :], rhs=vi[rows, s, hd * 128:(hd + 1) * 128],
                                           start=True, stop=True)
                        return ins
                    op("pe", dsm, reads=["ktok", ("vi", s)], writes=[P(pdb[cc])])
                op("pe", lambda e: e.matmul(ps[5][:], lhsT=onesb[:], rhs=zsq[:], start=True, stop=True),
                   reads=["zsq", "onesb"], writes=[P(5)])
                for cc in range(2):
                    c = 2 * s + cc
                    pds = h4(ps[pdb[cc]][:])

                    def bc(t, c=c):
                        return t[:, :, c:c + 1].to_broadcast([128, 4, 128])
                    op("dve", lambda e, cc=cc, bc=bc: e.tensor_tensor(out=Shat[:, cc, :, :], in0=Sst[:, l, :, :],
                                                                      in1=bc(er), op=ALU.mult),
                       reads=[("S", l)] + [("er", hd) for hd in range(4)], writes=[("Shat", cc)])
                    op("dve", lambda e, pds=pds, bc=bc: e.tensor_tensor(out=dst[:], in0=pds, in1=bc(dl), op=ALU.mult),
                       reads=[P(pdb[cc])] + [("dl", hd) for hd in range(4)], writes=["dst"])
                    op("dve", lambda e, bc=bc: e.tensor_tensor(out=Sst[:, l, :, :], in0=Sst[:, l, :, :], in1=bc(wc),
                                                               op=ALU.mult),
                       reads=[("S", l)] + [("wc", hd) for hd in range(4)], writes=[("S", l)])
                    op("dve", lambda e: e.tensor_tensor(out=Sst[:, l, :, :], in0=Sst[:, l, :, :], in1=dst[:],
                                                        op=ALU.add), reads=[("S", l), "dst"], writes=[("S", l)])
            def st_b2():
                rsqrt_psum(z2, "z2", 5, 1.0 / 128)
                op("dve", lambda e: e.tensor_tensor(out=z2[:], in0=z2[:], in1=zz[:], op=ALU.mult),
                   reads=["z2", "zz"], writes=["z2"])
                for hd in range(4):
                    op("dve", lambda e, hd=hd: e.tensor_scalar(
                        out=mixT[:, 4 + hd, ts], in0=z2[:, hd * 128:(hd + 1) * 128],
                        scalar1=vec[:, C_OG + l * 4 + hd:C_OG + l * 4 + hd + 1], scalar2=None, op0=ALU.mult),
                       reads=["z2", "vec"], writes=[("mixg", s, hd)])

            def st_c():
                def om(e):
                    for hd in range(4):
                        e.matmul(po[:, hd, :], lhsT=vi[:, s, hd * 128:(hd + 1) * 128], rhs=ATm[:, hd, :],
                                 start=True, stop=False)
                        for cc in range(2):
                            tc = slice(s * 128 + cc * 64, s * 128 + (cc + 1) * 64)
                            ins = e.matmul(po[:, hd, cc * 64:(cc + 1) * 64], lhsT=Shat[:, cc, hd, :],
                                           rhs=qtT[:, hd, tc], start=False, stop=(cc == 1))
                    return ins
                op("pe", om, reads=[("vi", s), "ATm", ("Shat", 0), ("Shat", 1)] + qts, writes=[P(6)])
                op("act", lambda e: e.copy(out=osb[:], in_=ps[6][:]), reads=[P(6)], writes=["osb"])
                op("act", lambda e: e.activation(out=osq[:], in_=ps[6][:], func=AF.Square), reads=[P(6)],
                   writes=["osq"])

            def st_d():
                op("pe", lambda e: e.matmul(ps[5][:], lhsT=onesb[:], rhs=osq[:], start=True, stop=True),
                   reads=["osq", "onesb"], writes=[P(5)])
                rsqrt_psum(o1, "o1", 5, 1.0 / 128)
                op("dve", lambda e: e.tensor_tensor(out=o1[:], in0=o1[:], in1=osb[:], op=ALU.mult),
                   reads=["o1", "osb"], writes=["o1"])
                op("dve", lambda e: e.tensor_tensor(out=mixT[:, 0:4, ts], in0=h4(o1[:]), in1=sgT[:, :, ts],
                                                    op=ALU.mult),
                   reads=["o1"] + [("sg", hd) for hd in range(4)], writes=[("mix", s)])
            return [st_a, st_b, st_b2, st_c, st_d]

        for s in range(4):
            steps = [None] * 5 if dry else subtile_steps(s)
            for st in steps:
                if st is not None:
                    st()
                yield None
        mkeys = [("mix", s) for s in range(4)] + [("mixg", s, hd) for s in range(4) for hd in range(4)]
        for wb in range(2):
            sl, sk = next_w(("A", l, 28 + wb))
            if not dry:
                slv = sl[:, 0:4096].rearrange("p (k n) -> p k n", k=8)
                for dcl in range(4):
                    dc = wb * 4 + dcl
                    po = 4 + pctr["mo"] % 4
                    pctr["mo"] += 1

                    def mm(e, slv=slv, dcl=dcl, po=po):
                        for k in range(8):
                            ins = e.matmul(ps[po][:], lhsT=slv[:, k, dcl * 128:(dcl + 1) * 128], rhs=mixT[:, k, :],
                                           start=(k == 0), stop=(k == 7))
                        return ins
                    op("pe", mm, reads=[sk] + mkeys, writes=[P(po)])
                    op("dve", lambda e, dc=dc, po=po: e.scalar_tensor_tensor(
                        out=h[:, dc, :], in0=ps[po][:], scalar=modG[:, l, 1, b, dc:dc + 1], in1=h[:, dc, :],
                        op0=ALU.mult, op1=ALU.add),
                       reads=[P(po), ("modG", l, 1, b), hkeys[dc]], writes=[hkeys[dc]])
                    sq_accum(h, hkeys, si, dc)
            yield None
        mdone[l] = ti + 1
        yield ("rel", ("M",))

    tile_list = [(b, j) for b in range(BPC) for j in range(nt)]
    out_keys = []

    def load_tile(ti, hb):
        if dry:
            return
        b, j = tile_list[ti]
        src = xT[b].rearrange("(dc p) t -> p dc t", p=128)[:, :, j * T:(j + 1) * T]
        op("pool", lambda e, hb=hb, src=src: e.dma_start(out=hT[hb][:], in_=src),
           writes=[("h", hb, dc) for dc in range(8)], dma=f"x{hb}")

    def stream(si):
        hb = si
        h = None if dry else hT[hb]
        hkeys = [("h", hb, dc) for dc in range(8)]
        if si == 1:
            yield ("sync",)
        for ti in range(si, len(tile_list), 2):
            b, j = tile_list[ti]
            load_tile(ti, hb)
            for l in range(DEPTH):
                dd = (ti == 0 and l == 0)
                yield from ffn_norm(h, hkeys, l, 0, b, si, need_sq=(l == 0), pb=0)
                yield from ffn_body(h, hkeys, l, 0, b, si, True)
                if dd:
                    dump("h_ffn1", None if dry else h[:], hkeys)
                yield from mixer_norm(h, hkeys, l, b, si)
                yield ("sync",)
                yield from mixer(h, hkeys, l, b, j, ti, si)
                if dd:
                    dump("h_mix", None if dry else h[:], hkeys)
                yield from ffn_norm(h, hkeys, l, 1, b, si, pb=4,
                                    cond=(lambda: fy_done[1 - si] or not active[1 - si]))
                yield ("sync",)
                fy_done[si] = False
                last = (l == DEPTH - 1 and ti + 2 >= len(tile_list))
                if ti == 0 and l == 0:
                    for r in ffn_body(h, hkeys, l, 1, b, si, last):
                        if isinstance(r, tuple) and r[0] == "rel" and r[1] == ("Fg",):
                            yield from adaln(1)
                        yield r
                else:
                    yield from ffn_body(h, hkeys, l, 1, b, si, last)
                if dd:
                    dump("h_ffn2", None if dry else h[:], hkeys)
            yield ("acq", ("N",), None)
            yield None
            if not dry:
                rms_rstd(si, 0)
                for dc in range(8):
                    og = tmpf[dc % 2]
                    op("dve", lambda e, dc=dc, og=og: e.scalar_tensor_tensor(
                        out=og[:], in0=h[:, dc, :], scalar=vec[:, C_FG + dc:C_FG + dc + 1], in1=rstd[:],
                        op0=ALU.mult, op1=ALU.mult), reads=[hkeys[dc], "vec", "rstd"], writes=[("tmpf", dc % 2)])
                    ok = ("out", ti, dc)
                    op("pool", lambda e, dc=dc, og=og, b=b, j=j: e.dma_start(
                        out=outT[b, dc * 128:(dc + 1) * 128, j * T:(j + 1) * T], in_=og[:]),
                       reads=[("tmpf", dc % 2)], writes=[ok], dma=f"o{dc % 2}")
                    out_keys.append(ok)
            yield ("rel", ("N",))

    for _ in adaln(0):
        pass

    gens = [stream(0), stream(1)]
    active = [True, len(tile_list) > 1]
    pending = [None, None]
    at_sync = [False, False]
    held = {}
    seg_idx = [0, 0]
    done = [0, 0]
    seg_counts = [[0], [0]]

    def frac(si):
        if counts is None or seg_idx[si] >= len(counts[si]):
            return done[si]
        return done[si] / max(counts[si][seg_idx[si]], 1)

    while any(active):
        cands = [si for si in (0, 1) if active[si] and not at_sync[si]]
        if not cands:
            for si in (0, 1):
                if at_sync[si]:
                    at_sync[si] = False
                    seg_idx[si] += 1
                    done[si] = 0
                    seg_counts[si].append(0)
            continue
        cands.sort(key=lambda si: (frac(si), si))
        advanced = False
        for si in cands:
            if pending[si] is not None:
                locks, cond = pending[si]
                if all(k not in held for k in locks) and (cond is None or cond()):
                    for k in locks:
                        held[k] = si
                    pending[si] = None
                else:
                    continue
            advanced = True
            try:
                r = next(gens[si])
            except StopIteration:
                active[si] = False
                break
            done[si] += 1
            seg_counts[si][-1] += 1
            if isinstance(r, tuple):
                if r[0] == "acq":
                    pending[si] = (r[1], r[2])
                elif r[0] == "rel":
                    for k in r[1]:
                        assert held.get(k) == si, (k, held, si)
                        del held[k]
                elif r[0] == "sync":
                    at_sync[si] = True
            break
        assert advanced, ("scheduler deadlock", pending, held, at_sync)
    if dry:
        return wseq, seg_counts
    assert wstate["cur"] == len(wseq), (wstate, len(wseq))
    op("sp", None, reads=out_keys[-16:])
    return wseq, dbg_names


def build_nc(nt=SEQ // T, dbg=False):
    nc = bass.Bass("TRN2", target_bir_lowering=False)
    xT = nc.dram_tensor("xT", [BPC, D, SEQ], F32, kind="ExternalInput").ap()
    wA = nc.dram_tensor("wA", [DEPTH, 30, 128, 4096], F32, kind="ExternalInput").ap()
    wB = nc.dram_tensor("wB", [DEPTH, 16, 128, 2816], F32, kind="ExternalInput").ap()
    wada = nc.dram_tensor("wada", [DEPTH, 18, 128, 4096], F32, kind="ExternalInput").ap()
    vecsT = nc.dram_tensor("vecsT", [128, NV], F32, kind="ExternalInput").ap()
    wspT = nc.dram_tensor("wspT", [DEPTH, 128, 512], F32, kind="ExternalInput").ap()
    bsp = nc.dram_tensor("bsp", [DEPTH, 512], F32, kind="ExternalInput").ap()
    outT = nc.dram_tensor("outT", [BPC, D, SEQ], F32, kind="ExternalOutput").ap()
    wsA = nc.dram_tensor("wsA", [DEPTH, 30, 128, 4096], BF16, kind="Internal").ap()
    wsB = nc.dram_tensor("wsB", [DEPTH, 16, 128, 2816], BF16, kind="Internal").ap()
    D_ = (xT, wA, wB, wada, vecsT, wspT, bsp, outT, wsA, wsB)
    _, counts = _program(nc, None, Sched(nc, None, dry=True), nt, None, dbg, D_)
    wseq, _ = _program(nc, None, Sched(nc, None, dry=True), nt, None, dbg, D_, counts=counts)
    print("segment step counts", [c[:6] for c in counts])
    with ExitStack() as es:
        S = Sched(nc, es)
        _, dbg_names = _program(nc, es, S, nt, wseq, dbg, D_, counts=counts)
        print("sbuf bytes remaining", nc.sbuf_bytes_remaining, "ops", {k: len(v) for k, v in S.ops.items()})
        S.emit()
    nc._dbg_names = dbg_names
    return nc


def _blkA(W):
    return W.reshape(8, 128, 512).transpose(1, 0, 2).reshape(128, 4096)


def _blkB(W):
    return W.reshape(22, 128, 128).transpose(1, 0, 2).reshape(128, 2816)


def _prep_shared(w_ada, b_ada, norm_gain, ffn1_w13, ffn1_w2, w_in, hg_lb_logits, hg_gnorm, gm_ln_gain,
                 gm_w_spatial, gm_b_spatial, gm_out_gain, w_out, ffn2_w13, ffn2_w2, final_gain):
    f = lambda a: np.asarray(a, dtype=np.float32)
    wA = np.empty((DEPTH, 30, 128, 4096), np.float32)
    wB = np.empty((DEPTH, 16, 128, 2816), np.float32)
    wada = np.empty((DEPTH, 18, 128, 4096), np.float32)
    for l in range(DEPTH):
        for fi, (w13, w2) in enumerate(((f(ffn1_w13)[l], f(ffn1_w2)[l]), (f(ffn2_w13)[l], f(ffn2_w2)[l]))):
            for i in range(11):
                blk = np.concatenate([w13[:, 256 * i:256 * (i + 1)], w13[:, DFF + 256 * i:DFF + 256 * (i + 1)]], axis=1)
                wA[l, fi * 11 + i] = _blkA(blk)
            for i in range(8):
                wB[l, fi * 8 + i] = _blkB(w2[:, 128 * i:128 * (i + 1)])
        for i in range(6):
            wA[l, 22 + i] = _blkA(f(w_in)[l][:, 512 * i:512 * (i + 1)])
        for i in range(2):
            wA[l, 28 + i] = _blkA(f(w_out)[l][:, 512 * i:512 * (i + 1)])
        for i in range(18):
            wada[l, i] = _blkA(f(w_ada)[l][:, 512 * i:512 * (i + 1)])
    vecs = np.zeros((128, NV), np.float32)
    for l in range(DEPTH):
        vecs[:, C_BADA + l * 72:C_BADA + (l + 1) * 72] = f(b_ada)[l].reshape(72, 128).T
        for sub in range(3):
            c0 = C_NG + (l * 3 + sub) * 8
            vecs[:, c0:c0 + 8] = f(norm_gain)[l, sub].reshape(8, 128).T
        vecs[:, C_LB + l * 4:C_LB + l * 4 + 4] = f(hg_lb_logits)[l].reshape(4, 128).T
        vecs[:, C_GN + l * 4:C_GN + l * 4 + 4] = f(hg_gnorm)[l].reshape(4, 128).T
        vecs[:, C_LNG + l * 4:C_LNG + l * 4 + 4] = f(gm_ln_gain)[l].reshape(4, 128).T
        vecs[:, C_OG + l * 4:C_OG + l * 4 + 4] = f(gm_out_gain)[l].reshape(4, 128).T
    vecs[:, C_FG:C_FG + 8] = f(final_gain).reshape(8, 128).T
    wspT = np.ascontiguousarray(f(gm_w_spatial).transpose(0, 3, 1, 2).reshape(DEPTH, 128, 512))
    bspv = np.ascontiguousarray(f(gm_b_spatial).reshape(DEPTH, 512))
    return wA, wB, wada, vecs, wspT, bspv


def kernel(x, c, w_ada, b_ada, norm_gain, ffn1_w13, ffn1_w2, w_in, hg_lb_logits, hg_gnorm, gm_ln_gain,
           gm_w_spatial, gm_b_spatial, gm_out_gain, w_out, ffn2_w13, ffn2_w2, final_gain, _nt=SEQ // T,
           _cores=NCORES, _dbg=None):
    x = np.asarray(x, dtype=np.float32)
    c = np.asarray(c, dtype=np.float32)
    wA, wB, wada, vecs, wspT, bspv = _prep_shared(
        w_ada, b_ada, norm_gain, ffn1_w13, ffn1_w2, w_in, hg_lb_logits, hg_gnorm, gm_ln_gain, gm_w_spatial,
        gm_b_spatial, gm_out_gain, w_out, ffn2_w13, ffn2_w2, final_gain)
    in_maps = []
    for core in range(_cores):
        xs = x[core * BPC:(core + 1) * BPC]
        xTc = np.ascontiguousarray(xs.transpose(0, 2, 1))
        v = vecs.copy()
        cs = c[core * BPC:(core + 1) * BPC]
        v[:, C_C:C_C + 16] = cs.reshape(BPC, 8, 128).transpose(2, 1, 0).reshape(128, 16)
        in_maps.append({"xT": xTc, "wA": wA, "wB": wB, "wada": wada, "vecsT": v, "wspT": wspT, "bsp": bspv})
    nc = build_nc(_nt, dbg=_dbg is not None)
    res = run_bass_kernel_spmd(nc, in_maps, core_ids=list(range(_cores)))
    out = np.empty((NCORES * BPC, SEQ, D), np.float32)
    if _dbg is not None:
        for k in nc._dbg_names:
            _dbg[k] = np.asarray(res.results[0][k])
    for core in range(_cores):
        out[core * BPC:(core + 1) * BPC] = res.results[core]["outT"].transpose(0, 2, 1)
    return out
```

```python
from contextlib import ExitStack
import numpy as np
import concourse.bass as bass
import concourse.mybir as mybir
from concourse.bass_utils import run_bass_kernel_spmd

F32 = mybir.dt.float32
BF16 = mybir.dt.bfloat16
AF = mybir.ActivationFunctionType
ALU = mybir.AluOpType
AX = mybir.AxisListType

D = 1024
SEQ = 2048
DEPTH = 2
DFF = 2816
T = 512
NCORES = 8
BPC = 2
RMS_EPS = 1e-6
LN_EPS = 1e-5
NSLOT = 4

C_BADA = 0
C_NG = 144
C_LB = 192
C_GN = 200
C_LNG = 208
C_OG = 216
C_FG = 224
C_C = 232
NV = 256


class Sched:
    ENGS = ("pe", "act", "dve", "pool", "sp")

    def __init__(self, nc, es, dry=False):
        self.nc = nc
        self.es = es
        self.dry = dry
        self.ops = {e: [] for e in self.ENGS}
        self.sem = {}
        self.cnt = {}
        self.seen = {e: {} for e in self.ENGS}
        self.bufs = {}
        for e in self.ENGS:
            self._mk("E_" + e)

    def _mk(self, name):
        if name not in self.cnt:
            if not self.dry:
                self.sem[name] = self.es.enter_context(self.nc.semaphore(name))
            self.cnt[name] = 0
        return name

    def op(self, eng, fn, reads=(), writes=(), dma=None):
        if self.dry:
            return None
        deps = {}
        for k in reads:
            b = self.bufs.get(k)
            if b and b["w"]:
                s, v = b["w"]
                deps[s] = max(deps.get(s, 0), v)
        for k in writes:
            b = self.bufs.get(k)
            if b:
                if b["w"]:
                    s, v = b["w"]
                    deps[s] = max(deps.get(s, 0), v)
                for s, v in b["r"]:
                    deps[s] = max(deps.get(s, 0), v)
        own = "E_" + eng
        waits = []
        for s, v in deps.items():
            if s == own and eng == "pe":
                continue
            if self.seen[eng].get(s, 0) >= v:
                continue
            self.seen[eng][s] = v
            waits.append((s, v))
        if dma is None:
            sname, inc = own, 1
        else:
            sname, inc = self._mk("D_" + dma), 16
        self.cnt[sname] += inc
        tok = (sname, self.cnt[sname])
        self.ops[eng].append((waits, fn, sname, inc))
        for k in reads:
            b = self.bufs.setdefault(k, {"w": None, "r": []})
            b["r"].append(tok)
        for k in writes:
            b = self.bufs.setdefault(k, {"w": None, "r": []})
            b["w"] = tok
            b["r"] = []
        return tok

    def emit(self):
        S = self
        with self.nc.Block() as block:
            def run(name):
                def body(e):
                    for waits, fn, sname, inc in S.ops[name]:
                        for s, v in waits:
                            e.wait_ge(S.sem[s], v)
                        if fn is None:
                            e.nop().then_inc(S.sem[sname], inc)
                            continue
                        fn(e).then_inc(S.sem[sname], inc)
                return body
            block.tensor(run("pe"))
            block.scalar(run("act"))
            block.vector(run("dve"))
            block.gpsimd(run("pool"))
            block.sync(run("sp"))


def _program(nc, es, S, nt, wseq_known, dbg, D_, counts=None):
    xT, wA, wB, wada, vecsT, wspT, bsp, outT, wsA, wsB = D_
    op = S.op
    dry = S.dry

    def sb(name, shape, dt=F32):
        return es.enter_context(nc.sbuf_tensor(name, shape, dt))

    if not dry:
        hT = [sb(f"hT{i}", [128, 8, T]) for i in range(2)]
        yTf = sb("yTf", [128, 8, T], BF16)
        yTm = sb("yTm", [128, 8, T], BF16)
        gT = sb("gT", [128, 22, T], BF16)
        ssq = [sb(f"ssq{i}", [128, T]) for i in range(2)]
        ssqb = [sb(f"ssqb{i}", [128, T], BF16) for i in range(2)]
        tmpsq = sb("tmpsq", [128, T])
        slots = [sb(f"slot{i}", [128, 4096], BF16) for i in range(NSLOT)]
        tmpf = [sb(f"tmpf{i}", [128, T]) for i in range(2)]
        rstd = sb("rstd", [128, T])
        sact = [sb(f"sact{i}", [128, T]) for i in range(2)]
        vec = sb("vec", [128, NV])
        cact = sb("cact", [128, 16])
        cactb = sb("cactb", [128, 16], BF16)
        modT = sb("modT", [128, DEPTH, 72, 2])
        modA = sb("modA", [128, DEPTH, 3, 2, 8])
        modG = sb("modG", [128, DEPTH, 3, 2, 8])
        lbt = sb("lbt", [128, DEPTH, 4])
        omlt = sb("omlt", [128, DEPTH, 4])
        nomlt = sb("nomlt", [128, DEPTH, 4])
        lbd = sb("lbd", [128, 4])
        identb = sb("identb", [128, 128], BF16)
        identf = sb("identf", [128, 128])
        onesb = sb("onesb", [128, 128], BF16)
        cmask = sb("cmask", [128, 128])
        gmask = sb("gmask", [128, 128])
        rmask = sb("rmask", [128, T])
        WmT = sb("WmT", [128, DEPTH, 4, 128], BF16)
        bspb = sb("bspb", [128, DEPTH, 512])
        epsr = sb("epsr", [128, 1])
        epsl = sb("epsl", [128, 1])
        dummy = sb("dummyt", [128, 1])
        epraw = sb("epraw", [128, 4096], BF16)
        Ep = epraw[:, 0:2048].rearrange("p (h t) -> p h t", h=4)
        Ep2 = epraw[:, 2048:4096].rearrange("p (h t) -> p h t", h=4)
        epf = epraw[:].bitcast(F32)
        osb, o1, zz, z1 = epf[:, 0:512], epf[:, 512:1024], epf[:, 1024:1536], epf[:, 1536:2048]
        z2 = sb("z2", [128, 512])
        wspf = z2
        vg4 = epf[:, 0:2048].rearrange("p (a t) -> p a t", a=4)
        ktT = sb("ktT", [128, 4, T], BF16)
        kdT = sb("kdT", [128, 4, T], BF16)
        qtT = sb("qtT", [128, 4, T], BF16)
        qdT = sb("qdT", [128, 4, T], BF16)
        sgT = sb("sgT", [128, 4, T], BF16)
        ugT = sb("ugT", [128, 4, T], BF16)
        vi = sb("vi", [128, 4, 512], BF16)
        vhat = sb("vhat", [128, 4, 512], BF16)
        mixraw = sb("mixraw", [128, 8 * T], BF16)
        mixT = mixraw[:].rearrange("p (k t) -> p k t", k=8)
        mixf = mixraw[:].bitcast(F32)
        sig, logf, kk, bcs = mixf[:, 0:512], mixf[:, 512:1024], mixf[:, 1024:1536], mixf[:, 1536:2048]
        br = sb("br", [128, T])
        sigx = sb("sigx", [128, 3, T])
        Em = sb("Em", [128, T])
        er = sb("er", [128, 4, 8])
        wc = sb("wc", [128, 4, 8])
        dl = sb("dl", [128, 4, 8])
        Sst = sb("Sst", [128, DEPTH, 4, 128])
        Shat = sb("Shat", [128, 2, 4, 128], BF16)
        dst = sb("dst", [128, 4, 128])
        ktok = sb("ktok", [128, 4, 128], BF16)
        ATm = sb("ATm", [128, 4, 128], BF16)
        osq = sb("osq", [128, 512], BF16)
        zsq = sb("zsq", [128, 512], BF16)
        st1 = sb("st1", [128, 4, 4])
        st2 = sb("st2", [128, 4, 4])
        mean = sb("mean", [128, 4, 4])
        msq = sb("msq", [128, 4, 4])
        var = sb("var", [128, 4, 4])
        lrs = sb("lrs", [128, 4, 4])
        ps = [es.enter_context(nc.psum_tensor(f"ps{i}", [128, 512], F32)) for i in range(8)]

    def P(i):
        return ("ps", i)

    dbg_names = []

    wseq = [] if wseq_known is None else wseq_known
    wstate = {"issued": 0, "cur": 0, "seen": set()}

    def slot_key(i):
        return ("slot", i % NSLOT)

    def prefetch(upto):
        while wstate["issued"] <= min(upto, len(wseq) - 1):
            i = wstate["issued"]
            kind, l, blk = wseq[i]
            sl = slots[i % NSLOT]
            n = 2816 if kind == "B" else 4096
            src = {"A": wA, "B": wB, "ada": wada}[kind]
            scr = {"A": wsA, "B": wsB}.get(kind)
            bkey = (kind, l, blk)
            if scr is None or bkey not in wstate["seen"]:
                op("pool", lambda e, sl=sl, l=l, blk=blk, n=n, src=src: e.dma_start(out=sl[:, 0:n], in_=src[l, blk]),
                   writes=[slot_key(i)], dma=f"w{i % NSLOT}")
                if scr is not None:
                    wstate["seen"].add(bkey)
                    op("sp", lambda e, sl=sl, l=l, blk=blk, n=n, scr=scr: e.dma_start(out=scr[l, blk], in_=sl[:, 0:n]),
                       reads=[slot_key(i)], writes=[("scr",) + bkey], dma=f"sw{i % NSLOT}")
            else:
                op("sp", lambda e, sl=sl, l=l, blk=blk, n=n, scr=scr: e.dma_start(out=sl[:, 0:n], in_=scr[l, blk]),
                   reads=[("scr",) + bkey], writes=[slot_key(i)], dma=f"w{i % NSLOT}")
            wstate["issued"] += 1

    def next_w(spec):
        i = wstate["cur"]
        wstate["cur"] += 1
        if dry:
            wseq.append(spec)
            return None, None
        assert wseq[i] == spec, (wseq[i], spec)
        prefetch(i + NSLOT - 1)
        return slots[i % NSLOT], slot_key(i)

    if not dry:
        op("sp", lambda e: e.dma_start(out=vec[:], in_=vecsT), writes=["vec"], dma="vec")
        for l in range(DEPTH):
            op("sp", lambda e, l=l: e.dma_start(out=bspb[:, l, :], in_=bsp[l].partition_broadcast(128)),
               writes=[("bspb", l)], dma=f"bspb{l}")
        op("dve", lambda e: e.memset(epsr[:], RMS_EPS), writes=["epsr"])
        op("dve", lambda e: e.memset(epsl[:], LN_EPS), writes=["epsl"])
        op("pool", lambda e: e.memset(identf[:], 0.0), writes=["identf"])
        op("pool", lambda e: e.affine_select(out=identf[:], in_=identf[:], pattern=[[-1, 128]],
                                             compare_op=ALU.not_equal, fill=1.0, base=0, channel_multiplier=1),
           reads=["identf"], writes=["identf"])
        op("dve", lambda e: e.tensor_copy(out=identb[:], in_=identf[:]), reads=["identf"], writes=["identb"])
        op("dve", lambda e: e.memset(onesb[:], 1.0), writes=["onesb"])
        op("pool", lambda e: e.memset(cmask[:], 1.0), writes=["cmask"])
        op("pool", lambda e: e.affine_select(out=cmask[:], in_=cmask[:], pattern=[[1, 128]], compare_op=ALU.is_ge,
                                             fill=0.0, base=0, channel_multiplier=-1),
           reads=["cmask"], writes=["cmask"])
        for i_ in range(3):
            op("pool", lambda e, i_=i_: e.memset(cmask[i_ * 32:(i_ + 1) * 32, (i_ + 1) * 32:128], 0.0),
               reads=["cmask"], writes=["cmask"])
        op("pool", lambda e: e.memset(gmask[:], 1.0), writes=["gmask"])
        op("pool", lambda e: e.memset(gmask[64:128, 0:64], 0.0), reads=["gmask"], writes=["gmask"])
        op("dve", lambda e: e.memset(rmask[:], 1.0), writes=["rmask"])
        op("dve", lambda e: e.memset(rmask[:].rearrange("p (c t) -> p c t", t=64)[:, :, 0:1], 0.0),
           reads=["rmask"], writes=["rmask"])
        for l in range(DEPTH):
            op("sp", lambda e, l=l: e.dma_start(out=wspf[:], in_=wspT[l]), writes=["z2"], dma="wsp")
            op("dve", lambda e, l=l: e.tensor_tensor(
                out=WmT[:, l, :, :], in0=wspf[:].rearrange("p (h t) -> p h t", h=4),
                in1=gmask[:].unsqueeze(1).to_broadcast([128, 4, 128]), op=ALU.mult),
               reads=["z2", "gmask"], writes=[("WmT", l)])
        op("dve", lambda e: e.memset(lbt[:, 0, :], 0.0), writes=["lb0"])
        op("dve", lambda e: e.memset(omlt[:, 0, :], 1.0), writes=["oml0"])
        op("dve", lambda e: e.memset(nomlt[:, 0, :], -1.0), writes=["noml0"])
        op("dve", lambda e: e.tensor_tensor(out=lbd[:], in0=vec[:, C_LB + 4:C_LB + 8], in1=vec[:, C_LB:C_LB + 4],
                                            op=ALU.subtract), reads=["vec"], writes=["lbd"])
        op("act", lambda e: e.activation(out=lbt[:, 1, :], in_=lbd[:], func=AF.Sigmoid), reads=["lbd"], writes=["lb1"])
        op("act", lambda e: e.activation(out=omlt[:, 1, :], in_=lbd[:], func=AF.Sigmoid, scale=-1.0),
           reads=["lbd"], writes=["oml1"])
        op("dve", lambda e: e.tensor_scalar(out=nomlt[:, 1, :], in0=omlt[:, 1, :], scalar1=-1.0, scalar2=None,
                                            op0=ALU.mult), reads=["oml1"], writes=["noml1"])
        op("act", lambda e: e.activation(out=cact[:], in_=vec[:, C_C:C_C + 16], func=AF.Silu),
           reads=["vec"], writes=["cact"])
        op("dve", lambda e: e.tensor_copy(out=cactb[:], in_=cact[:]), reads=["cact"], writes=["cactb"])

    def dump(name, ap, keys):
        if not dbg or dry:
            return
        shp = [int(x) for x in ap.shape]
        dt_ = nc.dram_tensor("dbg_" + name, shp, ap.dtype, kind="ExternalOutput").ap()
        dbg_names.append("dbg_" + name)
        op("sp", lambda e: e.dma_start(out=dt_, in_=ap), reads=keys, writes=[("dbg", name)], dma="dbg_" + name)
        op("sp", None, reads=[("dbg", name)])

    def adaln(l):
        def mm_blk(cb, sl, sk):
            slv = sl[:, 0:4096].rearrange("p (k n) -> p k n", k=8)
            r = cb % 2

            def mm(e):
                for k in range(8):
                    ins = e.matmul(ps[r][0:2, :], lhsT=cactb[:, 2 * k:2 * k + 2], rhs=slv[:, k, :],
                                   start=(k == 0), stop=(k == 7))
                return ins
            op("pe", mm, reads=[sk, "cactb"], writes=[P(r)])
            op("act", lambda e: e.copy(out=sact[r][0:2, :], in_=ps[r][0:2, :]), reads=[P(r)], writes=[("sact", r)])

        def tr_blk(cb):
            r = cb % 2

            def tr(e):
                for q in range(4):
                    col = (cb * 4 + q) * 2
                    ins = e.transpose(out=ps[2][:, col:col + 2], in_=sact[r][0:2, q * 128:(q + 1) * 128],
                                      identity=identf[0:2, 0:2])
                return ins
            op("pe", tr, reads=[("sact", r), "identf"], writes=[P(2)])

        for cb in range(18):
            sl, sk = next_w(("ada", l, cb))
            if not dry:
                mm_blk(cb, sl, sk)
                if cb > 0:
                    tr_blk(cb - 1)
            if cb % 2 == 1:
                yield None
        if not dry:
            tr_blk(17)
        if dry:
            return
        pm = ps[2]
        op("dve", lambda e: e.tensor_tensor(
            out=modT[:, l, :, :], in0=pm[:, 0:144].rearrange("p (c b) -> p c b", b=2),
            in1=vec[:, C_BADA + l * 72:C_BADA + (l + 1) * 72].unsqueeze(2).to_broadcast([128, 72, 2]),
            op=ALU.add), reads=[P(2), "vec"], writes=[("modT", l)])
        for sub in range(3):
            for b in range(BPC):
                ng = vec[:, C_NG + (l * 3 + sub) * 8:C_NG + (l * 3 + sub) * 8 + 8]
                op("dve", lambda e, sub=sub, b=b, ng=ng: e.scalar_tensor_tensor(
                    out=modA[:, l, sub, b, :], in0=modT[:, l, (3 * sub + 1) * 8:(3 * sub + 2) * 8, b], scalar=1.0,
                    in1=ng, op0=ALU.add, op1=ALU.mult), reads=[("modT", l), "vec"], writes=[("modA", l, sub, b)])
                gs = 1.0 if sub == 1 else 0.5
                op("dve", lambda e, sub=sub, b=b, gs=gs: e.tensor_scalar(
                    out=modG[:, l, sub, b, :], in0=modT[:, l, (3 * sub + 2) * 8:(3 * sub + 3) * 8, b], scalar1=gs,
                    scalar2=None, op0=ALU.mult), reads=[("modT", l)], writes=[("modG", l, sub, b)])

    def rsqrt_psum(dst_t, dkey, pb, scale):
        op("act", lambda e: e.activation(out=dst_t[:], in_=ps[pb][:], func=AF.Ln, scale=scale, bias=epsr[:]),
           reads=[P(pb), "epsr"], writes=[dkey])
        op("act", lambda e: e.activation(out=dst_t[:], in_=dst_t[:], func=AF.Exp, scale=-0.5),
           reads=[dkey], writes=[dkey])

    def sq_accum(h, hkeys, si, dc):
        if dc == 0:
            op("dve", lambda e: e.tensor_tensor(out=ssq[si][:], in0=h[:, 0, :], in1=h[:, 0, :], op=ALU.mult),
               reads=[hkeys[0]], writes=[("ssq", si)])
            return
        op("dve", lambda e: e.tensor_tensor(out=tmpsq[:], in0=h[:, dc, :], in1=h[:, dc, :], op=ALU.mult),
           reads=[hkeys[dc]], writes=["tmpsq"])
        if dc < 7:
            op("dve", lambda e: e.tensor_tensor(out=ssq[si][:], in0=ssq[si][:], in1=tmpsq[:], op=ALU.add),
               reads=[("ssq", si), "tmpsq"], writes=[("ssq", si)])
        else:
            op("dve", lambda e: e.tensor_tensor(out=ssqb[si][:], in0=ssq[si][:], in1=tmpsq[:], op=ALU.add),
               reads=[("ssq", si), "tmpsq"], writes=[("ssqb", si)])

    def rms_squares(h, hkeys, si):
        for dc in range(8):
            sq_accum(h, hkeys, si, dc)

    def rms_rstd(si, pb):
        op("pe", lambda e: e.matmul(ps[pb][:], lhsT=onesb[:], rhs=ssqb[si][:], start=True, stop=True),
           reads=[("ssqb", si), "onesb"], writes=[P(pb)])
        rsqrt_psum(rstd, "rstd", pb, 1.0 / D)

    def norm_mod(h, hkeys, y, ykeys, l, sub, b, pb):
        for dc in range(8):
            tf = tmpf[dc % 2]
            op("dve", lambda e, dc=dc, tf=tf: e.scalar_tensor_tensor(
                out=tf[:], in0=h[:, dc, :], scalar=modA[:, l, sub, b, dc:dc + 1], in1=rstd[:],
                op0=ALU.mult, op1=ALU.mult),
               reads=[hkeys[dc], "rstd", ("modA", l, sub, b)], writes=[("tmpf", dc % 2)])
            op("act", lambda e, dc=dc, tf=tf: e.activation(
                out=y[:, dc, :], in_=tf[:], func=AF.Identity,
                bias=modT[:, l, 3 * sub * 8 + dc, b:b + 1], scale=1.0),
               reads=[("tmpf", dc % 2), ("modT", l)], writes=[ykeys[dc]])

    ykf = [("yf", dc) for dc in range(8)]
    ykm = [("ym", dc) for dc in range(8)]
    pctr = {"ab": 0, "o": 0, "mo": 0}

    fy_done = [False, False]

    def ffn_norm(h, hkeys, l, f, b, si, need_sq=False, pb=0, cond=None):
        sub = 0 if f == 0 else 2
        yield ("acq", ("N", "Fy"), cond)
        if need_sq:
            if not dry:
                rms_squares(h, hkeys, si)
            yield None
        if not dry:
            rms_rstd(si, pb)
        yield None
        if not dry:
            norm_mod(h, hkeys, yTf, ykf, l, sub, b, pb)
        yield ("rel", ("N",))

    def ffn_body(h, hkeys, l, f, b, si, last_in_sp):
        sub = 0 if f == 0 else 2
        yield ("acq", ("Fg",), None)
        for blk in range(11):
            sl, sk = next_w(("A", l, f * 11 + blk))
            if not dry:
                slv = sl[:, 0:4096].rearrange("p (k n) -> p k n", k=8)
                for ch in range(2):
                    r = pctr["ab"] % 2
                    pctr["ab"] += 1
                    pa, pb = r, 2 + r
                    hc = blk * 2 + ch

                    def mm(e, slv=slv, ch=ch, pa=pa, pb=pb):
                        for k in range(8):
                            e.matmul(ps[pa][:], lhsT=slv[:, k, ch * 128:(ch + 1) * 128], rhs=yTf[:, k, :],
                                     start=(k == 0), stop=(k == 7))
                        for k in range(8):
                            ins = e.matmul(ps[pb][:], lhsT=slv[:, k, 256 + ch * 128:256 + (ch + 1) * 128],
                                           rhs=yTf[:, k, :], start=(k == 0), stop=(k == 7))
                        return ins
                    op("pe", mm, reads=[sk] + ykf, writes=[P(pa), P(pb)])
                    op("act", lambda e, pa=pa, r=r: e.activation(out=sact[r][:], in_=ps[pa][:], func=AF.Silu),
                       reads=[P(pa)], writes=[("sact", r)])
                    op("dve", lambda e, pb=pb, r=r, hc=hc: e.tensor_tensor(out=gT[:, hc, :], in0=sact[r][:],
                                                                           in1=ps[pb][:], op=ALU.mult),
                       reads=[("sact", r), P(pb)], writes=[("g", hc)])
            yield None
        yield ("rel", ("Fy",))
        if last_in_sp:
            fy_done[si] = True
        gkeys = [("g", i) for i in range(22)]
        for dc in range(8):
            sl, sk = next_w(("B", l, f * 8 + dc))
            if not dry:
                slv = sl[:, 0:2816].rearrange("p (k n) -> p k n", k=22)
                po = 2 + pctr["o"] % 2
                pctr["o"] += 1

                def mm(e, slv=slv, po=po):
                    for k in range(22):
                        ins = e.matmul(ps[po][:], lhsT=slv[:, k, :], rhs=gT[:, k, :], start=(k == 0), stop=(k == 21))
                    return ins
                op("pe", mm, reads=[sk] + gkeys, writes=[P(po)])
                op("dve", lambda e, dc=dc, po=po: e.scalar_tensor_tensor(
                    out=h[:, dc, :], in0=ps[po][:], scalar=modG[:, l, sub, b, dc:dc + 1], in1=h[:, dc, :],
                    op0=ALU.mult, op1=ALU.add),
                   reads=[P(po), ("modG", l, sub, b), hkeys[dc]], writes=[hkeys[dc]])
                sq_accum(h, hkeys, si, dc)
            if dc % 2 == 1:
                yield None
        yield ("rel", ("Fg",))

    def c3(ap):
        return ap.rearrange("p (c t) -> p c t", t=64)

    def c32(ap):
        return ap.rearrange("p (c t) -> p c t", t=32)

    def h4(ap):
        return ap.rearrange("p (h t) -> p h t", h=4)

    mdone = [0] * DEPTH

    def mixer_norm(h, hkeys, l, b, si):
        yield ("acq", ("N", "Y"), None)
        if not dry:
            rms_rstd(si, 0)
        yield None
        if not dry:
            norm_mod(h, hkeys, yTm, ykm, l, 1, b, 0)
        yield ("rel", ("N",))

    def mixer(h, hkeys, l, b, j, ti, si):
        yield ("acq", ("M",), (lambda: mdone[l] == ti))
        if j == 0 and not dry:
            op("dve", lambda e: e.memset(Sst[:, l, :, :], 0.0), writes=[("S", l)])

        def proj_fm(slv, hd, pb):
            def mm(e):
                for k in range(8):
                    ins = e.matmul(ps[pb][:], lhsT=slv[:, k, hd * 128:(hd + 1) * 128], rhs=yTm[:, k, :],
                                   start=(k == 0), stop=(k == 7))
                return ins
            return mm

        def proj_tm(slv, s, pb):
            def mm(e):
                for k in range(8):
                    ins = e.matmul(ps[pb][:], lhsT=yTm[:, k, s * 128:(s + 1) * 128], rhs=slv[:, k, :],
                                   start=(k == 0), stop=(k == 7))
                return ins
            return mm

        A_ = lambda sl: sl[:, 0:4096].rearrange("p (k n) -> p k n", k=8)
        sigs = None if dry else [(sig, "sig0"), (sigx[:, 0, :], "sig1"), (sigx[:, 1, :], "sig2"), (sigx[:, 2, :], "sig3")]
        sl, sk = next_w(("A", l, 23))
        if not dry:
            slv = A_(sl)
            for hd in range(4):
                op("pe", proj_fm(slv, hd, 4 + hd), reads=[sk] + ykm, writes=[P(4 + hd)])
            for hd in range(4):
                sb_, skey = sigs[hd]
                op("act", lambda e, hd=hd, sb_=sb_: e.activation(out=sb_, in_=ps[4 + hd][:], func=AF.Sigmoid),
                   reads=[P(4 + hd)], writes=[skey])
        yield None
        sl, sk = next_w(("A", l, 25))
        if not dry:
            slv = A_(sl)
            for hd in range(4):
                op("pe", proj_fm(slv, hd, 4 + hd), reads=[sk] + ykm, writes=[P(4 + hd)])
            for hd in range(4):
                tb, tk = (br, "br") if hd % 2 == 0 else (Em, "Em")
                op("act", lambda e, hd=hd, tb=tb: e.activation(out=tb[:], in_=ps[4 + hd][:], func=AF.Silu),
                   reads=[P(4 + hd)], writes=[tk])
                op("dve", lambda e, hd=hd, tb=tb: e.tensor_scalar(
                    out=sgT[:, hd, :], in0=tb[:], scalar1=vec[:, C_GN + l * 4 + hd:C_GN + l * 4 + hd + 1],
                    scalar2=None, op0=ALU.mult), reads=[tk, "vec"], writes=[("sg", hd)])
        yield None
        sl, sk = next_w(("A", l, 26))
        if not dry:
            slv = A_(sl)
            for hd in range(4):
                op("pe", proj_fm(slv, hd, 4 + hd), reads=[sk] + ykm, writes=[P(4 + hd)])
            for hd in range(4):
                op("act", lambda e, hd=hd: e.activation(out=ugT[:, hd, :], in_=ps[4 + hd][:],
                                                        func=AF.Gelu_apprx_tanh),
                   reads=[P(4 + hd)], writes=[("ug", hd)])
        yield None
        sl, sk = next_w(("A", l, 24))
        if not dry:
            slv = A_(sl)
            for s in range(4):
                op("pe", proj_tm(slv, s, 4 + s), reads=[sk] + ykm, writes=[P(4 + s)])
            for s in range(4):
                op("act", lambda e, s=s: e.copy(out=vi[:, s, :], in_=ps[4 + s][:]), reads=[P(4 + s)],
                   writes=[("vi", s)])
        yield None
        sl, sk = next_w(("A", l, 27))
        vgk4 = ["vg0", "vg1", "vg2", "vg3"]
        if not dry:
            slv = A_(sl)
            for s in range(4):
                op("pe", proj_tm(slv, s, 4 + s), reads=[sk] + ykm, writes=[P(4 + s)])
            for s in range(4):
                op("act", lambda e, s=s: e.activation(out=vg4[:, s, :], in_=ps[4 + s][:], func=AF.Gelu_apprx_tanh),
                   reads=[P(4 + s)], writes=[vgk4[s]])
            for s in range(4):
                op("dve", lambda e, s=s: e.tensor_reduce(out=st1[:, s, :], in_=h4(vg4[:, s, :]), axis=AX.X, op=ALU.add),
                   reads=[vgk4[s]], writes=["st1"])
                op("dve", lambda e, s=s: e.tensor_tensor(out=z2[:], in0=vg4[:, s, :], in1=vg4[:, s, :], op=ALU.mult),
                   reads=[vgk4[s]], writes=["z2"])
                op("dve", lambda e, s=s: e.tensor_reduce(out=st2[:, s, :], in_=h4(z2[:]), axis=AX.X, op=ALU.add),
                   reads=["z2"], writes=["st2"])
            op("dve", lambda e: e.tensor_scalar(out=mean[:], in0=st1[:], scalar1=1.0 / 128, scalar2=None,
                                                op0=ALU.mult), reads=["st1"], writes=["mean"])
            op("dve", lambda e: e.tensor_tensor(out=msq[:], in0=mean[:], in1=mean[:], op=ALU.mult),
               reads=["mean"], writes=["msq"])
            op("dve", lambda e: e.scalar_tensor_tensor(out=var[:], in0=st2[:], scalar=1.0 / 128, in1=msq[:],
                                                       op0=ALU.mult, op1=ALU.subtract),
               reads=["st2", "msq"], writes=["var"])
            op("act", lambda e: e.activation(out=lrs[:], in_=var[:], func=AF.Ln, scale=1.0, bias=epsl[:]),
               reads=["var", "epsl"], writes=["lrs"])
            op("act", lambda e: e.activation(out=lrs[:], in_=lrs[:], func=AF.Exp, scale=-0.5),
               reads=["lrs"], writes=["lrs"])
        yield None
        if not dry:
            for s in range(4):
                for hd in range(4):
                    op("dve", lambda e, s=s, hd=hd: e.tensor_scalar(
                        out=vhat[:, s, hd * 128:(hd + 1) * 128], in0=vg4[:, s, hd * 128:(hd + 1) * 128],
                        scalar1=mean[:, s, hd:hd + 1], scalar2=lrs[:, s, hd:hd + 1], op0=ALU.subtract, op1=ALU.mult),
                       reads=[vgk4[s], "mean", "lrs"], writes=[("vhat", s, hd)])
        yield None
        for hd in range(4):
            if not dry:
                sb_, skey = sigs[hd]
                op("act", lambda e, hd=hd, sb_=sb_: e.activation(out=logf[:], in_=sb_, func=AF.Ln,
                                                                  scale=omlt[:, l, hd:hd + 1], bias=lbt[:, l, hd:hd + 1]),
                   reads=[skey, f"oml{l}", f"lb{l}"], writes=["logf"])
                op("dve", lambda e, hd=hd, sb_=sb_: e.tensor_scalar(out=kk[:], in0=sb_, scalar1=nomlt[:, l, hd:hd + 1],
                                                                    scalar2=omlt[:, l, hd:hd + 1], op0=ALU.mult, op1=ALU.add),
                   reads=[skey, f"oml{l}", f"noml{l}"], writes=["kk"])
                op("dve", lambda e: e.tensor_tensor_scan(out=bcs[:], data0=rmask[:], data1=logf[:], initial=0.0,
                                                         op0=ALU.mult, op1=ALU.add),
                   reads=["logf", "rmask"], writes=["bcs"])
                op("dve", lambda e: e.tensor_tensor(out=c3(br[:]), in0=c3(bcs[:]),
                                                    in1=c3(bcs[:])[:, :, 31:32].to_broadcast([128, 8, 64]),
                                                    op=ALU.subtract), reads=["bcs"], writes=["br"])
                op("act", lambda e, hd=hd: e.activation(out=Ep[:, hd, :], in_=br[:], func=AF.Exp),
                   reads=["br"], writes=[("Ep", hd), "vg0", "vg1", "vg2", "vg3"])
                op("act", lambda e: e.activation(out=Em[:], in_=br[:], func=AF.Exp, scale=-1.0),
                   reads=["br"], writes=["Em"])
                op("act", lambda e, hd=hd: e.activation(out=dl[:, hd, :], in_=c3(br[:])[:, :, 63], func=AF.Exp),
                   reads=["br"], writes=[("dl", hd)])
                op("act", lambda e, hd=hd: e.activation(out=er[:, hd, :], in_=c3(bcs[:])[:, :, 31], func=AF.Exp),
                   reads=["bcs"], writes=[("er", hd)])
                op("act", lambda e, hd=hd: e.activation(out=wc[:, hd, :], in_=c3(bcs[:])[:, :, 63], func=AF.Exp),
                   reads=["bcs"], writes=[("wc", hd)])
                op("dve", lambda e, hd=hd: e.tensor_tensor(out=ktT[:, hd, :], in0=kk[:], in1=Em[:], op=ALU.mult),
                   reads=["kk", "Em"], writes=[("kt", hd)])
                op("dve", lambda e: e.tensor_tensor(out=c32(br[:]), in0=c32(bcs[:]),
                                                    in1=c32(bcs[:])[:, :, 15:16].to_broadcast([128, 16, 32]),
                                                    op=ALU.subtract), reads=["bcs"], writes=["br"])
                op("act", lambda e, hd=hd: e.activation(out=Ep2[:, hd, :], in_=br[:], func=AF.Exp),
                   reads=["br"], writes=[("Ep2", hd), "vg0", "vg1", "vg2", "vg3"])
                op("act", lambda e: e.activation(out=Em[:], in_=br[:], func=AF.Exp, scale=-1.0),
                   reads=["br"], writes=["Em"])
                op("dve", lambda e, hd=hd: e.tensor_tensor(out=kdT[:, hd, :], in0=kk[:], in1=Em[:], op=ALU.mult),
                   reads=["kk", "Em"], writes=[("kd", hd)])
            yield None
        sl, sk = next_w(("A", l, 22))
        if not dry:
            slv = A_(sl)
            for hd in range(4):
                op("pe", proj_fm(slv, hd, 4 + hd), reads=[sk] + ykm, writes=[P(4 + hd)])
            for hd in range(4):
                pb = 4 + hd
                op("dve", lambda e, hd=hd, pb=pb: e.tensor_tensor(out=qtT[:, hd, :], in0=ps[pb][:], in1=Ep[:, hd, :],
                                                                  op=ALU.mult),
                   reads=[P(pb), ("Ep", hd)], writes=[("qt", hd)])
                op("dve", lambda e, hd=hd, pb=pb: e.tensor_tensor(out=qdT[:, hd, :], in0=ps[pb][:], in1=Ep2[:, hd, :],
                                                                  op=ALU.mult),
                   reads=[P(pb), ("Ep2", hd)], writes=[("qd", hd)])
        yield ("rel", ("Y",))
        if not dry:
            op("dve", lambda e: e.memset(dummy[:], 0.0),
               reads=[("Ep", hd) for hd in range(4)] + [("Ep2", hd) for hd in range(4)],
               writes=["osb", "o1", "zz", "z1", "dummy"])

        def subtile_steps(s):
            ts = slice(s * 128, (s + 1) * 128)
            kts = [("kt", hd) for hd in range(4)]
            qts = [("qt", hd) for hd in range(4)]
            ptr = ps[4][:].bitcast(BF16)[:, 0:512].rearrange("p (h k) -> p h k", h=4)
            pat = h4(ps[5][:])
            pat2 = h4(ps[6][:])
            pdb = (4, 7)
            po = h4(ps[6][:])
            pmm = h4(ps[7][:])

            def st_a():
                def tr(e):
                    for hd in range(4):
                        ins = e.transpose(out=ptr[:, hd, :], in_=ktT[:, hd, ts], identity=identb[:])
                    return ins
                op("pe", tr, reads=kts + ["identb"], writes=[P(4)])

                def at(e):
                    for hd in range(4):
                        ins = e.matmul(pat[:, hd, :], lhsT=kdT[:, hd, ts], rhs=qdT[:, hd, ts], start=True, stop=True)
                    return ins
                op("pe", at, reads=[("kd", hd) for hd in range(4)] + [("qd", hd) for hd in range(4)], writes=[P(5)])

                def at2(e):
                    for hd in range(4):
                        for cc in range(2):
                            r0 = cc * 64
                            t0 = s * 128 + cc * 64
                            ins = e.matmul(pat2[r0:r0 + 32, hd, r0 + 32:r0 + 64], lhsT=ktT[:, hd, t0:t0 + 32],
                                           rhs=qtT[:, hd, t0 + 32:t0 + 64], start=True, stop=True)
                    return ins
                op("pe", at2, reads=kts + qts, writes=[P(6)])

                def gm(e):
                    for hd in range(4):
                        ins = e.matmul(pmm[:, hd, :], lhsT=vhat[:, s, hd * 128:(hd + 1) * 128], rhs=WmT[:, l, hd, :],
                                       start=True, stop=True)
                    return ins
                op("pe", gm, reads=[("vhat", s, hd) for hd in range(4)] + [("WmT", l)], writes=[P(7)])
                op("act", lambda e: e.copy(out=ktok[:], in_=ptr), reads=[P(4)], writes=["ktok"])
                op("dve", lambda e: e.tensor_tensor(out=ATm[:], in0=pat,
                                                    in1=cmask[:].unsqueeze(1).to_broadcast([128, 4, 128]),
                                                    op=ALU.mult), reads=[P(5), "cmask"], writes=["ATm"])
                for cc in range(2):
                    r0 = cc * 64
                    op("act", lambda e, r0=r0: e.copy(out=ATm[r0:r0 + 32, :, r0 + 32:r0 + 64],
                                                      in_=pat2[r0:r0 + 32, :, r0 + 32:r0 + 64]),
                       reads=[P(6)], writes=["ATm"])
                for hd in range(4):
                    op("dve", lambda e, hd=hd: e.scalar_tensor_tensor(
                        out=z1[:, hd * 128:(hd + 1) * 128], in0=pmm[:, hd, :],
                        scalar=vec[:, C_LNG + l * 4 + hd:C_LNG + l * 4 + hd + 1],
                        in1=bspb[:, l, hd * 128:(hd + 1) * 128], op0=ALU.mult, op1=ALU.add),
                       reads=[P(7), "vec", ("bspb", l)], writes=["z1"])
                op("dve", lambda e: e.tensor_tensor(out=h4(zz[:]), in0=h4(z1[:]), in1=ugT[:, :, ts], op=ALU.mult),
                   reads=["z1"] + [("ug", hd) for hd in range(4)], writes=["zz"])
                op("act", lambda e: e.activation(out=zsq[:], in_=zz[:], func=AF.Square), reads=["zz"], writes=["zsq"])

            def st_b():
                for cc in range(2):
                    pds = h4(ps[pdb[cc]][:])
                    rows = slice(cc * 64, (cc + 1) * 64)

                    def dsm(e, pds=pds, rows=rows):
                        for hd in range(4):
                            ins = e.matmul(pds[:, hd, :], lhsT=ktok[rows, hd, :], rhs=vi[rows, s, hd * 128:(hd + 1) * 128],
                                           start=True, stop=True)
                        return ins
                    op("pe", dsm, reads=["ktok", ("vi", s)], writes=[P(pdb[cc])])
                op("pe", lambda e: e.matmul(ps[5][:], lhsT=onesb[:], rhs=zsq[:], start=True, stop=True),
                   reads=["zsq", "onesb"], writes=[P(5)])
                for cc in range(2):
                    c = 2 * s + cc
                    pds = h4(ps[pdb[cc]][:])

                    def bc(t, c=c):
                        return t[:, :, c:c + 1].to_broadcast([128, 4, 128])
                    op("dve", lambda e, cc=cc, bc=bc: e.tensor_tensor(out=Shat[:, cc, :, :], in0=Sst[:, l, :, :],
                                                                      in1=bc(er), op=ALU.mult),
                       reads=[("S", l)] + [("er", hd) for hd in range(4)], writes=[("Shat", cc)])
                    op("dve", lambda e, pds=pds, bc=bc: e.tensor_tensor(out=dst[:], in0=pds, in1=bc(dl), op=ALU.mult),
                       reads=[P(pdb[cc])] + [("dl", hd) for hd in range(4)], writes=["dst"])
                    op("dve", lambda e, bc=bc: e.tensor_tensor(out=Sst[:, l, :, :], in0=Sst[:, l, :, :], in1=bc(wc),
                                                               op=ALU.mult),
                       reads=[("S", l)] + [("wc", hd) for hd in range(4)], writes=[("S", l)])
                    op("dve", lambda e: e.tensor_tensor(out=Sst[:, l, :, :], in0=Sst[:, l, :, :], in1=dst[:],
                                                        op=ALU.add), reads=[("S", l), "dst"], writes=[("S", l)])
            def st_b2():
                rsqrt_psum(z2, "z2", 5, 1.0 / 128)
                op("dve", lambda e: e.tensor_tensor(out=z2[:], in0=z2[:], in1=zz[:], op=ALU.mult),
                   reads=["z2", "zz"], writes=["z2"])
                for hd in range(4):
                    op("dve", lambda e, hd=hd: e.tensor_scalar(
                        out=mixT[:, 4 + hd, ts], in0=z2[:, hd * 128:(hd + 1) * 128],
                        scalar1=vec[:, C_OG + l * 4 + hd:C_OG + l * 4 + hd + 1], scalar2=None, op0=ALU.mult),
                       reads=["z2", "vec"], writes=[("mixg", s, hd)])

            def st_c():
                def om(e):
                    for hd in range(4):
                        e.matmul(po[:, hd, :], lhsT=vi[:, s, hd * 128:(hd + 1) * 128], rhs=ATm[:, hd, :],
                                 start=True, stop=False)
                        for cc in range(2):
                            tc = slice(s * 128 + cc * 64, s * 128 + (cc + 1) * 64)
                            ins = e.matmul(po[:, hd, cc * 64:(cc + 1) * 64], lhsT=Shat[:, cc, hd, :],
                                           rhs=qtT[:, hd, tc], start=False, stop=(cc == 1))
                    return ins
                op("pe", om, reads=[("vi", s), "ATm", ("Shat", 0), ("Shat", 1)] + qts, writes=[P(6)])
                op("act", lambda e: e.copy(out=osb[:], in_=ps[6][:]), reads=[P(6)], writes=["osb"])
                op("act", lambda e: e.activation(out=osq[:], in_=ps[6][:], func=AF.Square), reads=[P(6)],
                   writes=["osq"])

            def st_d():
                op("pe", lambda e: e.matmul(ps[5][:], lhsT=onesb[:], rhs=osq[:], start=True, stop=True),
                   reads=["osq", "onesb"], writes=[P(5)])
                rsqrt_psum(o1, "o1", 5, 1.0 / 128)
                op("dve", lambda e: e.tensor_tensor(out=o1[:], in0=o1[:], in1=osb[:], op=ALU.mult),
                   reads=["o1", "osb"], writes=["o1"])
                op("dve", lambda e: e.tensor_tensor(out=mixT[:, 0:4, ts], in0=h4(o1[:]), in1=sgT[:, :, ts],
                                                    op=ALU.mult),
                   reads=["o1"] + [("sg", hd) for hd in range(4)], writes=[("mix", s)])
            return [st_a, st_b, st_b2, st_c, st_d]

        for s in range(4):
            steps = [None] * 5 if dry else subtile_steps(s)
            for st in steps:
                if st is not None:
                    st()
                yield None
        mkeys = [("mix", s) for s in range(4)] + [("mixg", s, hd) for s in range(4) for hd in range(4)]
        for wb in range(2):
            sl, sk = next_w(("A", l, 28 + wb))
            if not dry:
                slv = sl[:, 0:4096].rearrange("p (k n) -> p k n", k=8)
                for dcl in range(4):
                    dc = wb * 4 + dcl
                    po = 4 + pctr["mo"] % 4
                    pctr["mo"] += 1

                    def mm(e, slv=slv, dcl=dcl, po=po):
                        for k in range(8):
                            ins = e.matmul(ps[po][:], lhsT=slv[:, k, dcl * 128:(dcl + 1) * 128], rhs=mixT[:, k, :],
                                           start=(k == 0), stop=(k == 7))
                        return ins
                    op("pe", mm, reads=[sk] + mkeys, writes=[P(po)])
                    op("dve", lambda e, dc=dc, po=po: e.scalar_tensor_tensor(
                        out=h[:, dc, :], in0=ps[po][:], scalar=modG[:, l, 1, b, dc:dc + 1], in1=h[:, dc, :],
                        op0=ALU.mult, op1=ALU.add),
                       reads=[P(po), ("modG", l, 1, b), hkeys[dc]], writes=[hkeys[dc]])
                    sq_accum(h, hkeys, si, dc)
            yield None
        mdone[l] = ti + 1
        yield ("rel", ("M",))

    tile_list = [(b, j) for b in range(BPC) for j in range(nt)]
    out_keys = []

    def load_tile(ti, hb):
        if dry:
            return
        b, j = tile_list[ti]
        src = xT[b].rearrange("(dc p) t -> p dc t", p=128)[:, :, j * T:(j + 1) * T]
        op("pool", lambda e, hb=hb, src=src: e.dma_start(out=hT[hb][:], in_=src),
           writes=[("h", hb, dc) for dc in range(8)], dma=f"x{hb}")

    def stream(si):
        hb = si
        h = None if dry else hT[hb]
        hkeys = [("h", hb, dc) for dc in range(8)]
        if si == 1:
            yield ("sync",)
        for ti in range(si, len(tile_list), 2):
            b, j = tile_list[ti]
            load_tile(ti, hb)
            for l in range(DEPTH):
                dd = (ti == 0 and l == 0)
                yield from ffn_norm(h, hkeys, l, 0, b, si, need_sq=(l == 0), pb=0)
                yield from ffn_body(h, hkeys, l, 0, b, si, True)
                if dd:
                    dump("h_ffn1", None if dry else h[:], hkeys)
                yield from mixer_norm(h, hkeys, l, b, si)
                yield ("sync",)
                yield from mixer(h, hkeys, l, b, j, ti, si)
                if dd:
                    dump("h_mix", None if dry else h[:], hkeys)
                yield from ffn_norm(h, hkeys, l, 1, b, si, pb=4,
                                    cond=(lambda: fy_done[1 - si] or not active[1 - si]))
                yield ("sync",)
                fy_done[si] = False
                last = (l == DEPTH - 1 and ti + 2 >= len(tile_list))
                if ti == 0 and l == 0:
                    for r in ffn_body(h, hkeys, l, 1, b, si, last):
                        if isinstance(r, tuple) and r[0] == "rel" and r[1] == ("Fg",):
                            yield from adaln(1)
                        yield r
                else:
                    yield from ffn_body(h, hkeys, l, 1, b, si, last)
                if dd:
                    dump("h_ffn2", None if dry else h[:], hkeys)
            yield ("acq", ("N",), None)
            yield None
            if not dry:
                rms_rstd(si, 0)
                for dc in range(8):
                    og = tmpf[dc % 2]
                    op("dve", lambda e, dc=dc, og=og: e.scalar_tensor_tensor(
                        out=og[:], in0=h[:, dc, :], scalar=vec[:, C_FG + dc:C_FG + dc + 1], in1=rstd[:],
                        op0=ALU.mult, op1=ALU.mult), reads=[hkeys[dc], "vec", "rstd"], writes=[("tmpf", dc % 2)])
                    ok = ("out", ti, dc)
                    op("pool", lambda e, dc=dc, og=og, b=b, j=j: e.dma_start(
                        out=outT[b, dc * 128:(dc + 1) * 128, j * T:(j + 1) * T], in_=og[:]),
                       reads=[("tmpf", dc % 2)], writes=[ok], dma=f"o{dc % 2}")
                    out_keys.append(ok)
            yield ("rel", ("N",))

    for _ in adaln(0):
        pass

    gens = [stream(0), stream(1)]
    active = [True, len(tile_list) > 1]
    pending = [None, None]
    at_sync = [False, False]
    held = {}
    seg_idx = [0, 0]
    done = [0, 0]
    seg_counts = [[0], [0]]

    def frac(si):
        if counts is None or seg_idx[si] >= len(counts[si]):
            return done[si]
        return done[si] / max(counts[si][seg_idx[si]], 1)

    while any(active):
        cands = [si for si in (0, 1) if active[si] and not at_sync[si]]
        if not cands:
            for si in (0, 1):
                if at_sync[si]:
                    at_sync[si] = False
                    seg_idx[si] += 1
                    done[si] = 0
                    seg_counts[si].append(0)
            continue
        cands.sort(key=lambda si: (frac(si), si))
        advanced = False
        for si in cands:
            if pending[si] is not None:
                locks, cond = pending[si]
                if all(k not in held for k in locks) and (cond is None or cond()):
                    for k in locks:
                        held[k] = si
                    pending[si] = None
                else:
                    continue
            advanced = True
            try:
                r = next(gens[si])
            except StopIteration:
                active[si] = False
                break
            done[si] += 1
            seg_counts[si][-1] += 1
            if isinstance(r, tuple):
                if r[0] == "acq":
                    pending[si] = (r[1], r[2])
                elif r[0] == "rel":
                    for k in r[1]:
                        assert held.get(k) == si, (k, held, si)
                        del held[k]
                elif r[0] == "sync":
                    at_sync[si] = True
            break
        assert advanced, ("scheduler deadlock", pending, held, at_sync)
    if dry:
        return wseq, seg_counts
    assert wstate["cur"] == len(wseq), (wstate, len(wseq))
    op("sp", None, reads=out_keys[-16:])
    return wseq, dbg_names


def build_nc(nt=SEQ // T, dbg=False):
    nc = bass.Bass("TRN2", target_bir_lowering=False)
    xT = nc.dram_tensor("xT", [BPC, D, SEQ], F32, kind="ExternalInput").ap()
    wA = nc.dram_tensor("wA", [DEPTH, 30, 128, 4096], F32, kind="ExternalInput").ap()
    wB = nc.dram_tensor("wB", [DEPTH, 16, 128, 2816], F32, kind="ExternalInput").ap()
    wada = nc.dram_tensor("wada", [DEPTH, 18, 128, 4096], F32, kind="ExternalInput").ap()
    vecsT = nc.dram_tensor("vecsT", [128, NV], F32, kind="ExternalInput").ap()
    wspT = nc.dram_tensor("wspT", [DEPTH, 128, 512], F32, kind="ExternalInput").ap()
    bsp = nc.dram_tensor("bsp", [DEPTH, 512], F32, kind="ExternalInput").ap()
    outT = nc.dram_tensor("outT", [BPC, D, SEQ], F32, kind="ExternalOutput").ap()
    wsA = nc.dram_tensor("wsA", [DEPTH, 30, 128, 4096], BF16, kind="Internal").ap()
    wsB = nc.dram_tensor("wsB", [DEPTH, 16, 128, 2816], BF16, kind="Internal").ap()
    D_ = (xT, wA, wB, wada, vecsT, wspT, bsp, outT, wsA, wsB)
    _, counts = _program(nc, None, Sched(nc, None, dry=True), nt, None, dbg, D_)
    wseq, _ = _program(nc, None, Sched(nc, None, dry=True), nt, None, dbg, D_, counts=counts)
    print("segment step counts", [c[:6] for c in counts])
    with ExitStack() as es:
        S = Sched(nc, es)
        _, dbg_names = _program(nc, es, S, nt, wseq, dbg, D_, counts=counts)
        print("sbuf bytes remaining", nc.sbuf_bytes_remaining, "ops", {k: len(v) for k, v in S.ops.items()})
        S.emit()
    nc._dbg_names = dbg_names
    return nc


def _blkA(W):
    return W.reshape(8, 128, 512).transpose(1, 0, 2).reshape(128, 4096)


def _blkB(W):
    return W.reshape(22, 128, 128).transpose(1, 0, 2).reshape(128, 2816)


def _prep_shared(w_ada, b_ada, norm_gain, ffn1_w13, ffn1_w2, w_in, hg_lb_logits, hg_gnorm, gm_ln_gain,
                 gm_w_spatial, gm_b_spatial, gm_out_gain, w_out, ffn2_w13, ffn2_w2, final_gain):
    f = lambda a: np.asarray(a, dtype=np.float32)
    wA = np.empty((DEPTH, 30, 128, 4096), np.float32)
    wB = np.empty((DEPTH, 16, 128, 2816), np.float32)
    wada = np.empty((DEPTH, 18, 128, 4096), np.float32)
    for l in range(DEPTH):
        for fi, (w13, w2) in enumerate(((f(ffn1_w13)[l], f(ffn1_w2)[l]), (f(ffn2_w13)[l], f(ffn2_w2)[l]))):
            for i in range(11):
                blk = np.concatenate([w13[:, 256 * i:256 * (i + 1)], w13[:, DFF + 256 * i:DFF + 256 * (i + 1)]], axis=1)
                wA[l, fi * 11 + i] = _blkA(blk)
            for i in range(8):
                wB[l, fi * 8 + i] = _blkB(w2[:, 128 * i:128 * (i + 1)])
        for i in range(6):
            wA[l, 22 + i] = _blkA(f(w_in)[l][:, 512 * i:512 * (i + 1)])
        for i in range(2):
            wA[l, 28 + i] = _blkA(f(w_out)[l][:, 512 * i:512 * (i + 1)])
        for i in range(18):
            wada[l, i] = _blkA(f(w_ada)[l][:, 512 * i:512 * (i + 1)])
    vecs = np.zeros((128, NV), np.float32)
    for l in range(DEPTH):
        vecs[:, C_BADA + l * 72:C_BADA + (l + 1) * 72] = f(b_ada)[l].reshape(72, 128).T
        for sub in range(3):
            c0 = C_NG + (l * 3 + sub) * 8
            vecs[:, c0:c0 + 8] = f(norm_gain)[l, sub].reshape(8, 128).T
        vecs[:, C_LB + l * 4:C_LB + l * 4 + 4] = f(hg_lb_logits)[l].reshape(4, 128).T
        vecs[:, C_GN + l * 4:C_GN + l * 4 + 4] = f(hg_gnorm)[l].reshape(4, 128).T
        vecs[:, C_LNG + l * 4:C_LNG + l * 4 + 4] = f(gm_ln_gain)[l].reshape(4, 128).T
        vecs[:, C_OG + l * 4:C_OG + l * 4 + 4] = f(gm_out_gain)[l].reshape(4, 128).T
    vecs[:, C_FG:C_FG + 8] = f(final_gain).reshape(8, 128).T
    wspT = np.ascontiguousarray(f(gm_w_spatial).transpose(0, 3, 1, 2).reshape(DEPTH, 128, 512))
    bspv = np.ascontiguousarray(f(gm_b_spatial).reshape(DEPTH, 512))
    return wA, wB, wada, vecs, wspT, bspv


def kernel(x, c, w_ada, b_ada, norm_gain, ffn1_w13, ffn1_w2, w_in, hg_lb_logits, hg_gnorm, gm_ln_gain,
           gm_w_spatial, gm_b_spatial, gm_out_gain, w_out, ffn2_w13, ffn2_w2, final_gain, _nt=SEQ // T,
           _cores=NCORES, _dbg=None):
    x = np.asarray(x, dtype=np.float32)
    c = np.asarray(c, dtype=np.float32)
    wA, wB, wada, vecs, wspT, bspv = _prep_shared(
        w_ada, b_ada, norm_gain, ffn1_w13, ffn1_w2, w_in, hg_lb_logits, hg_gnorm, gm_ln_gain, gm_w_spatial,
        gm_b_spatial, gm_out_gain, w_out, ffn2_w13, ffn2_w2, final_gain)
    in_maps = []
    for core in range(_cores):
        xs = x[core * BPC:(core + 1) * BPC]
        xTc = np.ascontiguousarray(xs.transpose(0, 2, 1))
        v = vecs.copy()
        cs = c[core * BPC:(core + 1) * BPC]
        v[:, C_C:C_C + 16] = cs.reshape(BPC, 8, 128).transpose(2, 1, 0).reshape(128, 16)
        in_maps.append({"xT": xTc, "wA": wA, "wB": wB, "wada": wada, "vecsT": v, "wspT": wspT, "bsp": bspv})
    nc = build_nc(_nt, dbg=_dbg is not None)
    res = run_bass_kernel_spmd(nc, in_maps, core_ids=list(range(_cores)))
    out = np.empty((NCORES * BPC, SEQ, D), np.float32)
    if _dbg is not None:
        for k in nc._dbg_names:
            _dbg[k] = np.asarray(res.results[0][k])
    for core in range(_cores):
        out[core * BPC:(core + 1) * BPC] = res.results[core]["outT"].transpose(0, 2, 1)
    return out
```
